# Optimizing a Trainium2 kernel written in Bass

```python
import math
import jax, jax.numpy as jnp
from jax import lax
import numpy as np

D_MODEL = 1024
BATCH = 2
SEQ = 8192
DEPTH = 4
DEC_BATCH = 128
DEC_SEQ = 1
PAST_LEN = 8192
PAGE_SIZE = 128

F32 = jnp.float32
A_HEADS = 4
A_QK = 64
A_V = 128
A_WIDTH = A_HEADS * A_V
B_HEADS = 8
B_P = 64
B_WIDTH = B_HEADS * B_P
B_GROUPS = 2
B_STATE = 128
CONV_W = 4
B_CONV_DIM = B_WIDTH + 2 * B_GROUPS * B_STATE
C_HEADS = 8
C_KV = 2
C_HD = 64
C_WIDTH = C_HEADS * C_HD
WINDOW = 128
ROPE_THETA = 10000.0
D_MIX = A_WIDTH + B_WIDTH + C_WIDTH
CHUNK = 64
NORM_EPS = 1e-6
IN_SIZES = (A_HEADS * A_QK, A_HEADS * A_QK, A_WIDTH, A_WIDTH, A_WIDTH, A_HEADS, A_HEADS,
            B_WIDTH, B_CONV_DIM, B_HEADS,
            C_WIDTH, C_KV * C_HD, C_KV * C_HD, C_WIDTH)
D_IN = 2 * A_HEADS * A_QK + 3 * A_WIDTH + 2 * A_HEADS + B_WIDTH + B_CONV_DIM + B_HEADS + 2 * C_WIDTH + 2 * C_KV * C_HD

kernel_name = 'hymba_mlstm_ssd_swa_step'


def _rmsnorm(x, w):
    xf = x.astype(F32)
    y = xf * lax.rsqrt(jnp.mean(xf * xf, axis=-1, keepdims=True) + NORM_EPS)
    return (y * w.astype(F32)).astype(x.dtype)


def _split(x, sizes):
    return jnp.split(x, np.cumsum(sizes)[:-1].tolist(), axis=-1)


def _rope(x, pos):
    half = x.shape[-1] // 2
    inv = ROPE_THETA ** (-jnp.arange(half, dtype=F32) / half)
    ang = pos.astype(F32)[:, None] * inv[None, :]
    cos = jnp.cos(ang)[None, :, None, :]
    sin = jnp.sin(ang)[None, :, None, :]
    xf = x.astype(F32)
    x1, x2 = xf[..., :half], xf[..., half:]
    return jnp.concatenate([x1 * cos - x2 * sin, x2 * cos + x1 * sin], axis=-1).astype(x.dtype)


def _sink_softmax(s, sink):
    sink = sink.astype(F32)
    m = jnp.maximum(jnp.max(s, axis=-1, keepdims=True), sink)
    e = jnp.exp(s - m)
    return e / (jnp.sum(e, axis=-1, keepdims=True) + jnp.exp(sink - m))


def _mlstm(q, k, v, ig, fg, C0, n0, m0):
    N, T, H, DK = q.shape
    DV = v.shape[-1]
    L = math.gcd(T, CHUNK)
    NC = T // L
    q = q.astype(F32).reshape(N, NC, L, H, DK)
    k = (k.astype(F32) * (DK ** -0.5)).reshape(N, NC, L, H, DK)
    v = v.astype(F32).reshape(N, NC, L, H, DV)
    ig = ig.reshape(N, NC, L, H)
    b = jnp.cumsum(jax.nn.log_sigmoid(fg).reshape(N, NC, L, H), axis=2)
    bL = b[:, :, -1]
    g = bL[:, :, None] - b + ig
    m_loc = jnp.max(g, axis=2)
    w = jnp.exp(g - m_loc[:, :, None])
    C_loc = jnp.einsum('nclh,nclhv,nclhk->nchvk', w, v, k)
    n_loc = jnp.einsum('nclh,nclhk->nchk', w, k)

    def step(carry, inp):
        C, n, m = carry
        Cl, nl, ml, bl = inp
        m_new = jnp.maximum(bl + m, ml)
        sp = jnp.exp(bl + m - m_new)
        sl = jnp.exp(ml - m_new)
        C_new = sp[..., None, None] * C + sl[..., None, None] * Cl
        n_new = sp[..., None] * n + sl[..., None] * nl
        return (C_new, n_new, m_new), (C, n, m)

    xs = (jnp.moveaxis(C_loc, 1, 0), jnp.moveaxis(n_loc, 1, 0), jnp.moveaxis(m_loc, 1, 0), jnp.moveaxis(bL, 1, 0))
    (Cf, nf, mf), (Cs, ns, ms) = lax.scan(step, (C0.astype(F32), n0.astype(F32), m0.astype(F32)), xs)
    Cs = jnp.moveaxis(Cs, 0, 1)
    ns = jnp.moveaxis(ns, 0, 1)
    ms = jnp.moveaxis(ms, 0, 1)
    causal = (jnp.arange(L)[:, None] >= jnp.arange(L)[None, :])[:, :, None]
    dmat = jnp.where(causal, b[:, :, :, None] - b[:, :, None] + ig[:, :, None], -jnp.inf)
    inter = b + ms[:, :, None]
    m_t = jnp.maximum(inter, jnp.max(dmat, axis=3))
    s = jnp.exp(dmat - m_t[:, :, :, None]) * jnp.einsum('ncthk,ncshk->nctsh', q, k)
    si = jnp.exp(inter - m_t)
    num = jnp.einsum('nctsh,ncshv->ncthv', s, v) + si[..., None] * jnp.einsum('nchvk,ncthk->ncthv', Cs, q)
    den = jnp.sum(s, axis=3) + si * jnp.einsum('nchk,ncthk->ncth', ns, q)
    h = num / jnp.maximum(jnp.abs(den), jnp.exp(-m_t))[..., None]
    return h.reshape(N, T, H, DV), Cf, nf, mf


def _causal_conv(u, buf, w, b):
    T = u.shape[1]
    full = jnp.concatenate([buf.astype(u.dtype), u], axis=1)
    acc = b
    for j in range(CONV_W):
        acc = acc + full[:, j:j + T] * w[j]
    return jax.nn.silu(acc), full[:, T:]


def _ssd(x, dt, A, Bm, Cm, h0):
    N, T, H, P = x.shape
    G, S = Bm.shape[2], Bm.shape[3]
    E = H // G
    L = math.gcd(T, CHUNK)
    NC = T // L
    x = x.astype(F32).reshape(N, NC, L, G, E, P)
    dt = dt.reshape(N, NC, L, G, E)
    Bm = Bm.astype(F32).reshape(N, NC, L, G, S)
    Cm = Cm.astype(F32).reshape(N, NC, L, G, S)
    a = jnp.cumsum(dt * A.reshape(G, E), axis=2)
    aL = a[:, :, -1]
    h_loc = jnp.einsum('nclge,nclgep,nclgs->ncgeps', jnp.exp(aL[:, :, None] - a) * dt, x, Bm)

    def step(h, inp):
        hl, al = inp
        return jnp.exp(al)[..., None, None] * h + hl, h

    hf, hs = lax.scan(step, h0.astype(F32).reshape(N, G, E, P, S),
                      (jnp.moveaxis(h_loc, 1, 0), jnp.moveaxis(aL, 1, 0)))
    hs = jnp.moveaxis(hs, 0, 1)
    causal = (jnp.arange(L)[:, None] >= jnp.arange(L)[None, :])[:, :, None, None]
    decay = jnp.exp(jnp.where(causal, a[:, :, :, None] - a[:, :, None], -jnp.inf))
    cb = jnp.einsum('nctgs,ncugs->nctug', Cm, Bm)
    wmat = decay * cb[..., None] * dt[:, :, None]
    y = jnp.einsum('nctuge,ncugep->nctgep', wmat, x) + \
        jnp.einsum('nctgs,ncgeps->nctgep', Cm, hs) * jnp.exp(a)[..., None]
    return y.reshape(N, T, H, P), hf.reshape(N, H, P, S)


def _swa_prompt(q, k, v, sinks):
    N, T, HQ, Dh = q.shape
    HK = k.shape[2]
    G = HQ // HK
    Bk = WINDOW
    NB = T // Bk
    q = q.reshape(N, NB, Bk, HK, G, Dh)
    k = k.reshape(N, NB, Bk, HK, Dh)
    v = v.reshape(N, NB, Bk, HK, Dh)
    pad = ((0, 0), (1, 0), (0, 0), (0, 0), (0, 0))
    kk = jnp.concatenate([jnp.pad(k, pad)[:, :-1], k], axis=2)
    vv = jnp.concatenate([jnp.pad(v, pad)[:, :-1], v], axis=2)
    s = jnp.einsum('nbqhgd,nbkhd->nbhgqk', q, kk).astype(F32) * (Dh ** -0.5)
    qpos = jnp.arange(NB)[:, None, None] * Bk + jnp.arange(Bk)[None, :, None]
    kpos = jnp.arange(NB)[:, None, None] * Bk - Bk + jnp.arange(2 * Bk)[None, None, :]
    delta = qpos - kpos
    mask = (delta >= 0) & (delta < WINDOW) & (kpos >= 0)
    s = jnp.where(mask[None, :, None, None], s, -jnp.inf)
    p = _sink_softmax(s, sinks.reshape(HK, G)[:, :, None, None])
    o = jnp.einsum('nbhgqk,nbkhd->nbqhgd', p.astype(vv.dtype), vv)
    return o.reshape(N, T, HQ, Dh)


def _swa_decode(q, k, v, kc, vc, sinks):
    N, T, HQ, Dh = q.shape
    HK = k.shape[2]
    G = HQ // HK
    Wc = kc.shape[1]
    kk = jnp.concatenate([kc.astype(k.dtype), k], axis=1)
    vv = jnp.concatenate([vc.astype(v.dtype), v], axis=1)
    s = jnp.einsum('nqhgd,nkhd->nhgqk', q.reshape(N, T, HK, G, Dh), kk).astype(F32) * (Dh ** -0.5)
    delta = (Wc + jnp.arange(T))[:, None] - jnp.arange(Wc + T)[None, :]
    mask = (delta >= 0) & (delta < WINDOW)
    s = jnp.where(mask, s, -jnp.inf)
    p = _sink_softmax(s, sinks.reshape(HK, G)[:, :, None, None])
    o = jnp.einsum('nhgqk,nkhd->nqhgd', p.astype(vv.dtype), vv).reshape(N, T, HQ, Dh)
    return o, kk[:, -Wc:], vv[:, -Wc:]


def _layer(x, pos, conv_buf, ssm_h, C0, n0, m0, kc, vc,
           norm_w, w_in, a_ib, a_fb, a_nw, conv_w, conv_b, dt_bias, A_log, D_skip, b_nw,
           qn_w, kn_w, sinks, w_out):
    N, T, _ = x.shape
    u = _rmsnorm(x, norm_w) @ w_in
    (aq, ak, av, ao, az, ai, af, bz, bxbc, bdt, cq, ck, cv, cz) = _split(u, IN_SIZES)
    ha, C1, n1, m1 = _mlstm(aq.reshape(N, T, A_HEADS, A_QK), ak.reshape(N, T, A_HEADS, A_QK),
                            av.reshape(N, T, A_HEADS, A_V),
                            ai.astype(F32) + a_ib.astype(F32), af.astype(F32) + a_fb.astype(F32), C0, n0, m0)
    ha = _rmsnorm(ha, a_nw.reshape(A_HEADS, A_V)).reshape(N, T, A_WIDTH)
    ya = (ha * jax.nn.sigmoid(ao.astype(F32)) * jax.nn.silu(az.astype(F32))).astype(x.dtype)
    xbc, conv1 = _causal_conv(bxbc, conv_buf, conv_w, conv_b)
    bx, bB, bC = _split(xbc, (B_WIDTH, B_GROUPS * B_STATE, B_GROUPS * B_STATE))
    dt = jax.nn.softplus(bdt.astype(F32) + dt_bias.astype(F32))
    A = -jnp.exp(A_log.astype(F32))
    bx = bx.reshape(N, T, B_HEADS, B_P)
    yb, h1 = _ssd(bx, dt, A, bB.reshape(N, T, B_GROUPS, B_STATE), bC.reshape(N, T, B_GROUPS, B_STATE), ssm_h)
    yb = yb + D_skip.astype(F32)[:, None] * bx.astype(F32)
    gb = (yb.reshape(N, T, B_WIDTH) * jax.nn.silu(bz.astype(F32))).reshape(N, T, B_GROUPS, B_WIDTH // B_GROUPS)
    yb = _rmsnorm(gb, b_nw.reshape(B_GROUPS, B_WIDTH // B_GROUPS)).reshape(N, T, B_WIDTH).astype(x.dtype)
    q = _rope(_rmsnorm(cq.reshape(N, T, C_HEADS, C_HD), qn_w), pos)
    k = _rope(_rmsnorm(ck.reshape(N, T, C_KV, C_HD), kn_w), pos)
    v = cv.reshape(N, T, C_KV, C_HD)
    if kc is None:
        o = _swa_prompt(q, k, v, sinks)
        k1, v1 = k[:, -WINDOW:], v[:, -WINDOW:]
    else:
        o, k1, v1 = _swa_decode(q, k, v, kc, vc, sinks)
    yc = (o.reshape(N, T, C_WIDTH).astype(F32) * jax.nn.silu(cz.astype(F32))).astype(x.dtype)
    y = jnp.concatenate([ya, yb, yc], axis=-1) @ w_out
    return x + y.astype(x.dtype), (C1, n1, m1, h1, conv1, k1, v1)


def setup_inputs(seed: int = 0) -> dict:
    key = jax.random.key(seed)
    ks = jax.random.split(key, 26)

    def nrm(k, shape, scale):
        return scale * jax.random.normal(k, shape, F32)

    win = min(WINDOW, PAST_LEN)
    dt0 = jnp.exp(jax.random.uniform(ks[16], (DEPTH, B_HEADS), F32, math.log(1e-3), math.log(1e-1)))
    return {
        'x_prompt': nrm(ks[0], (BATCH, SEQ, D_MODEL), 1.0),
        'x_sample': nrm(ks[1], (DEC_BATCH, DEC_SEQ, D_MODEL), 1.0),
        'state_mlstm_C': nrm(ks[2], (DEPTH, DEC_BATCH, A_HEADS, A_V, A_QK), 0.3),
        'state_mlstm_n': nrm(ks[3], (DEPTH, DEC_BATCH, A_HEADS, A_QK), 0.3),
        'state_mlstm_m': nrm(ks[4], (DEPTH, DEC_BATCH, A_HEADS), 1.0),
        'state_ssm': nrm(ks[5], (DEPTH, DEC_BATCH, B_HEADS, B_P, B_STATE), 0.3),
        'state_conv': nrm(ks[6], (DEPTH, DEC_BATCH, CONV_W - 1, B_CONV_DIM), 1.0),
        'cache_k': nrm(ks[7], (DEPTH, DEC_BATCH, win, C_KV, C_HD), 1.0),
        'cache_v': nrm(ks[8], (DEPTH, DEC_BATCH, win, C_KV, C_HD), 1.0),
        'norm_w': 1.0 + nrm(ks[9], (DEPTH, D_MODEL), 0.02),
        'w_in': nrm(ks[10], (DEPTH, D_MODEL, D_IN), D_MODEL ** -0.5),
        'a_igate_b': nrm(ks[11], (DEPTH, A_HEADS), 0.1) - 1.0,
        'a_fgate_b': jnp.linspace(3.0, 6.0, A_HEADS, dtype=F32)[None, :] + nrm(ks[12], (DEPTH, A_HEADS), 0.1),
        'a_norm_w': 1.0 + nrm(ks[13], (DEPTH, A_WIDTH), 0.02),
        'b_conv_w': nrm(ks[14], (DEPTH, CONV_W, B_CONV_DIM), CONV_W ** -0.5),
        'b_conv_b': nrm(ks[15], (DEPTH, B_CONV_DIM), 0.02),
        'b_dt_bias': dt0 + jnp.log(-jnp.expm1(-dt0)),
        'b_A_log': jnp.log(jax.random.uniform(ks[17], (DEPTH, B_HEADS), F32, 1.0, 16.0)),
        'b_D': 1.0 + nrm(ks[18], (DEPTH, B_HEADS), 0.1),
        'b_norm_w': 1.0 + nrm(ks[19], (DEPTH, B_WIDTH), 0.02),
        'c_qnorm_w': 1.0 + nrm(ks[20], (DEPTH, C_HD), 0.02),
        'c_knorm_w': 1.0 + nrm(ks[21], (DEPTH, C_HD), 0.02),
        'c_sinks': nrm(ks[22], (DEPTH, C_HEADS), 0.5),
        'w_out': nrm(ks[23], (DEPTH, D_MIX, D_MODEL), 0.5 * D_MIX ** -0.5),
    }


def reference(x_prompt, x_sample, state_mlstm_C, state_mlstm_n, state_mlstm_m, state_ssm, state_conv,
              cache_k, cache_v, norm_w, w_in, a_igate_b, a_fgate_b, a_norm_w, b_conv_w, b_conv_b,
              b_dt_bias, b_A_log, b_D, b_norm_w, c_qnorm_w, c_knorm_w, c_sinks, w_out):
    Bp, Tp, _ = x_prompt.shape
    Ts = x_sample.shape[1]
    pos_p = jnp.arange(Tp, dtype=jnp.int32)
    pos_s = PAST_LEN + jnp.arange(Ts, dtype=jnp.int32)
    zC = jnp.zeros((Bp, A_HEADS, A_V, A_QK), F32)
    zn = jnp.zeros((Bp, A_HEADS, A_QK), F32)
    zm = jnp.zeros((Bp, A_HEADS), F32)
    zh = jnp.zeros((Bp, B_HEADS, B_P, B_STATE), F32)
    zconv = jnp.zeros((Bp, CONV_W - 1, B_CONV_DIM), x_prompt.dtype)
    hp, hs = x_prompt, x_sample
    st_prompt, st_sample = [], []
    for l in range(DEPTH):
        lw = (norm_w[l], w_in[l], a_igate_b[l], a_fgate_b[l], a_norm_w[l], b_conv_w[l], b_conv_b[l],
              b_dt_bias[l], b_A_log[l], b_D[l], b_norm_w[l], c_qnorm_w[l], c_knorm_w[l], c_sinks[l], w_out[l])
        hp, sp = _layer(hp, pos_p, zconv, zh, zC, zn, zm, None, None, *lw)
        hs, ss = _layer(hs, pos_s, state_conv[l], state_ssm[l], state_mlstm_C[l], state_mlstm_n[l],
                        state_mlstm_m[l], cache_k[l], cache_v[l], *lw)
        st_prompt.append(sp)
        st_sample.append(ss)
    p_C, p_n, p_m, p_h, p_conv, p_k, p_v = [jnp.stack(t) for t in zip(*st_prompt)]
    s_C, s_n, s_m, s_h, s_conv, s_k, s_v = [jnp.stack(t) for t in zip(*st_sample)]
    return (hp, hs, p_C, p_n, p_m, p_h, p_conv, p_k, p_v, s_C, s_n, s_m, s_h, s_conv, s_k, s_v)
```

```python
import contextlib
import math
import os
import numpy as np
import concourse.bass as bass
import concourse.mybir as mybir
from concourse.bass_utils import run_bass_kernel_spmd

F32 = mybir.dt.float32
BF16 = mybir.dt.bfloat16
AF = mybir.ActivationFunctionType
ALU = mybir.AluOpType
AX = mybir.AxisListType

D_MODEL = 1024
D_IN = 4880
D_MIX = 1536
EPS = 1e-6
NEG = -30000.0


class _Op:
    __slots__ = ("eng", "fn", "reads", "writes", "dma", "deps", "sig", "waits", "clock", "idx")


class Sched:
    ENGS = ("tensor", "vector", "scalar", "gpsimd", "sync")

    def __init__(self, nc):
        self.nc = nc
        self.ops = []
        self.last_w = {}
        self.readers = {}

    def op(self, eng, fn, reads=(), writes=(), dma=None):
        o = _Op()
        o.eng, o.fn, o.reads, o.writes, o.dma = eng, fn, tuple(reads), tuple(writes), dma
        o.idx = len(self.ops)
        deps = set()
        for k in o.reads:
            w = self.last_w.get(k)
            if w is not None:
                deps.add(w)
        for k in o.writes:
            w = self.last_w.get(k)
            if w is not None:
                deps.add(w)
            for r in self.readers.get(k, ()):
                deps.add(r)
        if eng == "tensor":
            deps = {d for d in deps if self.ops[d].eng != "tensor"}
        o.deps = deps
        for k in o.reads:
            self.readers.setdefault(k, []).append(o.idx)
        for k in o.writes:
            self.last_w[k] = o.idx
            self.readers[k] = []
        self.ops.append(o)
        return o

    def emit(self, stack, final_wait_eng="sync"):
        nc = self.nc
        ops = self.ops
        needed = set()
        for o in ops:
            needed.update(o.deps)
        sems = {}
        counts = {}
        for o in ops:
            if o.dma is not None:
                sname = "d_" + o.dma
                counts[sname] = counts.get(sname, 0) + 16
                o.sig = (sname, counts[sname])
            elif o.idx in needed:
                sname = "e_" + o.eng
                counts[sname] = counts.get(sname, 0) + 1
                o.sig = (sname, counts[sname])
            else:
                o.sig = None
        known = {e: {} for e in self.ENGS}
        for o in ops:
            kn = known[o.eng]
            waits = []
            for d in sorted(o.deps, reverse=True):
                do = ops[d]
                sname, val = do.sig
                if kn.get(sname, 0) >= val:
                    continue
                waits.append((sname, val))
                kn[sname] = val
                for s2, v2 in do.clock.items():
                    if kn.get(s2, 0) < v2:
                        kn[s2] = v2
            o.waits = waits
            o.clock = dict(kn)
        for s in counts:
            sems[s] = stack.enter_context(nc.semaphore(s))
        by_eng = {e: [o for o in ops if o.eng == e] for e in self.ENGS}
        self.stats = dict(n_ops=len(ops), n_waits=sum(len(o.waits) for o in ops), n_sems=len(sems),
                          per_eng={e: len(v) for e, v in by_eng.items()})

        def run(engobj, lst, do_final):
            for o in lst:
                for sname, val in o.waits:
                    engobj.wait_ge(sems[sname], val)
                ins = o.fn(engobj)
                if o.sig is not None:
                    ins.then_inc(sems[o.sig[0]], 16 if o.dma is not None else 1)
            if do_final:
                for sname, val in counts.items():
                    if sname.startswith("d_"):
                        engobj.wait_ge(sems[sname], val)

        with nc.Block() as block:
            @block.tensor
            def _(e):
                run(e, by_eng["tensor"], final_wait_eng == "tensor")

            @block.vector
            def _(e):
                run(e, by_eng["vector"], final_wait_eng == "vector")

            @block.scalar
            def _(e):
                run(e, by_eng["scalar"], final_wait_eng == "scalar")

            @block.gpsimd
            def _(e):
                run(e, by_eng["gpsimd"], final_wait_eng == "gpsimd")

            @block.sync
            def _(e):
                run(e, by_eng["sync"], final_wait_eng == "sync")


def host_consts(T):
    c = {}
    ident = np.eye(128, dtype=np.float32)
    s = np.arange(128)
    tri = (s[:, None] <= s[None, :]).astype(np.float32)
    negm = np.where(s[None, :] < s[:, None], NEG, 0.0).astype(np.float32)
    c["cmat"] = np.concatenate([ident, tri, 1.0 - tri, np.ones((128, 128), np.float32),
                                np.tile(negm[:, None, :], (1, 1, 1)).reshape(128, 128)], axis=1)
    pos = np.arange(T, dtype=np.float32)
    inv = (10000.0 ** (-np.arange(32, dtype=np.float32) / 32)).astype(np.float32)
    ang = pos[:, None] * inv[None, :]
    cos, sin = np.cos(ang).astype(np.float32), np.sin(ang).astype(np.float32)
    c["rope"] = np.concatenate([cos, cos, -sin, sin], axis=1).astype(np.float32)
    cs = np.zeros((128, 1152), np.float32)
    p = np.arange(128)
    for b in range(16):
        cs[b, 7 + 8 * b] = 1.0
        for j in range(8):
            cs[b * 8 + j, 136 + j * 16 + b] = 1.0
    cs[p // 2, 264 + p] = 1.0
    cs[(p % 8) // 2, 392 + p] = 1.0
    cs[:, 520:648] = (p[:, None] // 2 == p[None, :] // 2)
    cs[:, 648:776] = (p[:, None] // 4 == p[None, :] // 4)
    cs[2 * np.arange(64), 776 + np.arange(64)] = 1.0
    cs[p, 840 + p % 8] = 1.0
    angs = np.float32(8192.0) * inv
    cS, sS = np.cos(angs).astype(np.float32), np.sin(angs).astype(np.float32)
    cs[0:16, 1024:1152] = np.concatenate([cS, cS, -sS, sS])[None, :]
    c["csel"] = cs
    return c


def build(NT, DEPTH, with_sample=False, nc=None, ins=None, outs=None):
    T = NT * 128
    if nc is None:
        nc = bass.Bass("TRN2", target_bir_lowering=False)

    def din(name, shape):
        if ins is not None:
            assert list(ins[name].shape) == list(shape), (name, ins[name].shape, shape)
            return ins[name]
        return nc.dram_tensor(name, shape, F32, kind="ExternalInput").ap()

    def dout(name, shape):
        if outs is not None:
            assert list(outs[name].shape) == list(shape), (name, outs[name].shape, shape)
            return outs[name]
        return nc.dram_tensor(name, shape, F32, kind="ExternalOutput").ap()

    xp = din("xp", [T, 1024])
    w_in = din("w_in", [DEPTH, 1024, D_IN])
    w_out = din("w_out", [DEPTH, D_MIX, 1024])
    norm_w = din("norm_w", [DEPTH, 1024])
    a_ib = din("a_igate_b", [DEPTH, 4])
    a_fb = din("a_fgate_b", [DEPTH, 4])
    a_nw = din("a_norm_w", [DEPTH, 512])
    cw = din("b_conv_w", [DEPTH, 4, 1024])
    cb = din("b_conv_b", [DEPTH, 1024])
    dtb = din("b_dt_bias", [DEPTH, 8])
    alog = din("b_A_log", [DEPTH, 8])
    dsk = din("b_D", [DEPTH, 8])
    b_nw = din("b_norm_w", [DEPTH, 512])
    qnw = din("c_qnorm_w", [DEPTH, 64])
    knw = din("c_knorm_w", [DEPTH, 64])
    snk = din("c_sinks", [DEPTH, 8])
    cmat = din("cmat", [128, 640])
    rope = din("rope", [T, 128])

    if with_sample:
        csel = din("csel", [128, 1152])
        xs_in = din("xs_in", [16, 1024])
        i_sC = din("sC", [DEPTH, 16, 4, 128, 64])
        i_sn = din("sn", [DEPTH, 16, 4, 64])
        i_sm = din("sm", [DEPTH, 16, 4])
        i_sh = din("sh", [DEPTH, 16, 8, 64, 128])
        i_sconv = din("sconv", [DEPTH, 16, 3, 1024])
        i_ck = din("ck", [DEPTH, 16, 128, 2, 64])
        i_cv = din("cv", [DEPTH, 16, 128, 2, 64])
        o_ys = dout("ys", [16, 1024])
        o_sC = dout("oC", [DEPTH, 16, 4, 128, 64])
        o_sn = dout("on", [DEPTH, 16, 4, 64])
        o_sm = dout("om", [DEPTH, 16, 4])
        o_sh = dout("oh", [DEPTH, 16, 8, 64, 128])
        o_sconv = dout("oconv", [DEPTH, 16, 3, 1024])
        o_sk = dout("ok", [DEPTH, 16, 128, 2, 64])
        o_sv = dout("ov", [DEPTH, 16, 128, 2, 64])
    yp = dout("yp", [T, 1024])
    o_pC = dout("pC", [DEPTH, 4, 128, 64])
    o_pn = dout("pn", [DEPTH, 4, 64])
    o_pm = dout("pm", [DEPTH, 4])
    o_ph = dout("ph", [DEPTH, 8, 64, 128])
    o_pconv = dout("pconv", [DEPTH, 3, 1024])
    o_pk = dout("pk", [DEPTH, 128, 2, 64])
    o_pv = dout("pv", [DEPTH, 128, 2, 64])

    st = contextlib.ExitStack()
    with st:
        def sb(name, shape, dt=F32):
            return st.enter_context(nc.sbuf_tensor(name, shape, dt))

        def ps(name, shape, dt=F32):
            return st.enter_context(nc.psum_tensor(name, shape, dt))

        S = Sched(nc)
        chain_ctx = [None, None]

        def op(eng, fn, reads=(), writes=(), dma=None, cost=None):
            if chain_ctx[0] is None:
                return S.op(eng, fn, reads, writes, dma)
            item = (eng, fn, tuple(reads), tuple(writes), dma, cost)
            if chain_ctx[1] is not None:
                chain_ctx[1].append(item)
            else:
                chain_ctx[0].append([item])
            return None

        def label(name):
            chain_ctx[0].append([("label", name)])

        def wait_label(name):
            chain_ctx[0].append([("wait", name)])

        class atomic:
            def __enter__(self):
                if chain_ctx[0] is not None:
                    chain_ctx[1] = []
            def __exit__(self, *a):
                if chain_ctx[0] is not None:
                    chain_ctx[0].append(chain_ctx[1])
                    chain_ctx[1] = None
                return False

        COST = {"tensor": 0.18, "vector": 0.55, "scalar": 0.5, "gpsimd": 1.4, "sync": 2.0}
        SYNC_LAT = 0.6
        eng_free = {e: 0.0 for e in Sched.ENGS}
        fin = {}

        def est_ready(item):
            eng, fn, r, w, d = item[:5]
            tmax = 0.0
            for k in r:
                x = S.last_w.get(k)
                if x is not None:
                    tmax = max(tmax, fin.get(x, 0.0) + (SYNC_LAT if S.ops[x].eng != eng else 0.1))
            for k in w:
                x = S.last_w.get(k)
                if x is not None:
                    tmax = max(tmax, fin.get(x, 0.0) + (SYNC_LAT if S.ops[x].eng != eng else 0.1))
                for x in S.readers.get(k, ()):
                    tmax = max(tmax, fin.get(x, 0.0) + (SYNC_LAT if S.ops[x].eng != eng else 0.1))
            return max(tmax, eng_free[eng])

        def emit_item(item):
            t0 = est_ready(item)
            o = S.op(*item[:5])
            t1 = t0 + (item[5] if item[5] is not None else COST[item[0]])
            eng_free[item[0]] = t1
            fin[o.idx] = t1

        def run_chains(fns, prio=None):
            prio = prio or [0.0] * len(fns)
            chains = []
            for f in fns:
                chain_ctx[0] = []
                f()
                chains.append(chain_ctx[0])
            chain_ctx[0] = None
            base = max(eng_free.values())
            for e in eng_free:
                eng_free[e] = max(eng_free[e], base - 3.0)
            pos = [0] * len(chains)
            labels = set()
            while True:
                best, bt = None, None
                progressed = False
                for i, c in enumerate(chains):
                    while pos[i] < len(c) and c[pos[i]][0][0] in ("label", "wait"):
                        kind, name = c[pos[i]][0][0], c[pos[i]][0][1]
                        if kind == "label":
                            labels.add(name)
                            pos[i] += 1
                            progressed = True
                        elif name in labels:
                            pos[i] += 1
                            progressed = True
                        else:
                            break
                    if pos[i] < len(c) and c[pos[i]][0][0] not in ("label", "wait"):
                        t_ = est_ready(c[pos[i]][0]) + prio[i]
                        if best is None or t_ < bt:
                            best, bt = i, t_
                if best is None:
                    if progressed:
                        continue
                    assert all(pos[i] >= len(c) for i, c in enumerate(chains)), "chain deadlock"
                    break
                for it in chains[best][pos[best]]:
                    emit_item(it)
                pos[best] += 1

        def dma(out, in_, r, w, sem, eng="sync", slow=False):
            if slow:
                return op(eng, lambda e: e.dma_start(out=out, in_=in_, allow_slow_non_contiguous=True), r, w, dma=sem)
            return op(eng, lambda e: e.dma_start(out=out, in_=in_), r, w, dma=sem)

        def fsz(ap):
            n = 1
            for d_ in ap.shape[1:]:
                n *= d_
            return n

        def mm(out, lhsT, rhs, start, stop, r, w):
            c_ = max(0.06, fsz(out) / 2400.0) * (4.0 if lhsT.dtype == F32 else 1.0)
            return op("tensor", lambda e: e.matmul(out, lhsT=lhsT, rhs=rhs, start=start, stop=stop), r, w, cost=c_)

        def tr(out, in_, ident, r, w):
            return op("tensor", lambda e: e.transpose(out=out, in_=in_, identity=ident), r, w, cost=0.12)

        def act(out, in_, func, r, w, **kw):
            return op("scalar", lambda e: e.activation(out=out, in_=in_, func=func, **kw), r, w, cost=0.2 + fsz(out) / 1200.0)

        def ecost(eng, out):
            return (0.1 + fsz(out) / 900.0) * (2.2 if eng == "gpsimd" else 1.0)

        def tt(out, in0, in1, o, r, w, eng="vector"):
            return op(eng, lambda e: e.tensor_tensor(out=out, in0=in0, in1=in1, op=o), r, w, cost=ecost(eng, out))

        def ts(out, in0, s1, s2, o0, o1, r, w, eng="vector"):
            if s2 is None:
                return op(eng, lambda e: e.tensor_scalar(out=out, in0=in0, scalar1=s1, scalar2=None, op0=o0), r, w, cost=ecost(eng, out))
            return op(eng, lambda e: e.tensor_scalar(out=out, in0=in0, scalar1=s1, scalar2=s2, op0=o0, op1=o1), r, w, cost=ecost(eng, out))

        def stt(out, in0, scalar, in1, o0, o1, r, w):
            return op("vector", lambda e: e.scalar_tensor_tensor(out=out, in0=in0, scalar=scalar, in1=in1, op0=o0, op1=o1), r, w, cost=ecost("vector", out))

        def red(out, in_, o, r, w):
            return op("vector", lambda e: e.tensor_reduce(out=out, in_=in_, axis=AX.X, op=o), r, w, cost=ecost("vector", in_))

        def cp(out, in_, r, w, eng="vector"):
            if eng == "scalar":
                return op("scalar", lambda e: e.copy(out=out, in_=in_), r, w, cost=0.2 + fsz(out) / 1200.0)
            return op(eng, lambda e: e.tensor_copy(out=out, in_=in_), r, w, cost=ecost(eng, out))

        def mset(ap, val, w, eng="gpsimd"):
            return op(eng, lambda e: e.memset(ap, val), (), w)

        def rsq(out, in_, scale, r, w):
            act(out, in_, AF.Ln, r, w, scale=scale, bias=EPS)
            act(out, out, AF.Exp, w, w, scale=-0.5)

        cm = sb("cm", [128, 640])
        dma(cm[:], cmat, (), ["cm"], "cm")
        identf = cm[:, 0:128]
        trif = cm[:, 128:256]
        onesf = cm[:, 384:512]
        negmf = cm[:, 512:640]
        cmb = sb("cmb", [128, 384], BF16)
        cp(cmb[:], cm[:, 0:384], ["cm"], ["cmb"])
        identb = cmb[:, 0:128]
        trib = cmb[:, 128:256]
        ntrib = cmb[:, 256:384]
        nw = sb("nw", [128, DEPTH, 8])
        for l_ in range(DEPTH):
            dma(nw[:, l_, :], norm_w[l_].rearrange("(c p) -> p c", p=128), (), ["nw"], "nw", slow=True)
        ow = sb("ow", [128, DEPTH, 12])
        mset(ow[:], 1.0, ["ow"])
        for l_ in range(DEPTH):
            dma(ow[:, l_, 0:4], a_nw[l_].rearrange("(c p) -> p c", p=128), (), ["ow"], "ow_a", slow=True)
            dma(ow[:, l_, 4:8], b_nw[l_].rearrange("(c p) -> p c", p=128), (), ["ow"], "ow_b", slow=True)

        Wb = sb("Wb", [128, 8, D_IN], BF16)
        Wo = sb("Wo", [128, 12, 1024], BF16)
        wst = [sb("wst%d" % i, [128, 1024]) for i in range(2)]

        qkw_b = sb("qkw_b", [128, 2, 64])
        par8 = sb("par8", [128, 5, 8])
        cwt = sb("cwt", [128, 8, 4])
        cbt = sb("cbt", [128, 8])

        pj = ps("pj", [128, 2, 512])
        ptr = ps("ptr", [128, 8, 128], BF16)
        ptf = ps("ptf", [128, 512])
        pmA = ps("pmA", [128, 2, 512])
        pmB = ps("pmB", [128, 2, 512])

        xt = [sb("xt%d" % i, [128, 1024]) for i in range(2)]
        Eall = sb("Eall", [128, 4096])
        E = [Eall[:, i * 1024:(i + 1) * 1024] for i in range(4)]
        st1 = sb("st1", [128, 16])
        xsb = sb("xsb", [128, 1024], BF16)
        xnT = sb("xnT", [128, 8, 128], BF16)
        utm = sb("utm", [128, 3600])
        qkT = sb("qkT", [64, 8, 128], BF16)
        cvb = sb("cvb", [128, 8, 131])
        cacc = E[0].rearrange("p (c t) -> p c t", c=8)
        ctmp = E[1].rearrange("p (c t) -> p c t", c=8)
        xbcT = sb("xbcT", [128, 8, 128], BF16)
        ycat = sb("ycat", [128, 1536], BF16)
        ycatT = sb("ycatT", [128, 12, 128], BF16)
        ropet = [sb("ropet%d" % i, [128, 128]) for i in range(2)]
        sm8 = sb("sm8", [128, 8, 8])
        Rt = E[0].rearrange("p (c t) -> p c t", c=8)
        dec = E[1].rearrange("p (c t) -> p c t", c=8)
        wm = sb("wm", [128, 8, 128], BF16)
        STm_t = sb("STm", [128, 4, 128], BF16)
        STm = STm_t[:]
        xBtm = sb("xBtm", [128, 6, 128], BF16)
        xdt = sb("xdt", [128, 512], BF16)
        xss = sb("xss", [128, 512], BF16)
        hT = sb("hT", [128, 512])
        hTb = sb("hTb", [128, 512], BF16)
        yb1 = wst[0][:, 0:512]
        yb2 = wst[0][:, 512:1024]
        g4 = sb("g4", [128, 8, 4])
        gT = sb("gT", [4, 8, 128])
        mst = sb("mst", [4, 8])
        dg4 = sb("dg4", [4, 4])
        facb = sb("facb", [64, 4])
        kbf = sb("kbf", [128, 256], BF16)
        vaug = sb("vaug", [128, 4, 132], BF16)
        CnT = sb("CnT", [64, 4, 132])
        CnB = sb("CnB", [64, 4, 132], BF16)
        hb = E[2][:, 0:512]
        hb2 = E[2][:, 512:1024]
        gsig = E[3][:, 0:512]
        gsil = E[3][:, 512:1024]
        qkn = wst[1][:, 0:640]
        qkA_t = sb("qkA", [128, 640], BF16)
        qkA = qkA_t[:]
        qkB_t = sb("qkB", [128, 640], BF16)
        qkB = qkB_t[:]
        qkp = sb("qkp", [128, 640], BF16)
        kvf = wst[1][:, 640:896]
        qTs = sb("qTs", [64, 8, 128], BF16)
        kTs = [sb("kTs%d" % i, [64, 2, 128], BF16) for i in range(2)]
        vsw = [sb("vsw%d" % i, [128, 2, 66], BF16) for i in range(2)]
        pex = [sb("pex%d" % i, [128, 512], BF16) for i in range(2)]
        osw = wst[1][:, 0:512]
        gsil2t = sb("gsil2", [128, 512], BF16)
        gsil2 = gsil2t[:]
        dsk_t = E[0][:, 0:512]
        st10 = sb("st10", [128, 4, 16])
        ost = E[0].rearrange("p (c t) -> p c t", c=8)
        ostm = E[2].rearrange("p (c t) -> p c t", c=8)

        def load_layer(l):
            dma(qkw_b[:, 0, :], qnw[l].partition_broadcast(128), (), ["qkw_b"], "par2")
            dma(qkw_b[:, 1, :], knw[l].partition_broadcast(128), (), ["qkw_b"], "par3")
            dma(par8[:, 0, :], dtb[l].partition_broadcast(128), (), ["par8"], "par4")
            dma(par8[:, 1, :], alog[l].partition_broadcast(128), (), ["par8"], "par5")
            dma(par8[:, 2, :], dsk[l].partition_broadcast(128), (), ["par8"], "par6")
            dma(par8[:, 3, :], snk[l].partition_broadcast(128), (), ["par8"], "par7")
            dma(par8[:, 4, 0:4], a_ib[l].partition_broadcast(128), (), ["par8"], "par8")
            dma(par8[:, 4, 4:8], a_fb[l].partition_broadcast(128), (), ["par8"], "par9")
            act(par8[:, 1, :], par8[:, 1, :], AF.Exp, ["par8"], ["par8"])
            ts(par8[:, 1, :], par8[:, 1, :], -1.0, None, ALU.mult, None, ["par8"], ["par8"])
            act(par8[:, 3, :], par8[:, 3, :], AF.Exp, ["par8"], ["par8"])
            for j in range(4):
                dma(cwt[:, :, j], cw[l, j].rearrange("(c p) -> p c", p=128), (), ["cwt"], "par10", slow=True)
            dma(cbt[:], cb[l].rearrange("(c p) -> p c", p=128), (), ["cbt"], "par11", slow=True)
            i = 0
            for kc in range(8):
                for q4 in range(5):
                    c0 = q4 * 1024
                    w_ = min(1024, D_IN - c0)
                    stg = wst[i % 2]
                    key = "wst%d" % (i % 2)
                    dma(stg[:, 0:w_], w_in[l, kc * 128:(kc + 1) * 128, c0:c0 + w_], (), [key], key)
                    if i % 2:
                        act(Wb[:, kc, c0:c0 + w_], stg[:, 0:w_], AF.Copy, [key, "nw"], ["Wb"], scale=nw[:, l, kc:kc + 1])
                    else:
                        ts(Wb[:, kc, c0:c0 + w_], stg[:, 0:w_], nw[:, l, kc:kc + 1], None, ALU.mult, None, [key, "nw"], ["Wb"])
                    i += 1
            ts(Wb[:, :, 256:512], Wb[:, :, 256:512], 0.125, None, ALU.mult, None, ["Wb"], ["Wb"])
            for kc in range(12):
                stg = wst[i % 2]
                key = "wst%d" % (i % 2)
                dma(stg[:, 0:1024], w_out[l, kc * 128:(kc + 1) * 128, :], (), [key], key)
                if i % 2:
                    act(Wo[:, kc, :], stg[:, 0:1024], AF.Copy, [key, "ow"], ["Wo"], scale=ow[:, l, kc:kc + 1])
                else:
                    ts(Wo[:, kc, :], stg[:, 0:1024], ow[:, l, kc:kc + 1], None, ALU.mult, None, [key, "ow"], ["Wo"])
                i += 1
            mset(CnT[:], 0.0, ["CnT"])
            mset(hT[:], 0.0, ["hT"])
            mset(mst[:], 0.0, ["mst"])
            mset(cvb[:], 0.0, ["cvb"])
            mset(hTb[:], 0.0, ["hTb"])

        def front_a(l, t):
            par = t % 2
            x = xt[par]
            xk = "xt%d" % par
            src = xp if l == 0 else yp
            dma(x[:], src[t * 128:(t + 1) * 128, :], ["yp%d" % t], [xk], xk)
            rp = ropet[par]
            rk = "ropet%d" % par
            dma(rp[:], rope[t * 128:(t + 1) * 128, :], (), [rk], rk)
            act(xsb[:], x[:], AF.Square, [xk], ["xsb", "st1"], scale=1.0 / 32, accum_out=st1[:, 0:1])
            rsq(st1[:, 1:2], st1[:, 0:1], 1.0, ["st1"], ["st1"])
            act(xsb[:], x[:], AF.Copy, [xk, "st1"], ["xsb"], scale=st1[:, 1:2])
            with atomic():
                for c in range(8):
                    tr(ptr[:, c, :], xsb[:, c * 128:(c + 1) * 128], identb, ["xsb", "cmb"], ["ptr"])
                cp(xnT[:], ptr[:], ["ptr"], ["xnT"])

        def proj_utm(l, t, banks):
            chunks = []
            for (c0, n, off) in ((256, 2312, 0), (3592, 1288, 2312)):
                o = 0
                while o < n:
                    w_ = min(512, n - o)
                    chunks.append((c0 + o, w_, off + o))
                    o += w_
            for i, (c0, w_, off) in enumerate(chunks):
                b = banks[i % len(banks)]
                for kc in range(8):
                    mm(pj[:, b, 0:w_], xnT[:, kc, :], Wb[:, kc, c0:c0 + w_], kc == 0, kc == 7, ["xnT", "Wb"], ["pj%d" % b])
                cp(utm[:, off:off + w_], pj[:, b, 0:w_], ["pj%d" % b], ["utm"], eng=("scalar" if i % 2 else "vector"))

        def tile_step(l, t):
            par = t % 2
            x = xt[par]
            xk = "xt%d" % par
            rp = ropet[par]
            rk = "ropet%d" % par
            if t == 0:
                front_a(l, 0)
                proj_utm(l, 0, [0, 1])
            for c in range(8):
                for kc in range(8):
                    mm(pmA[0:64, c // 4, (c % 4) * 128:(c % 4 + 1) * 128], Wb[:, kc, c * 64:(c + 1) * 64], xnT[:, kc, :], kc == 0, kc == 7,
                       ["xnT", "Wb"], ["pmA%d" % (c // 4)])
            cp(qkT[:], pmA[0:64, :, :].rearrange("p b (c t) -> p (b c) t", c=4), ["pmA0", "pmA1"], ["qkT"], eng="scalar")
            for c in range(8):
                for kc in range(8):
                    mm(pmB[:, c // 4, (c % 4) * 128:(c % 4 + 1) * 128], Wb[:, kc, 2568 + c * 128:2568 + (c + 1) * 128], xnT[:, kc, :],
                       kc == 0, kc == 7, ["xnT", "Wb"], ["pmB%d" % (c // 4)])
            cp(cvb[:, :, 3:131], pmB[:].rearrange("p b (c t) -> p (b c) t", c=4), ["pmB0", "pmB1"], ["cvb"])

            act(gsig, utm[:, 768:1280], AF.Sigmoid, ["utm"], ["E3"])
            act(gsil, utm[:, 1280:1792], AF.Silu, ["utm"], ["E3"])
            act(yb2, utm[:, 1800:2312], AF.Silu, ["utm"], ["wst0g", "wst0"])
            act(gsil2, utm[:, 3088:3600], AF.Silu, ["utm"], ["gsil2"])
            tt(gsig, gsig, gsil, ALU.mult, ["E3", "E3"], ["E3"], eng="gpsimd")
            fns = [lambda: mlstm_tile(l, t), lambda: ssd_tile(l, t), lambda: swa_tile(l, t, rp, rk), lambda: ssd_a(l, t)]
            def tail_chain():
                if t > 0:
                    outproj_mm(l, t - 1)
                if t + 1 < NT:
                    front_a(l, t + 1)
                    for nm_ in ("utm_m", "utm_a", "utm_s"):
                        wait_label(nm_)
                    proj_utm(l, t + 1, [1])
            fns.append(tail_chain)
            run_chains(fns, prio=[0.0, 0.0, 0.0, 0.0, -1.0])
            for rnd, (k0, k1) in enumerate(((0, 8), (8, 12))):
                for kc in range(k0, k1):
                    tr(ptr[:, kc - k0, :], ycat[:, kc * 128:(kc + 1) * 128], identb, ["ycat0", "ycat1", "ycat2", "cmb"], ["ptr"])
                cp(ycatT[:, k0:k1, :], ptr[:, 0:k1 - k0, :], ["ptr"], ["ycatT"], eng=("vector" if rnd else "scalar"))
            if t == NT - 1:
                outproj_mm(l, t)

        def outproj_mm(l, t):
            par = t % 2
            x = xt[par]
            xk = "xt%d" % par
            for n in range(2):
                for kc in range(12):
                    mm(pj[:, 1, :], ycatT[:, kc, :], Wo[:, kc, n * 512:(n + 1) * 512], kc == 0, kc == 11, ["ycatT", "Wo"], ["pj1"])
                tt(x[:, n * 512:(n + 1) * 512], pj[:, 1, :], x[:, n * 512:(n + 1) * 512], ALU.add, ["pj1", xk], [xk])
            dma(yp[t * 128:(t + 1) * 128, :], x[:], [xk], ["yp%d" % t], "ypst%d" % par)

        def mlstm_tile(l, t):
            last = (t == NT - 1)
            tt(g4[:, 0, :], utm[:, 1792:1796], par8[:, 4, 0:4], ALU.add, ["utm", "par8"], ["g4"])
            tt(g4[:, 1, :], utm[:, 1796:1800], par8[:, 4, 4:8], ALU.add, ["utm", "par8"], ["g4"])
            act(g4[:, 2, :], g4[:, 1, :], AF.Exp, ["g4"], ["g4"], scale=-1.0)
            act(g4[:, 2, :], g4[:, 2, :], AF.Ln, ["g4"], ["g4"], bias=1.0)
            mm(ptf[:, 0:4], trif, g4[:, 2, :], True, True, ["cm", "g4"], ["ptf"])
            cp(g4[:, 3, :], ptf[:, 0:4], ["ptf"], ["g4"])
            tt(g4[:, 4, :], g4[:, 0, :], g4[:, 3, :], ALU.add, ["g4"], ["g4"])
            tr(ptf[0:4, 0:128], g4[:, 4, :], identf, ["g4", "cm"], ["ptf"])
            tr(ptf[0:4, 128:256], g4[:, 3, :], identf, ["g4", "cm"], ["ptf"])
            cp(gT[:, 0, :], ptf[0:4, 0:128], ["ptf"], ["gT"])
            cp(gT[:, 1, :], ptf[0:4, 128:256], ["ptf"], ["gT"])
            red(mst[:, 1:2], gT[:, 0, :], ALU.max, ["gT"], ["mst"])
            tt(mst[:, 2:3], mst[:, 1:2], mst[:, 0:1], ALU.max, ["mst"], ["mst"])
            ts(mst[:, 3:4], mst[:, 2:3], -1.0, None, ALU.mult, None, ["mst"], ["mst"])
            tt(mst[:, 5:6], mst[:, 0:1], mst[:, 2:3], ALU.subtract, ["mst"], ["mst"])
            act(mst[:, 4:5], mst[:, 5:6], AF.Exp, ["mst"], ["mst"])
            act(gT[:, 2, :], gT[:, 0, :], AF.Exp, ["gT", "mst"], ["gT"], bias=mst[:, 3:4])
            act(gT[:, 3, :], gT[:, 1, :], AF.Exp, ["gT", "mst"], ["gT"], bias=mst[:, 3:4])
            tt(mst[:, 0:1], mst[:, 2:3], gT[:, 1, 127:128], ALU.subtract, ["mst", "gT"], ["mst"])
            tr(ptf[:, 256:260], gT[:, 2, :], identf[0:4, 0:4], ["gT", "cm"], ["ptf"])
            tr(ptf[:, 260:264], gT[:, 3, :], identf[0:4, 0:4], ["gT", "cm"], ["ptf"])
            cp(g4[:, 5:7, :], ptf[:, 256:264].rearrange("p (a h) -> p a h", a=2), ["ptf"], ["g4"])
            ts(dg4[:], identf[0:4, 0:4], mst[:, 4:5], None, ALU.mult, None, ["cm", "mst"], ["dg4"])
            mm(ptf[0:64, 264:268], onesf[0:4, 0:64], dg4[:], True, True, ["cm", "dg4"], ["ptf"])
            cp(facb[:], ptf[0:64, 264:268], ["ptf"], ["facb"])
            tt(CnT[:, :, 0:129], CnT[:, :, 0:129], facb[:].unsqueeze(2).to_broadcast([64, 4, 129]), ALU.mult, ["CnT", "facb"], ["CnT"])
            cp(CnB[:], CnT[:], ["CnT"], ["CnB"], eng="scalar")
            tt(vaug[:, :, 0:128], utm[:, 256:768].rearrange("p (h v) -> p h v", h=4), g4[:, 5, :].unsqueeze(2).to_broadcast([128, 4, 128]),
               ALU.mult, ["utm", "g4"], ["vaug"])
            cp(vaug[:, :, 128:129], g4[:, 5, :].unsqueeze(2), ["g4"], ["vaug"], eng="gpsimd")
            cp(kbf[:], utm[:, 0:256], ["utm"], ["kbf"], eng="scalar")
            label("utm_m")
            for h in range(4):
                mm(pmA[:, 1, h * 128:(h + 1) * 128], qkT[:, 4 + h, :], qkT[:, h, :], True, True, ["qkT"], ["pmA1"])
            tt(STm, pmA[:, 1, :].rearrange("p (h t) -> p h t", h=4), trif.unsqueeze(1).to_broadcast([128, 4, 128]), ALU.mult,
               ["pmA1", "cm"], ["STm"])
            for h in range(4):
                p0 = (h % 2) * 64
                o_ = pmA[:, h // 2, (h % 2) * 256:(h % 2) * 256 + 129]
                mm(o_, STm[:, h, :], vaug[:, h, 0:129], True, False, ["STm", "vaug"], ["pmA%d" % (h // 2)])
                mm(o_, qkT[:, h, :], CnB[:, h, 0:129], False, True, ["qkT", "CnB"], ["pmA%d" % (h // 2)])
            pn4 = pmA[:].rearrange("p b (h c) -> p (b h) c", h=2)
            act(g4[:, 7, :], pn4[:, :, 128], AF.Abs, ["pmA0", "pmA1"], ["g4"])
            tt(g4[:, 7, :], g4[:, 7, :], g4[:, 6, :], ALU.max, ["g4"], ["g4"])
            op("vector", lambda e: e.reciprocal(out=g4[:, 7, :], in_=g4[:, 7, :]), ["g4"], ["g4"])
            tt(hb.rearrange("p (h v) -> p h v", h=4), pn4[:, :, 0:128], g4[:, 7, :].unsqueeze(2).to_broadcast([128, 4, 128]), ALU.mult,
               ["pmA0", "pmA1", "g4"], ["E2"])
            for h in range(4):
                p0 = (h % 2) * 64
                mm(pmA[0:64, h // 2, (h % 2) * 256:(h % 2) * 256 + 129], kbf[:, h * 64:(h + 1) * 64], vaug[:, h, 0:129], True, True,
                   ["kbf", "vaug"], ["pmA%d" % (h // 2)])
            for h in range(4):
                p0 = (h % 2) * 64
                tt(CnT[:, h, 0:129], CnT[:, h, 0:129], pmA[0:64, h // 2, (h % 2) * 256:(h % 2) * 256 + 129], ALU.add,
                   ["CnT", "pmA%d" % (h // 2)], ["CnT"])
            for h in range(4):
                act(hb2[:, h * 128:(h + 1) * 128], hb[:, h * 128:(h + 1) * 128], AF.Square, ["E2"], ["E2", "st10a"], accum_out=st10[:, 0, h:h + 1])
            rsq(st10[:, 0, 4:8], st10[:, 0, 0:4], 1.0 / 128, ["st10a"], ["st10a"])
            tt(hb.rearrange("p (h v) -> p h v", h=4), hb.rearrange("p (h v) -> p h v", h=4),
               st10[:, 0, 4:8].unsqueeze(2).to_broadcast([128, 4, 128]), ALU.mult, ["E2", "st10a"], ["E2"])
            tt(ycat[:, 0:512], hb, gsig, ALU.mult, ["E2", "E3"], ["ycat0"])
            if last:
                for h in range(4):
                    p0 = (h % 2) * 64
                    tr(ptf[:, h * 64:(h + 1) * 64], CnT[:, h, 0:128], identf[0:64, 0:64], ["CnT", "cm"], ["ptf"])
                cp(ostm[:, 0:2, :], ptf[:, 0:256].rearrange("p (a b) -> p a b", a=2), ["ptf"], ["E2"])
                dma(o_pC[l].rearrange("h v k -> v h k"), ostm[:, 0:2, :].rearrange("p a (c k) -> p (a c) k", c=2), ["E2"], [], "o_pC")
                for h in range(4):
                    p0 = (h % 2) * 64
                    dma(o_pn[l, h].unsqueeze(1), CnT[:, h, 128:129], ["CnT"], [], "o_pn", slow=True)
                dma(o_pm[l].unsqueeze(1), mst[:, 0:1], ["mst"], [], "o_pm", slow=True)

        def ssd_a(l, t):
            RtH = wst[0][:, 0:512]
            tt(sm8[:, 0, :], utm[:, 2312:2320], par8[:, 0, :], ALU.add, ["utm", "par8"], ["sm8"])
            label("utm_a")
            act(sm8[:, 0, :], sm8[:, 0, :], AF.Exp, ["sm8"], ["sm8"])
            act(sm8[:, 0, :], sm8[:, 0, :], AF.Ln, ["sm8"], ["sm8"], bias=1.0)
            tt(sm8[:, 1, :], sm8[:, 0, :], par8[:, 1, :], ALU.mult, ["sm8", "par8"], ["sm8"])
            mm(pmB[:, 1, 0:8], trif, sm8[:, 1, :], True, True, ["cm", "sm8"], ["pmB1"])
            ts(sm8[:, 2, :], pmB[:, 1, 0:8], -1.0, None, ALU.mult, None, ["pmB1"], ["sm8"])
            act(sm8[:, 3, :], pmB[:, 1, 0:8], AF.Exp, ["pmB1"], ["sm8"])
            for b in range(2):
                tt(RtH.rearrange("p (h t) -> p h t", h=4), trif.unsqueeze(1).to_broadcast([128, 4, 128]),
                   sm8[:, 1, 4 * b:4 * b + 4].unsqueeze(2).to_broadcast([128, 4, 128]), ALU.mult, ["cm", "sm8"], ["wst0"])
                mm(pmB[:, b, :], onesf, RtH, True, False, ["cm", "wst0"], ["pmB%d" % b])
                for h in range(4):
                    mm(pmB[:, b, h * 128:(h + 1) * 128], identf, negmf, False, h == 3, ["cm"], ["pmB%d" % b])
            pa8 = pmB[:].rearrange("p b (h t) -> p (b h) t", h=4)
            for h in range(8):
                act(dec[:, h, :], pa8[:, h, :], AF.Exp, ["pmB0", "pmB1", "sm8"], ["E1"], bias=sm8[:, 2, h:h + 1])
            act(sm8[:, 4, :], pa8[:, :, 127], AF.Exp, ["pmB0", "pmB1"], ["sm8"])
            label("dec_ready")

        def ssd_tile(l, t):
            last = (t == NT - 1)
            ck_ = ["E0c%d" % c for c in range(8)]
            for c in range(8):
                act(cacc[:, c, :], cvb[:, c, 3:131], AF.Identity, ["cvb", "cwt", "cbt", "E0"], [ck_[c]], scale=cwt[:, c, 3:4], bias=cbt[:, c:c + 1])
            for j in (2, 1, 0):
                for c in range(8):
                    stt(cacc[:, c, :], cvb[:, c, j:j + 128], cwt[:, c, j:j + 1], cacc[:, c, :], ALU.mult, ALU.add, ["cvb", "cwt", ck_[c]], [ck_[c]])
            act(xbcT[:], cacc, AF.Silu, ck_, ["xbcT", "E0"])
            if last:
                for r_ in range(3):
                    dma(o_pconv[l, r_].rearrange("(c p) -> p c", p=128), cvb[:, :, 128 + r_], ["cvb"], [], "o_pconv", slow=True)
            cp(cvb[:, :, 0:3], cvb[:, :, 128:131], ["cvb"], ["cvb"], eng="gpsimd")
            with atomic():
                for c in range(6):
                    tr(ptr[:, c, :], xbcT[:, c, :], identb, ["xbcT", "cmb"], ["ptr"])
                cp(xBtm[:], ptr[:, 0:6, :], ["ptr"], ["xBtm"], eng="scalar")
            xtm = xBtm[:, 0:4, :].rearrange("p c (e d) -> p (c e) d", e=2)
            wait_label("dec_ready")
            tt(xdt[:].rearrange("p (h d) -> p h d", h=8), xtm, sm8[:, 0, :].unsqueeze(2).to_broadcast([128, 8, 64]), ALU.mult,
               ["xBtm", "sm8"], ["xdt"])
            tt(xss[:].rearrange("p (h d) -> p h d", h=8), xdt[:].rearrange("p (h d) -> p h d", h=8),
               dec[:, :, 127].unsqueeze(2).to_broadcast([128, 8, 64]), ALU.mult, ["xdt", "E1"], ["xss"], eng="gpsimd")
            for g in range(2):
                mm(pmB[:, 0, g * 128:(g + 1) * 128], xbcT[:, 4 + g, :], xbcT[:, 6 + g, :], True, True, ["xbcT"], ["pmB0"])
            for g in range(2):
                tt(wm[:, g * 4:(g + 1) * 4, :], dec[:, g * 4:(g + 1) * 4, :], pmB[:, 0, g * 128:(g + 1) * 128].unsqueeze(1).to_broadcast([128, 4, 128]),
                   ALU.mult, ["E1", "pmB0"], ["wm"])
            for h in range(8):
                mm(pmB[:, 1, h * 64:(h + 1) * 64], wm[:, h, :], xdt[:, h * 64:(h + 1) * 64], True, True, ["wm", "xdt"], ["pmB1"])
            for g in range(2):
                mm(pmB[:, 0, g * 256:(g + 1) * 256], xbcT[:, 6 + g, :], hTb[:, g * 256:(g + 1) * 256], True, True, ["xbcT", "hTb"], ["pmB0"])
            tt(yb1.rearrange("p (h d) -> p h d", h=8), pmB[:, 0, :].rearrange("p (h d) -> p h d", h=8),
               sm8[:, 3, :].unsqueeze(2).to_broadcast([128, 8, 64]), ALU.mult, ["pmB0", "sm8"], ["wst0"])
            tt(yb1, yb1, pmB[:, 1, :], ALU.add, ["wst0", "pmB1"], ["wst0"])
            tt(dsk_t.rearrange("p (h d) -> p h d", h=8), xtm, par8[:, 2, :].unsqueeze(2).to_broadcast([128, 8, 64]), ALU.mult,
               ["xBtm", "par8"], ["E0"], eng="gpsimd")
            tt(yb1, yb1, dsk_t, ALU.add, ["wst0", "E0"], ["wst0"])
            tt(yb1, yb1, yb2, ALU.mult, ["wst0", "wst0g"], ["wst0"])
            for g in range(2):
                act(cacc[:, g, :], yb1[:, g * 256:(g + 1) * 256].rearrange("p (a b) -> p a b", a=2)[:, 0, :], AF.Square, ["wst0"], ["E0", "st10b"],
                    accum_out=st10[:, 1, 4 + g:5 + g])
                act(cacc[:, 2 + g, :], yb1[:, g * 256:(g + 1) * 256].rearrange("p (a b) -> p a b", a=2)[:, 1, :], AF.Square, ["wst0"], ["E0", "st10b"],
                    accum_out=st10[:, 1, 6 + g:7 + g])
            tt(st10[:, 1, 0:2], st10[:, 1, 4:6], st10[:, 1, 6:8], ALU.add, ["st10b"], ["st10b"])
            rsq(st10[:, 1, 2:4], st10[:, 1, 0:2], 1.0 / 256, ["st10b"], ["st10b"])
            for g in range(2):
                ts(ycat[:, 512 + g * 256:512 + (g + 1) * 256], yb1[:, g * 256:(g + 1) * 256], st10[:, 1, 2 + g:3 + g], None, ALU.mult, None,
                   ["wst0", "st10b"], ["ycat1"])
            for g in range(2):
                mm(pmB[:, 0, g * 256:(g + 1) * 256], xBtm[:, 4 + g, :], xss[:, g * 256:(g + 1) * 256], True, True, ["xBtm", "xss"], ["pmB0"])
            tt(hT[:].rearrange("p (h d) -> p h d", h=8), hT[:].rearrange("p (h d) -> p h d", h=8),
               sm8[:, 4, :].unsqueeze(2).to_broadcast([128, 8, 64]), ALU.mult, ["hT", "sm8"], ["hT"], eng="gpsimd")
            tt(hT[:], hT[:], pmB[:, 0, :], ALU.add, ["hT", "pmB0"], ["hT"])
            cp(hTb[:], hT[:], ["hT"], ["hTb"], eng="scalar")
            if last:
                for h in range(8):
                    tr(pmB[0:64, h // 4, (h % 4) * 128:(h % 4 + 1) * 128], hT[:, h * 64:(h + 1) * 64], identf, ["hT", "cm"], ["pmB%d" % (h // 4)])
                cp(ost[0:64, :, :], pmB[0:64, :, :].rearrange("p b (h s) -> p (b h) s", h=4), ["pmB0", "pmB1"], ["E0"])
                dma(o_ph[l].rearrange("h p s -> p h s"), ost[0:64, :, :], ["E0"], [], "o_ph")

        def swa_tile(l, t, rp, rk):
            last = (t == NT - 1)
            par = t % 2
            qkraw = utm[:, 2320:2960].rearrange("p (h d) -> p h d", h=10)
            tt(qkA, utm[:, 2320:2960], utm[:, 2320:2960], ALU.mult, ["utm"], ["qkA"], eng="gpsimd")
            red(st10[:, 2, 0:10], qkA.rearrange("p (h d) -> p h d", h=10), ALU.add, ["qkA"], ["st10c"])
            rsq(st10[:, 3, 0:10], st10[:, 2, 0:10], 1.0 / 64, ["st10c"], ["st10c"])
            qkn3 = qkn.rearrange("p (h d) -> p h d", h=10)
            tt(qkn3, qkraw, st10[:, 3, 0:10].unsqueeze(2).to_broadcast([128, 10, 64]), ALU.mult, ["utm", "st10c"], ["wst1"])
            vs = vsw[par]
            vk = "vsw%d" % par
            cp(vs[:, :, 0:64], utm[:, 2960:3088].rearrange("p (g d) -> p g d", g=2), ["utm"], [vk], eng="scalar")
            mset(vs[:, :, 64:65], 1.0, [vk])
            if last:
                cp(kvf[:, 128:256], utm[:, 2960:3088], ["utm"], ["wst1"], eng="gpsimd")
            label("utm_s")
            tt(qkn3[:, 0:8, :], qkn3[:, 0:8, :], qkw_b[:, 0, :].unsqueeze(1).to_broadcast([128, 8, 64]), ALU.mult, ["wst1", "qkw_b"], ["wst1"], eng="gpsimd")
            tt(qkn3[:, 8:10, :], qkn3[:, 8:10, :], qkw_b[:, 1, :].unsqueeze(1).to_broadcast([128, 2, 64]), ALU.mult, ["wst1", "qkw_b"], ["wst1"], eng="gpsimd")
            qkA3 = qkA.rearrange("p (h d) -> p h d", h=10)
            qkB3 = qkB.rearrange("p (h d) -> p h d", h=10)
            tt(qkA3, qkn3, rp[:, 0:64].unsqueeze(1).to_broadcast([128, 10, 64]), ALU.mult, ["wst1", rk], ["qkA"])
            tt(qkB3[:, :, 0:32], qkn3[:, :, 32:64], rp[:, 64:96].unsqueeze(1).to_broadcast([128, 10, 32]), ALU.mult, ["wst1", rk], ["qkB"], eng="gpsimd")
            tt(qkB3[:, :, 32:64], qkn3[:, :, 0:32], rp[:, 96:128].unsqueeze(1).to_broadcast([128, 10, 32]), ALU.mult, ["wst1", rk], ["qkB"], eng="gpsimd")
            tt(qkp[:], qkA, qkB, ALU.add, ["qkA", "qkB"], ["qkp"])
            if last:
                kt_ = wst[1][:, 896:1024].rearrange("p (h d) -> p h d", h=2)
                kk_ = qkn3[:, 8:10, :]
                ko_ = kvf[:, 0:128].rearrange("p (h d) -> p h d", h=2)
                tt(ko_, kk_, rp[:, 0:64].unsqueeze(1).to_broadcast([128, 2, 64]), ALU.mult, ["wst1", rk], ["wst1"])
                tt(kt_[:, :, 0:32], kk_[:, :, 32:64], rp[:, 64:96].unsqueeze(1).to_broadcast([128, 2, 32]), ALU.mult, ["wst1", rk], ["wst1"])
                tt(kt_[:, :, 32:64], kk_[:, :, 0:32], rp[:, 96:128].unsqueeze(1).to_broadcast([128, 2, 32]), ALU.mult, ["wst1", rk], ["wst1"])
                tt(ko_, ko_, kt_, ALU.add, ["wst1"], ["wst1"])
                dma(o_pk[l].rearrange("t g d -> t (g d)"), kvf[:, 0:128], ["wst1"], [], "o_pk")
                dma(o_pv[l].rearrange("t g d -> t (g d)"), kvf[:, 128:256], ["wst1"], [], "o_pv")
            kT = kTs[par]
            kTk = "kTs%d" % par
            with atomic():
                for c in range(8):
                    tr(ptr[0:64, c, :], qkp[:, c * 64:(c + 1) * 64], identb, ["qkp", "cmb"], ["ptr"])
                cp(qTs[:], ptr[0:64, :, :], ["ptr"], ["qkT2"])
            with atomic():
                for c in range(2):
                    tr(ptr[0:64, c, :], qkp[:, 512 + c * 64:512 + (c + 1) * 64], identb, ["qkp", "cmb"], ["ptr"])
                cp(kT[:], ptr[0:64, 0:2, :], ["ptr"], [kTk], eng="scalar")
            blocks = [(kTs[1 - par], "kTs%d" % (1 - par), vsw[1 - par], "vsw%d" % (1 - par), ntrib)] if t > 0 else []
            blocks.append((kT, kTk, vs, vk, trib))
            nb = len(blocks)
            for g in range(2):
                for bi, (kT_, kTk_, v_, vk_, msk) in enumerate(blocks):
                    bk = 0
                    mm(pj[:, bk, :], kT_[:, g, :], qTs[:, 4 * g:4 * g + 4, :].rearrange("p h t -> p (h t)"), True, True,
                       [kTk_, "qkT2"], ["pj%d" % bk])
                    act(pex[bi][:], pj[:, bk, :], AF.Exp, ["pj%d" % bk], ["pex%d" % bi], scale=0.125)
                    tt(pex[bi][:].rearrange("p (h t) -> p h t", h=4), pex[bi][:].rearrange("p (h t) -> p h t", h=4),
                       msk.unsqueeze(1).to_broadcast([128, 4, 128]), ALU.mult, ["pex%d" % bi, "cmb"], ["pex%d" % bi],
                       eng=("gpsimd" if bi else "vector"))
                for j in range(4):
                    for bi, (kT_, kTk_, v_, vk_, msk) in enumerate(blocks):
                        mm(pj[:, 0, j * 128:j * 128 + 65], pex[bi][:, j * 128:(j + 1) * 128], v_[:, g, 0:65], bi == 0, bi == nb - 1,
                           ["pex%d" % bi, vk_], ["pj0"])
                po = pj[:, 0, :].rearrange("p (h c) -> p h c", h=4)
                tt(st10[:, 2, 4 * g:4 * g + 4], po[:, :, 64], par8[:, 3, 4 * g:4 * g + 4], ALU.add, ["pj0", "par8"], ["st10c"])
                op("vector", lambda e, g=g: e.reciprocal(out=st10[:, 2, 4 * g:4 * g + 4], in_=st10[:, 2, 4 * g:4 * g + 4]), ["st10c"], ["st10c"])
                tt(osw[:, g * 256:(g + 1) * 256].rearrange("p (h d) -> p h d", h=4), po[:, :, 0:64],
                   st10[:, 2, 4 * g:4 * g + 4].unsqueeze(2).to_broadcast([128, 4, 64]), ALU.mult, ["pj0", "st10c"], ["wst1"])
            tt(ycat[:, 1024:1536], osw, gsil2, ALU.mult, ["wst1", "gsil2"], ["ycat2"])

        if with_sample:
            cst = sb("cst", [128, 1152])
            dma(cst[:], csel, (), ["cst"], "cst")

            class _Sel8:
                def __getitem__(self, key):
                    j = key[1]
                    return cst[0:16, 7 - j:135 - j]
            Sel8 = _Sel8()
            Sel8T = cst[:, 136:264].rearrange("p (j b) -> p j b", j=8)
            Sel2 = cst[0:64, 264:392]
            SelP4 = cst[0:4, 392:520]
            Pair = cst[:, 520:648]
            Quad = cst[:, 648:776]
            Sel2T = cst[:, 776:840]
            M8 = cst[:, 840:848]
            ropeS = cst[0:16, 1024:1152]
            xs_t = sb("xs_t", [16, 1024])
            dma(xs_t[:], xs_in, (), ["xs_t"], "xs_t")
            xnTs = xnT[:, :, 0:16]
            mA = wst[0][:, 0:392]
            mB = wst[0][:, 392:464]
            mP = sb("mP", [128, 8])
            ms = sb("ms", [128, 16])
            mn1 = wst[0][:, 464:536]
            t64 = wst[1][:, 256:320]
            mh = wst[1][:, 320:384]
            g1 = wst[1][:, 384:448]
            g2 = wst[1][:, 448:512]
            nm64 = wst[1][0:64, 512:584]
            par4 = sb("par4", [4, 2])
            sd = sb("sd", [16, 16])
            fmc = xt[1][:, 0:512].rearrange("p (r c b) -> p r c b", r=4, c=8)
            fma = wst[1][:, 0:128].rearrange("p (c b) -> p c b", c=8)
            fmb = wst[1][:, 128:256].rearrange("p (c b) -> p c b", c=8)
            sTt = wst[0][:, 664:792]
            oTs = wst[0][0:64, 536:664]
            qm = xt[1][0:16, 0:512]
            ycs = ycat[0:16, :]
            ycTs = ycatT[:, :, 0:16]
            sb_qm2 = xt[1][0:16, 512:1024]
            sTmp = xt[0][:, 0:512]
            sA = Eall[:, 0:2048]
            sB = Eall[:, 2048:4096]
            KA = ["E0", "E1"]
            KB = ["E2", "E3"]

        def sample_layer(l):
            u = utm[0:16, :]
            uq = xt[1][0:16, 0:256]
            ubx = xt[0][0:16, :]
            act(xsb[0:16, :], xs_t[:], AF.Square, ["xs_t"], ["xsb", "st1"], scale=1.0 / 32, accum_out=st1[0:16, 0:1])
            rsq(st1[0:16, 1:2], st1[0:16, 0:1], 1.0, ["st1"], ["st1"])
            act(xsb[0:16, :], xs_t[:], AF.Copy, ["xs_t", "st1"], ["xsb"], scale=st1[0:16, 1:2])
            for c in range(8):
                tr(ptr[:, c, 0:16], xsb[0:16, c * 128:(c + 1) * 128], identb[0:16, 0:16], ["xsb", "cmb"], ["ptr"])
            cp(xnTs, ptr[:, :, 0:16], ["ptr"], ["xnT"])
            plan = []
            for (c0, n, dst, dk, off) in ((0, 256, uq, "xt1", 0), (256, 2312, u, "utm", 0), (2568, 1024, ubx, "xt0", 0), (3592, 1288, u, "utm", 2312)):
                o = 0
                while o < n:
                    w_ = min(512, n - o)
                    plan.append((c0 + o, w_, dst, dk, off + o))
                    o += w_
            for i, (c0, w_, dst, dk, off) in enumerate(plan):
                b = i % 2
                for kc in range(8):
                    mm(pj[0:16, b, 0:w_], xnTs[:, kc, :], Wb[:, kc, c0:c0 + w_], kc == 0, kc == 7, ["xnT", "Wb"], ["pj%d" % b])
                cp(dst[:, off:off + w_], pj[0:16, b, 0:w_], ["pj%d" % b], [dk], eng=("scalar" if i % 2 else "vector"))

            def expand(dst_ps, pkey, srcs, rkeys):
                for j in range(8):
                    mm(dst_ps, Sel8[:, j, :], srcs[j], j == 0, j == 7, ["cst"] + rkeys, [pkey])

            hv = [(j // 2, j % 2) for j in range(8)]
            expand(pmA[:, 0, 0:64], "pmA0", [uq[:, h * 64:(h + 1) * 64] for h, v in hv], ["xt1"])
            expand(pmA[:, 0, 64:128], "pmA0", [u[:, h * 64:(h + 1) * 64] for h, v in hv], ["utm"])
            expand(pmA[:, 0, 128:192], "pmA0", [u[:, 256 + h * 128 + v * 64:256 + h * 128 + v * 64 + 64] for h, v in hv], ["utm"])
            expand(pmA[:, 0, 192:256], "pmA0", [u[:, 768 + h * 128 + v * 64:768 + h * 128 + v * 64 + 64] for h, v in hv], ["utm"])
            expand(pmA[:, 0, 256:320], "pmA0", [u[:, 1280 + h * 128 + v * 64:1280 + h * 128 + v * 64 + 64] for h, v in hv], ["utm"])
            expand(pmA[:, 0, 320:321], "pmA0", [u[:, 1792 + h:1793 + h] for h, v in hv], ["utm"])
            expand(pmA[:, 0, 321:322], "pmA0", [u[:, 1796 + h:1797 + h] for h, v in hv], ["utm"])
            cp(mA[:, 0:322], pmA[:, 0, 0:322], ["pmA0"], ["wst0"])
            q_, k_, v_ = mA[:, 0:64], mA[:, 64:128], mA[:, 128:192]
            dma(nm64[:, 0:64], i_sn[l].rearrange("b h k -> (b h) k"), (), ["wst1"], "nm64a")
            dma(nm64[:, 64:65], i_sm[l].rearrange("b (h o) -> (b h) o", o=1), (), ["wst1"], "nm64b", slow=True)
            mm(pmA[:, 1, 0:65], Sel2, nm64[:, 0:65], True, True, ["cst", "wst1"], ["pmA1"])
            dma(par4[:, 0:1], a_ib[l].rearrange("(h o) -> h o", o=1), (), ["par4"], "par4a", slow=True)
            dma(par4[:, 1:2], a_fb[l].rearrange("(h o) -> h o", o=1), (), ["par4"], "par4b", slow=True)
            mm(pmA[:, 1, 128:130], SelP4, par4[:], True, True, ["cst", "par4"], ["pmA1"])
            cp(mB[:, 0:65], pmA[:, 1, 0:65], ["pmA1"], ["wst0"])
            cp(mP[:, 0:2], pmA[:, 1, 128:130], ["pmA1"], ["mP"], eng="scalar")
            tt(ms[:, 0:1], mA[:, 321:322], mP[:, 1:2], ALU.add, ["wst0", "mP"], ["ms"])
            act(ms[:, 1:2], ms[:, 0:1], AF.Exp, ["ms"], ["ms"], scale=-1.0)
            act(ms[:, 1:2], ms[:, 1:2], AF.Ln, ["ms"], ["ms"], bias=1.0)
            tt(ms[:, 2:3], mB[:, 64:65], ms[:, 1:2], ALU.subtract, ["wst0", "ms"], ["ms"])
            tt(ms[:, 3:4], mA[:, 320:321], mP[:, 0:1], ALU.add, ["wst0", "mP"], ["ms"])
            tt(ms[:, 4:5], ms[:, 2:3], ms[:, 3:4], ALU.max, ["ms"], ["ms"])
            tt(ms[:, 5:6], ms[:, 2:3], ms[:, 4:5], ALU.subtract, ["ms"], ["ms"])
            act(ms[:, 5:6], ms[:, 5:6], AF.Exp, ["ms"], ["ms"])
            tt(ms[:, 6:7], ms[:, 3:4], ms[:, 4:5], ALU.subtract, ["ms"], ["ms"])
            act(ms[:, 6:7], ms[:, 6:7], AF.Exp, ["ms"], ["ms"])
            act(ms[:, 7:8], ms[:, 4:5], AF.Exp, ["ms"], ["ms"], scale=-1.0)
            ts(mn1[:, 0:64], mB[:, 0:64], ms[:, 5:6], None, ALU.mult, None, ["wst0", "ms"], ["wst0"])
            stt(mn1[:, 0:64], k_, ms[:, 6:7], mn1[:, 0:64], ALU.mult, ALU.add, ["wst0", "ms", "wst0"], ["wst0"])
            cp(mn1[:, 64:65], ms[:, 4:5], ["ms"], ["wst0"])
            tt(t64, mn1[:, 0:64], q_, ALU.mult, ["wst0", "wst0"], ["wst1"])
            red(ms[:, 8:9], t64, ALU.add, ["wst1"], ["ms"])
            act(ms[:, 8:9], ms[:, 8:9], AF.Abs, ["ms"], ["ms"])
            tt(ms[:, 8:9], ms[:, 8:9], ms[:, 7:8], ALU.max, ["ms"], ["ms"])
            op("vector", lambda e: e.reciprocal(out=ms[:, 8:9], in_=ms[:, 8:9]), ["ms"], ["ms"])
            Cin = i_sC[l].rearrange("b h (vh v) k -> (b h vh) (v k)", vh=2)
            Cout = o_sC[l].rearrange("b h (vh v) k -> (b h vh) (v k)", vh=2)
            for r in range(2):
                dma(sA, Cin[:, r * 2048:(r + 1) * 2048], (), KA, "sA")
                tt(sB.rearrange("p (v k) -> p v k", v=32), v_[:, r * 32:(r + 1) * 32].unsqueeze(2).to_broadcast([128, 32, 64]),
                   k_.unsqueeze(1).to_broadcast([128, 32, 64]), ALU.mult, ["wst0"], KB)
                ts(sA, sA, ms[:, 5:6], None, ALU.mult, None, KA + ["ms"], KA)
                stt(sA, sB, ms[:, 6:7], sA, ALU.mult, ALU.add, KA + KB + ["ms"], KA)
                dma(Cout[:, r * 2048:(r + 1) * 2048], sA, KA, [], "oC")
                tt(sB.rearrange("p (v k) -> p v k", v=32), sA.rearrange("p (v k) -> p v k", v=32), q_.unsqueeze(1).to_broadcast([128, 32, 64]),
                   ALU.mult, KA + ["wst0"], KB)
                red(mh[:, r * 32:(r + 1) * 32], sB.rearrange("p (v k) -> p v k", v=32), ALU.add, KB, ["wst1"])
            ts(mh, mh, ms[:, 8:9], None, ALU.mult, None, ["wst1", "ms"], ["wst1"])
            tt(t64, mh, mh, ALU.mult, ["wst1"], ["wst1"])
            red(ms[:, 9:10], t64, ALU.add, ["wst1"], ["ms"])
            mm(ptf[:, 300:301], Pair, ms[:, 9:10], True, True, ["cst", "ms"], ["ptf"])
            rsq(ms[:, 10:11], ptf[:, 300:301], 1.0 / 128, ["ptf"], ["ms"])
            ts(mh, mh, ms[:, 10:11], None, ALU.mult, None, ["wst1", "ms"], ["wst1"])
            act(g1, mA[:, 192:256], AF.Sigmoid, ["wst0"], ["wst1"])
            act(g2, mA[:, 256:320], AF.Silu, ["wst0"], ["wst1"])
            tt(g1, g1, g2, ALU.mult, ["wst1", "wst1"], ["wst1"])
            tt(mh, mh, g1, ALU.mult, ["wst1", "wst1"], ["wst1"])
            for j in range(8):
                mm(pmB[0:16, 0, j * 64:(j + 1) * 64], Sel8T[:, j, :], mh, True, True, ["cst", "wst1"], ["pmB0"])
            cp(ycs[:, 0:512], pmB[0:16, 0, :], ["pmB0"], ["ycat0"])
            mm(ptf[0:64, 304:369], Sel2T, mn1[:, 0:65], True, True, ["cst", "wst0"], ["ptf"])
            cp(nm64[:, 0:65], ptf[0:64, 304:369], ["ptf"], ["wst1"])
            dma(o_sn[l].rearrange("b h k -> (b h) k"), nm64[:, 0:64], ["wst1"], [], "on")
            dma(o_sm[l].rearrange("b (h o) -> (b h) o", o=1), nm64[:, 64:65], ["wst1"], [], "om", slow=True)

            cb3 = Eall[0:16, 0:3072]
            xbs = Eall[0:16, 3072:4096]
            dma(cb3, i_sconv[l].rearrange("b r c -> b (r c)"), (), ["E0", "E1", "E2"], "cb3")
            dma(o_sconv[l, :, 0:2, :].rearrange("b r c -> b (r c)"), cb3[:, 1024:3072], ["E0", "E1", "E2"], [], "oconv_a")
            dma(o_sconv[l, :, 2, :], ubx, ["xt0"], [], "oconv_b")
            for r in range(4):
                src = cb3[:, r * 1024:(r + 1) * 1024] if r < 3 else ubx
                for c in range(8):
                    tr(ptf[:, c * 16:(c + 1) * 16], src[:, c * 128:(c + 1) * 128], identf[0:16, 0:16], ["E0", "E1", "E2", "xt0", "cm"], ["ptf"])
                cp(fmc[:, r, :, :], ptf[:, 0:128].rearrange("p (c b) -> p c b", c=8), ["ptf"], ["xt1"], eng=("scalar" if r % 2 else "vector"))
            tt(fma, fmc[:, 3, :, :], cwt[:, :, 3].unsqueeze(2).to_broadcast([128, 8, 16]), ALU.mult, ["xt1", "cwt"], ["wst1"])
            for j in range(3):
                tt(fmb, fmc[:, j, :, :], cwt[:, :, j].unsqueeze(2).to_broadcast([128, 8, 16]), ALU.mult, ["xt1", "cwt"], ["wst1"])
                tt(fma, fma, fmb, ALU.add, ["wst1", "wst1"], ["wst1"])
            tt(fma, fma, cbt[:].unsqueeze(2).to_broadcast([128, 8, 16]), ALU.add, ["wst1", "cbt"], ["wst1"])
            act(fma, fma, AF.Silu, ["wst1"], ["wst1"])
            for c in range(8):
                pt_ = pmA if c < 4 else pmB
                tr(pt_[0:16, 1, (c % 4) * 128:(c % 4 + 1) * 128], fma[:, c, :], identf, ["wst1", "cm"], ["pmA1" if c < 4 else "pmB1"])
            cp(xbs[:, 0:512], pmA[0:16, 1, :], ["pmA1"], ["E3"])
            cp(xbs[:, 512:1024], pmB[0:16, 1, :], ["pmB1"], ["E3"], eng="scalar")
            tt(sd[:, 0:8], u[:, 2312:2320], par8[0:16, 0, :], ALU.add, ["utm", "par8"], ["sd"])
            act(sd[:, 0:8], sd[:, 0:8], AF.Exp, ["sd"], ["sd"])
            act(sd[:, 0:8], sd[:, 0:8], AF.Ln, ["sd"], ["sd"], bias=1.0)
            tt(sd[:, 8:16], sd[:, 0:8], par8[0:16, 1, :], ALU.mult, ["sd", "par8"], ["sd"])
            act(sd[:, 8:16], sd[:, 8:16], AF.Exp, ["sd"], ["sd"])
            hh = list(range(8))
            expand(pmA[:, 0, 0:64], "pmA0", [xbs[:, h * 64:(h + 1) * 64] for h in hh], ["E3"])
            expand(pmA[:, 0, 64:192], "pmA0", [xbs[:, 512 + (h // 4) * 128:512 + (h // 4 + 1) * 128] for h in hh], ["E3"])
            expand(pmA[:, 0, 192:320], "pmA0", [xbs[:, 768 + (h // 4) * 128:768 + (h // 4 + 1) * 128] for h in hh], ["E3"])
            expand(pmA[:, 0, 320:384], "pmA0", [u[:, 1800 + h * 64:1800 + (h + 1) * 64] for h in hh], ["utm"])
            expand(pmA[:, 0, 384:385], "pmA0", [sd[:, h:h + 1] for h in hh], ["sd"])
            expand(pmA[:, 0, 385:386], "pmA0", [sd[:, 8 + h:9 + h] for h in hh], ["sd"])
            cp(mA[:, 0:386], pmA[:, 0, 0:386], ["pmA0"], ["wst0"])
            x_, B_, C_, z_ = mA[:, 0:64], mA[:, 64:192], mA[:, 192:320], mA[:, 320:384]
            tt(mP[:, 0:8], par8[:, 2, :], M8, ALU.mult, ["par8", "cst"], ["mP"])
            red(ms[:, 11:12], mP[:, 0:8], ALU.add, ["mP"], ["ms"])
            tt(mP[:, 0:8], par8[:, 3, :], M8, ALU.mult, ["par8", "cst"], ["mP"])
            red(ms[:, 12:13], mP[:, 0:8], ALU.add, ["mP"], ["ms"])
            ts(t64, x_, mA[:, 384:385], None, ALU.mult, None, ["wst0"], ["wst1"])
            Hin = i_sh[l].rearrange("b h p s -> (b h) (p s)")
            Hout = o_sh[l].rearrange("b h p s -> (b h) (p s)")
            for r in range(4):
                dma(sA, Hin[:, r * 2048:(r + 1) * 2048], (), KA, "sA")
                tt(sB.rearrange("p (a s) -> p a s", a=16), t64[:, r * 16:(r + 1) * 16].unsqueeze(2).to_broadcast([128, 16, 128]),
                   B_.unsqueeze(1).to_broadcast([128, 16, 128]), ALU.mult, ["wst1", "wst0"], KB)
                stt(sA, sA, mA[:, 385:386], sB, ALU.mult, ALU.add, KA + KB + ["wst0"], KA)
                dma(Hout[:, r * 2048:(r + 1) * 2048], sA, KA, [], "oh")
                tt(sB.rearrange("p (a s) -> p a s", a=16), sA.rearrange("p (a s) -> p a s", a=16), C_.unsqueeze(1).to_broadcast([128, 16, 128]),
                   ALU.mult, KA + ["wst0"], KB)
                red(mh[:, r * 16:(r + 1) * 16], sB.rearrange("p (a s) -> p a s", a=16), ALU.add, KB, ["wst1"])
            stt(mh, x_, ms[:, 11:12], mh, ALU.mult, ALU.add, ["wst0", "ms", "wst1"], ["wst1"])
            act(g1, z_, AF.Silu, ["wst0"], ["wst1"])
            tt(mh, mh, g1, ALU.mult, ["wst1", "wst1"], ["wst1"])
            tt(t64, mh, mh, ALU.mult, ["wst1"], ["wst1"])
            red(ms[:, 9:10], t64, ALU.add, ["wst1"], ["ms"])
            mm(ptf[:, 300:301], Quad, ms[:, 9:10], True, True, ["cst", "ms"], ["ptf"])
            rsq(ms[:, 10:11], ptf[:, 300:301], 1.0 / 256, ["ptf"], ["ms"])
            ts(mh, mh, ms[:, 10:11], None, ALU.mult, None, ["wst1", "ms"], ["wst1"])
            for j in range(8):
                mm(pmB[0:16, 1, j * 64:(j + 1) * 64], Sel8T[:, j, :], mh, True, True, ["cst", "wst1"], ["pmB1"])
            cp(ycs[:, 512:1024], pmB[0:16, 1, :], ["pmB1"], ["ycat1"])

            qkn_s, qkA_s, qkB_s = Eall[0:16, 0:640], Eall[0:16, 1024:1664], Eall[0:16, 2048:2688]
            tt(qkA_s, u[:, 2320:2960], u[:, 2320:2960], ALU.mult, ["utm"], ["E1"])
            red(sd[:, 0:10], qkA_s.rearrange("p (h d) -> p h d", h=10), ALU.add, ["E1"], ["sd"])
            rsq(sd[:, 0:10], sd[:, 0:10], 1.0 / 64, ["sd"], ["sd"])
            n3 = qkn_s.rearrange("p (h d) -> p h d", h=10)
            tt(n3, u[:, 2320:2960].rearrange("p (h d) -> p h d", h=10), sd[:, 0:10].unsqueeze(2).to_broadcast([16, 10, 64]), ALU.mult,
               ["utm", "sd"], ["E0"])
            tt(n3[:, 0:8, :], n3[:, 0:8, :], qkw_b[0:16, 0, :].unsqueeze(1).to_broadcast([16, 8, 64]), ALU.mult, ["E0", "qkw_b"], ["E0"])
            tt(n3[:, 8:10, :], n3[:, 8:10, :], qkw_b[0:16, 1, :].unsqueeze(1).to_broadcast([16, 2, 64]), ALU.mult, ["E0", "qkw_b"], ["E0"])
            A3 = qkA_s.rearrange("p (h d) -> p h d", h=10)
            B3 = qkB_s.rearrange("p (h d) -> p h d", h=10)
            tt(A3, n3, ropeS[:, 0:64].unsqueeze(1).to_broadcast([16, 10, 64]), ALU.mult, ["E0", "cst"], ["E1"])
            tt(B3[:, :, 0:32], n3[:, :, 32:64], ropeS[:, 64:96].unsqueeze(1).to_broadcast([16, 10, 32]), ALU.mult, ["E0", "cst"], ["E2"])
            tt(B3[:, :, 32:64], n3[:, :, 0:32], ropeS[:, 96:128].unsqueeze(1).to_broadcast([16, 10, 32]), ALU.mult, ["E0", "cst"], ["E2"])
            tt(qkn_s, qkA_s, qkB_s, ALU.add, ["E1", "E2"], ["E0"])
            cp(qm, qkn_s[:, 0:512], ["E0"], ["xt1"])
            okl = "ok%d" % l
            dma(o_sk[l, :, 127, :, :].rearrange("b g d -> b (g d)"), qkn_s[:, 512:640], ["E0"], [okl], "ok_new")
            dma(o_sv[l, :, 127, :, :].rearrange("b g d -> b (g d)"), u[:, 2960:3088], ["utm"], [okl], "ov_new")
            dma(o_sk[l, :, 0:127, :, :].rearrange("b j g d -> b (j g d)"), i_ck[l, :, 1:128, :, :].rearrange("b j g d -> b (j g d)"), (), [okl], "ok_cp")
            dma(o_sv[l, :, 0:127, :, :].rearrange("b j g d -> b (j g d)"), i_cv[l, :, 1:128, :, :].rearrange("b j g d -> b (j g d)"), (), [okl], "ov_cp")
            K1 = sA.rearrange("p (b n) -> p b n", b=16)
            V1 = sB.rearrange("p (b n) -> p b n", b=16)
            dma(K1, o_sk[l].rearrange("b j g d -> j b (g d)"), [okl], KA, "sA")
            dma(V1, o_sv[l].rearrange("b j g d -> j b (g d)"), [okl], KB, "sB")
            qm2 = sb_qm2
            for b in range(16):
                ts(qm2, qm, identf[0:16, b:b + 1], None, ALU.mult, None, ["xt1", "cm"], ["xt1"])
                mm(pj[:, b % 2, :], onesf[0:16, :], qm2, True, True, ["cm", "xt1"], ["pj%d" % (b % 2)])
                tt(sTmp.rearrange("p (g h d) -> p g h d", g=2, h=4), K1[:, b, :].rearrange("p (g d) -> p g d", g=2).unsqueeze(2).to_broadcast([128, 2, 4, 64]),
                   pj[:, b % 2, :].rearrange("p (g h d) -> p g h d", g=2, h=4), ALU.mult, KA + ["pj%d" % (b % 2)], ["xt0"])
                red(sTt[:, b * 8:(b + 1) * 8], sTmp.rearrange("p (h d) -> p h d", h=8), ALU.add, ["xt0"], ["wst0"])
            act(sTt, sTt, AF.Exp, ["wst0"], ["wst0"], scale=0.125)
            mm(ptf[:, 300:301], sTt, onesf[:, 0:1], True, True, ["wst0", "cm"], ["ptf"])
            for b in range(16):
                for g in range(2):
                    mm(pmA[0:64, 0, (b * 8 + g * 4):(b * 8 + g * 4 + 4)], V1[:, b, g * 64:(g + 1) * 64], sTt[:, b * 8 + g * 4:b * 8 + g * 4 + 4],
                       True, True, KB + ["wst0"], ["pmA0"])
            cp(oTs, pmA[0:64, 0, 0:128], ["pmA0"], ["wst0"])
            tr(ptf[:, 320:384], oTs, identf[0:64, 0:64], ["wst0", "cm"], ["ptf"])
            tt(ms[:, 13:14], ptf[:, 300:301], ms[:, 12:13], ALU.add, ["ptf", "ms"], ["ms"])
            op("vector", lambda e: e.reciprocal(out=ms[:, 13:14], in_=ms[:, 13:14]), ["ms"], ["ms"])
            ts(mh, ptf[:, 320:384], ms[:, 13:14], None, ALU.mult, None, ["ptf", "ms"], ["wst1"])
            expand(pmA[:, 1, 0:64], "pmA1", [u[:, 3088 + h * 64:3088 + (h + 1) * 64] for h in hh], ["utm"])
            act(g1, pmA[:, 1, 0:64], AF.Silu, ["pmA1"], ["wst1"])
            tt(mh, mh, g1, ALU.mult, ["wst1", "wst1"], ["wst1"])
            for j in range(8):
                mm(pmB[0:16, 0, j * 64:(j + 1) * 64], Sel8T[:, j, :], mh, True, True, ["cst", "wst1"], ["pmB0"])
            cp(ycs[:, 1024:1536], pmB[0:16, 0, :], ["pmB0"], ["ycat2"])

            for rnd, (k0, k1) in enumerate(((0, 8), (8, 12))):
                for kc in range(k0, k1):
                    tr(ptr[:, kc - k0, 0:16], ycs[:, kc * 128:(kc + 1) * 128], identb[0:16, 0:16], ["ycat0", "ycat1", "ycat2", "cmb"], ["ptr"])
                cp(ycTs[:, k0:k1, :], ptr[:, 0:k1 - k0, 0:16], ["ptr"], ["ycatT"])
            for n in range(2):
                for kc in range(12):
                    mm(pj[0:16, n, :], ycTs[:, kc, :], Wo[:, kc, n * 512:(n + 1) * 512], kc == 0, kc == 11, ["ycatT", "Wo"], ["pj%d" % n])
                tt(xs_t[:, n * 512:(n + 1) * 512], pj[0:16, n, :], xs_t[:, n * 512:(n + 1) * 512], ALU.add, ["pj%d" % n, "xs_t"], ["xs_t"])
            if l == DEPTH - 1:
                dma(o_ys, xs_t[:], ["xs_t"], [], "ys")

        for l in range(DEPTH):
            load_layer(l)
            if with_sample:
                sample_layer(l)
            for t in range(NT):
                tile_step(l, t)
        S.emit(st)
        build.stats = S.stats
    return nc


PNAMES = ["w_in", "w_out", "norm_w", "a_igate_b", "a_fgate_b", "a_norm_w", "b_conv_w", "b_conv_b", "b_dt_bias", "b_A_log",
          "b_D", "b_norm_w", "c_qnorm_w", "c_knorm_w", "c_sinks"]


def make_in_maps(inputs, NT, DEPTH, with_sample):
    T = NT * 128
    consts = host_consts(T)
    if not with_sample:
        consts.pop("csel")
    shared = {n: np.ascontiguousarray(inputs[n][:DEPTH], dtype=np.float32) for n in PNAMES}
    shared.update(consts)
    in_maps = []
    for c in range(8):
        m = dict(shared)
        m["xp"] = np.ascontiguousarray(inputs["x_prompt"][c % 2, :T], dtype=np.float32)
        if with_sample:
            b0 = c * 16
            m["xs_in"] = np.ascontiguousarray(inputs["x_sample"][b0:b0 + 16, 0, :], dtype=np.float32)
            for nm, key in (("sC", "state_mlstm_C"), ("sn", "state_mlstm_n"), ("sm", "state_mlstm_m"), ("sh", "state_ssm"),
                            ("sconv", "state_conv"), ("ck", "cache_k"), ("cv", "cache_v")):
                m[nm] = np.ascontiguousarray(inputs[key][:DEPTH, b0:b0 + 16], dtype=np.float32)
        in_maps.append(m)
    return in_maps


def prompt_in_maps(inputs, NT, DEPTH):
    return make_in_maps(inputs, NT, DEPTH, False)


def run_prompt_only(inputs, NT, DEPTH):
    nc = build(NT, DEPTH)
    in_maps = prompt_in_maps(inputs, NT, DEPTH)
    res = run_bass_kernel_spmd(nc, in_maps, core_ids=list(range(8)))
    return res.results


def assemble(results, DEPTH):
    r = results
    hp = np.stack([r[0]["yp"], r[1]["yp"]], 0)
    hs = np.concatenate([r[c]["ys"] for c in range(8)], 0)[:, None, :]

    def pstack(nm):
        return np.stack([r[0][nm], r[1][nm]], 1)

    def scat(nm):
        return np.concatenate([r[c][nm] for c in range(8)], 1)

    outs = (hp, hs, pstack("pC"), pstack("pn"), pstack("pm"), pstack("ph"), pstack("pconv"), pstack("pk"), pstack("pv"),
            scat("oC"), scat("on"), scat("om"), scat("oh"), scat("oconv"), scat("ok"), scat("ov"))
    return tuple(np.ascontiguousarray(o, dtype=np.float32) for o in outs)


def kernel(**inputs):
    NT = inputs["x_prompt"].shape[1] // 128
    DEPTH = inputs["w_in"].shape[0]
    nc = build(NT, DEPTH, with_sample=True)
    in_maps = make_in_maps(inputs, NT, DEPTH, True)
    res = run_bass_kernel_spmd(nc, in_maps, core_ids=list(range(8)))
    return assemble(res.results, DEPTH)
```

```python
import contextlib
import math
import os
import numpy as np
import concourse.bass as bass
import concourse.mybir as mybir
from concourse.bass_utils import run_bass_kernel_spmd

F32 = mybir.dt.float32
BF16 = mybir.dt.bfloat16
AF = mybir.ActivationFunctionType
ALU = mybir.AluOpType
AX = mybir.AxisListType

D_MODEL = 1024
D_IN = 4880
D_MIX = 1536
EPS = 1e-6
NEG = -30000.0


class _Op:
    __slots__ = ("eng", "fn", "reads", "writes", "dma", "deps", "sig", "waits", "clock", "idx")


class Sched:
    ENGS = ("tensor", "vector", "scalar", "gpsimd", "sync")

    def __init__(self, nc):
        self.nc = nc
        self.ops = []
        self.last_w = {}
        self.readers = {}

    def op(self, eng, fn, reads=(), writes=(), dma=None):
        o = _Op()
        o.eng, o.fn, o.reads, o.writes, o.dma = eng, fn, tuple(reads), tuple(writes), dma
        o.idx = len(self.ops)
        deps = set()
        for k in o.reads:
            w = self.last_w.get(k)
            if w is not None:
                deps.add(w)
        for k in o.writes:
            w = self.last_w.get(k)
            if w is not None:
                deps.add(w)
            for r in self.readers.get(k, ()):
                deps.add(r)
        if eng == "tensor":
            deps = {d for d in deps if self.ops[d].eng != "tensor"}
        o.deps = deps
        for k in o.reads:
            self.readers.setdefault(k, []).append(o.idx)
        for k in o.writes:
            self.last_w[k] = o.idx
            self.readers[k] = []
        self.ops.append(o)
        return o

    def emit(self, stack, final_wait_eng="sync"):
        nc = self.nc
        ops = self.ops
        needed = set()
        for o in ops:
            needed.update(o.deps)
        sems = {}
        counts = {}
        for o in ops:
            if o.dma is not None:
                sname = "d_" + o.dma
                counts[sname] = counts.get(sname, 0) + 16
                o.sig = (sname, counts[sname])
            elif o.idx in needed:
                sname = "e_" + o.eng
                counts[sname] = counts.get(sname, 0) + 1
                o.sig = (sname, counts[sname])
            else:
                o.sig = None
        known = {e: {} for e in self.ENGS}
        for o in ops:
            kn = known[o.eng]
            waits = []
            for d in sorted(o.deps, reverse=True):
                do = ops[d]
                sname, val = do.sig
                if kn.get(sname, 0) >= val:
                    continue
                waits.append((sname, val))
                kn[sname] = val
                for s2, v2 in do.clock.items():
                    if kn.get(s2, 0) < v2:
                        kn[s2] = v2
            o.waits = waits
            o.clock = dict(kn)
        for s in counts:
            sems[s] = stack.enter_context(nc.semaphore(s))
        by_eng = {e: [o for o in ops if o.eng == e] for e in self.ENGS}
        self.stats = dict(n_ops=len(ops), n_waits=sum(len(o.waits) for o in ops), n_sems=len(sems),
                          per_eng={e: len(v) for e, v in by_eng.items()})

        def run(engobj, lst, do_final):
            for o in lst:
                for sname, val in o.waits:
                    engobj.wait_ge(sems[sname], val)
                ins = o.fn(engobj)
                if o.sig is not None:
                    ins.then_inc(sems[o.sig[0]], 16 if o.dma is not None else 1)
            if do_final:
                for sname, val in counts.items():
                    if sname.startswith("d_"):
                        engobj.wait_ge(sems[sname], val)

        with nc.Block() as block:
            @block.tensor
            def _(e):
                run(e, by_eng["tensor"], final_wait_eng == "tensor")

            @block.vector
            def _(e):
                run(e, by_eng["vector"], final_wait_eng == "vector")

            @block.scalar
            def _(e):
                run(e, by_eng["scalar"], final_wait_eng == "scalar")

            @block.gpsimd
            def _(e):
                run(e, by_eng["gpsimd"], final_wait_eng == "gpsimd")

            @block.sync
            def _(e):
                run(e, by_eng["sync"], final_wait_eng == "sync")


def host_consts(T):
    c = {}
    ident = np.eye(128, dtype=np.float32)
    s = np.arange(128)
    tri = (s[:, None] <= s[None, :]).astype(np.float32)
    negm = np.where(s[None, :] < s[:, None], NEG, 0.0).astype(np.float32)
    c["cmat"] = np.concatenate([ident, tri, 1.0 - tri, np.ones((128, 128), np.float32),
                                np.tile(negm[:, None, :], (1, 1, 1)).reshape(128, 128)], axis=1)
    pos = np.arange(T, dtype=np.float32)
    inv = (10000.0 ** (-np.arange(32, dtype=np.float32) / 32)).astype(np.float32)
    ang = pos[:, None] * inv[None, :]
    cos, sin = np.cos(ang).astype(np.float32), np.sin(ang).astype(np.float32)
    c["rope"] = np.concatenate([cos, cos, -sin, sin], axis=1).astype(np.float32)
    cs = np.zeros((128, 1152), np.float32)
    p = np.arange(128)
    for b in range(16):
        cs[b, 7 + 8 * b] = 1.0
        for j in range(8):
            cs[b * 8 + j, 136 + j * 16 + b] = 1.0
    cs[p // 2, 264 + p] = 1.0
    cs[(p % 8) // 2, 392 + p] = 1.0
    cs[:, 520:648] = (p[:, None] // 2 == p[None, :] // 2)
    cs[:, 648:776] = (p[:, None] // 4 == p[None, :] // 4)
    cs[2 * np.arange(64), 776 + np.arange(64)] = 1.0
    cs[p, 840 + p % 8] = 1.0
    angs = np.float32(8192.0) * inv
    cS, sS = np.cos(angs).astype(np.float32), np.sin(angs).astype(np.float32)
    cs[0:16, 1024:1152] = np.concatenate([cS, cS, -sS, sS])[None, :]
    c["csel"] = cs
    return c


def build(NT, DEPTH, with_sample=False, nc=None, ins=None, outs=None):
    T = NT * 128
    if nc is None:
        nc = bass.Bass("TRN2", target_bir_lowering=False)

    def din(name, shape):
        if ins is not None:
            assert list(ins[name].shape) == list(shape), (name, ins[name].shape, shape)
            return ins[name]
        return nc.dram_tensor(name, shape, F32, kind="ExternalInput").ap()

    def dout(name, shape):
        if outs is not None:
            assert list(outs[name].shape) == list(shape), (name, outs[name].shape, shape)
            return outs[name]
        return nc.dram_tensor(name, shape, F32, kind="ExternalOutput").ap()

    xp = din("xp", [T, 1024])
    w_in = din("w_in", [DEPTH, 1024, D_IN])
    w_out = din("w_out", [DEPTH, D_MIX, 1024])
    norm_w = din("norm_w", [DEPTH, 1024])
    a_ib = din("a_igate_b", [DEPTH, 4])
    a_fb = din("a_fgate_b", [DEPTH, 4])
    a_nw = din("a_norm_w", [DEPTH, 512])
    cw = din("b_conv_w", [DEPTH, 4, 1024])
    cb = din("b_conv_b", [DEPTH, 1024])
    dtb = din("b_dt_bias", [DEPTH, 8])
    alog = din("b_A_log", [DEPTH, 8])
    dsk = din("b_D", [DEPTH, 8])
    b_nw = din("b_norm_w", [DEPTH, 512])
    qnw = din("c_qnorm_w", [DEPTH, 64])
    knw = din("c_knorm_w", [DEPTH, 64])
    snk = din("c_sinks", [DEPTH, 8])
    cmat = din("cmat", [128, 640])
    rope = din("rope", [T, 128])

    if with_sample:
        csel = din("csel", [128, 1152])
        xs_in = din("xs_in", [16, 1024])
        i_sC = din("sC", [DEPTH, 16, 4, 128, 64])
        i_sn = din("sn", [DEPTH, 16, 4, 64])
        i_sm = din("sm", [DEPTH, 16, 4])
        i_sh = din("sh", [DEPTH, 16, 8, 64, 128])
        i_sconv = din("sconv", [DEPTH, 16, 3, 1024])
        i_ck = din("ck", [DEPTH, 16, 128, 2, 64])
        i_cv = din("cv", [DEPTH, 16, 128, 2, 64])
        o_ys = dout("ys", [16, 1024])
        o_sC = dout("oC", [DEPTH, 16, 4, 128, 64])
        o_sn = dout("on", [DEPTH, 16, 4, 64])
        o_sm = dout("om", [DEPTH, 16, 4])
        o_sh = dout("oh", [DEPTH, 16, 8, 64, 128])
        o_sconv = dout("oconv", [DEPTH, 16, 3, 1024])
        o_sk = dout("ok", [DEPTH, 16, 128, 2, 64])
        o_sv = dout("ov", [DEPTH, 16, 128, 2, 64])
    yp = dout("yp", [T, 1024])
    o_pC = dout("pC", [DEPTH, 4, 128, 64])
    o_pn = dout("pn", [DEPTH, 4, 64])
    o_pm = dout("pm", [DEPTH, 4])
    o_ph = dout("ph", [DEPTH, 8, 64, 128])
    o_pconv = dout("pconv", [DEPTH, 3, 1024])
    o_pk = dout("pk", [DEPTH, 128, 2, 64])
    o_pv = dout("pv", [DEPTH, 128, 2, 64])

    st = contextlib.ExitStack()
    with st:
        def sb(name, shape, dt=F32):
            return st.enter_context(nc.sbuf_tensor(name, shape, dt))

        def ps(name, shape, dt=F32):
            return st.enter_context(nc.psum_tensor(name, shape, dt))

        S = Sched(nc)
        chain_ctx = [None, None]

        def op(eng, fn, reads=(), writes=(), dma=None, cost=None):
            if chain_ctx[0] is None:
                return S.op(eng, fn, reads, writes, dma)
            item = (eng, fn, tuple(reads), tuple(writes), dma, cost)
            if chain_ctx[1] is not None:
                chain_ctx[1].append(item)
            else:
                chain_ctx[0].append([item])
            return None

        def label(name):
            chain_ctx[0].append([("label", name)])

        def wait_label(name):
            chain_ctx[0].append([("wait", name)])

        class atomic:
            def __enter__(self):
                if chain_ctx[0] is not None:
                    chain_ctx[1] = []
            def __exit__(self, *a):
                if chain_ctx[0] is not None:
                    chain_ctx[0].append(chain_ctx[1])
                    chain_ctx[1] = None
                return False

        COST = {"tensor": 0.18, "vector": 0.55, "scalar": 0.5, "gpsimd": 1.4, "sync": 2.0}
        SYNC_LAT = 0.6
        eng_free = {e: 0.0 for e in Sched.ENGS}
        fin = {}

        def est_ready(item):
            eng, fn, r, w, d = item[:5]
            tmax = 0.0
            for k in r:
                x = S.last_w.get(k)
                if x is not None:
                    tmax = max(tmax, fin.get(x, 0.0) + (SYNC_LAT if S.ops[x].eng != eng else 0.1))
            for k in w:
                x = S.last_w.get(k)
                if x is not None:
                    tmax = max(tmax, fin.get(x, 0.0) + (SYNC_LAT if S.ops[x].eng != eng else 0.1))
                for x in S.readers.get(k, ()):
                    tmax = max(tmax, fin.get(x, 0.0) + (SYNC_LAT if S.ops[x].eng != eng else 0.1))
            return max(tmax, eng_free[eng])

        def emit_item(item):
            t0 = est_ready(item)
            o = S.op(*item[:5])
            t1 = t0 + (item[5] if item[5] is not None else COST[item[0]])
            eng_free[item[0]] = t1
            fin[o.idx] = t1

        def run_chains(fns, prio=None):
            prio = prio or [0.0] * len(fns)
            chains = []
            for f in fns:
                chain_ctx[0] = []
                f()
                chains.append(chain_ctx[0])
            chain_ctx[0] = None
            base = max(eng_free.values())
            for e in eng_free:
                eng_free[e] = max(eng_free[e], base - 3.0)
            pos = [0] * len(chains)
            labels = set()
            while True:
                best, bt = None, None
                progressed = False
                for i, c in enumerate(chains):
                    while pos[i] < len(c) and c[pos[i]][0][0] in ("label", "wait"):
                        kind, name = c[pos[i]][0][0], c[pos[i]][0][1]
                        if kind == "label":
                            labels.add(name)
                            pos[i] += 1
                            progressed = True
                        elif name in labels:
                            pos[i] += 1
                            progressed = True
                        else:
                            break
                    if pos[i] < len(c) and c[pos[i]][0][0] not in ("label", "wait"):
                        t_ = est_ready(c[pos[i]][0]) + prio[i]
                        if best is None or t_ < bt:
                            best, bt = i, t_
                if best is None:
                    if progressed:
                        continue
                    assert all(pos[i] >= len(c) for i, c in enumerate(chains)), "chain deadlock"
                    break
                for it in chains[best][pos[best]]:
                    emit_item(it)
                pos[best] += 1

        def dma(out, in_, r, w, sem, eng="sync", slow=False):
            if slow:
                return op(eng, lambda e: e.dma_start(out=out, in_=in_, allow_slow_non_contiguous=True), r, w, dma=sem)
            return op(eng, lambda e: e.dma_start(out=out, in_=in_), r, w, dma=sem)

        def fsz(ap):
            n = 1
            for d_ in ap.shape[1:]:
                n *= d_
            return n

        def mm(out, lhsT, rhs, start, stop, r, w):
            c_ = max(0.06, fsz(out) / 2400.0) * (4.0 if lhsT.dtype == F32 else 1.0)
            return op("tensor", lambda e: e.matmul(out, lhsT=lhsT, rhs=rhs, start=start, stop=stop), r, w, cost=c_)

        def tr(out, in_, ident, r, w):
            return op("tensor", lambda e: e.transpose(out=out, in_=in_, identity=ident), r, w, cost=0.12)

        def act(out, in_, func, r, w, **kw):
            return op("scalar", lambda e: e.activation(out=out, in_=in_, func=func, **kw), r, w, cost=0.2 + fsz(out) / 1200.0)

        def ecost(eng, out):
            return (0.1 + fsz(out) / 900.0) * (2.2 if eng == "gpsimd" else 1.0)

        def tt(out, in0, in1, o, r, w, eng="vector"):
            return op(eng, lambda e: e.tensor_tensor(out=out, in0=in0, in1=in1, op=o), r, w, cost=ecost(eng, out))

        def ts(out, in0, s1, s2, o0, o1, r, w, eng="vector"):
            if s2 is None:
                return op(eng, lambda e: e.tensor_scalar(out=out, in0=in0, scalar1=s1, scalar2=None, op0=o0), r, w, cost=ecost(eng, out))
            return op(eng, lambda e: e.tensor_scalar(out=out, in0=in0, scalar1=s1, scalar2=s2, op0=o0, op1=o1), r, w, cost=ecost(eng, out))

        def stt(out, in0, scalar, in1, o0, o1, r, w):
            return op("vector", lambda e: e.scalar_tensor_tensor(out=out, in0=in0, scalar=scalar, in1=in1, op0=o0, op1=o1), r, w, cost=ecost("vector", out))

        def red(out, in_, o, r, w):
            return op("vector", lambda e: e.tensor_reduce(out=out, in_=in_, axis=AX.X, op=o), r, w, cost=ecost("vector", in_))

        def cp(out, in_, r, w, eng="vector"):
            if eng == "scalar":
                return op("scalar", lambda e: e.copy(out=out, in_=in_), r, w, cost=0.2 + fsz(out) / 1200.0)
            return op(eng, lambda e: e.tensor_copy(out=out, in_=in_), r, w, cost=ecost(eng, out))

        def mset(ap, val, w, eng="gpsimd"):
            return op(eng, lambda e: e.memset(ap, val), (), w)

        def rsq(out, in_, scale, r, w):
            act(out, in_, AF.Ln, r, w, scale=scale, bias=EPS)
            act(out, out, AF.Exp, w, w, scale=-0.5)

        cm = sb("cm", [128, 640])
        dma(cm[:], cmat, (), ["cm"], "cm")
        identf = cm[:, 0:128]
        trif = cm[:, 128:256]
        onesf = cm[:, 384:512]
        negmf = cm[:, 512:640]
        cmb = sb("cmb", [128, 384], BF16)
        cp(cmb[:], cm[:, 0:384], ["cm"], ["cmb"])
        identb = cmb[:, 0:128]
        trib = cmb[:, 128:256]
        ntrib = cmb[:, 256:384]
        nw = sb("nw", [128, DEPTH, 8])
        for l_ in range(DEPTH):
            dma(nw[:, l_, :], norm_w[l_].rearrange("(c p) -> p c", p=128), (), ["nw"], "nw", slow=True)
        ow = sb("ow", [128, DEPTH, 12])
        mset(ow[:], 1.0, ["ow"])
        for l_ in range(DEPTH):
            dma(ow[:, l_, 0:4], a_nw[l_].rearrange("(c p) -> p c", p=128), (), ["ow"], "ow_a", slow=True)
            dma(ow[:, l_, 4:8], b_nw[l_].rearrange("(c p) -> p c", p=128), (), ["ow"], "ow_b", slow=True)

        Wb = sb("Wb", [128, 8, D_IN], BF16)
        Wo = sb("Wo", [128, 12, 1024], BF16)
        wst = [sb("wst%d" % i, [128, 1024]) for i in range(2)]

        qkw_b = sb("qkw_b", [128, 2, 64])
        par8 = sb("par8", [128, 5, 8])
        cwt = sb("cwt", [128, 8, 4])
        cbt = sb("cbt", [128, 8])

        pj = ps("pj", [128, 2, 512])
        ptr = ps("ptr", [128, 8, 128], BF16)
        ptf = ps("ptf", [128, 512])
        pmA = ps("pmA", [128, 2, 512])
        pmB = ps("pmB", [128, 2, 512])

        xt = [sb("xt%d" % i, [128, 1024]) for i in range(2)]
        Eall = sb("Eall", [128, 4096])
        E = [Eall[:, i * 1024:(i + 1) * 1024] for i in range(4)]
        st1 = sb("st1", [128, 16])
        xsb = sb("xsb", [128, 1024], BF16)
        xnT = sb("xnT", [128, 8, 128], BF16)
        utm = sb("utm", [128, 3600])
        qkT = sb("qkT", [64, 8, 128], BF16)
        cvb = sb("cvb", [128, 8, 131])
        cacc = E[0].rearrange("p (c t) -> p c t", c=8)
        ctmp = E[1].rearrange("p (c t) -> p c t", c=8)
        xbcT = sb("xbcT", [128, 8, 128], BF16)
        ycat = sb("ycat", [128, 1536], BF16)
        ycatT = sb("ycatT", [128, 12, 128], BF16)
        ropet = [sb("ropet%d" % i, [128, 128]) for i in range(2)]
        sm8 = sb("sm8", [128, 8, 8])
        Rt = E[0].rearrange("p (c t) -> p c t", c=8)
        dec = E[1].rearrange("p (c t) -> p c t", c=8)
        wm = sb("wm", [128, 8, 128], BF16)
        STm_t = sb("STm", [128, 4, 128], BF16)
        STm = STm_t[:]
        xBtm = sb("xBtm", [128, 6, 128], BF16)
        xdt = sb("xdt", [128, 512], BF16)
        xss = sb("xss", [128, 512], BF16)
        hT = sb("hT", [128, 512])
        hTb = sb("hTb", [128, 512], BF16)
        yb1 = wst[0][:, 0:512]
        yb2 = wst[0][:, 512:1024]
        g4 = sb("g4", [128, 8, 4])
        gT = sb("gT", [4, 8, 128])
        mst = sb("mst", [4, 8])
        dg4 = sb("dg4", [4, 4])
        facb = sb("facb", [64, 4])
        kbf = sb("kbf", [128, 256], BF16)
        vaug = sb("vaug", [128, 4, 132], BF16)
        CnT = sb("CnT", [64, 4, 132])
        CnB = sb("CnB", [64, 4, 132], BF16)
        hb = E[2][:, 0:512]
        hb2 = E[2][:, 512:1024]
        gsig = E[3][:, 0:512]
        gsil = E[3][:, 512:1024]
        qkn = wst[1][:, 0:640]
        qkA_t = sb("qkA", [128, 640], BF16)
        qkA = qkA_t[:]
        qkB_t = sb("qkB", [128, 640], BF16)
        qkB = qkB_t[:]
        qkp = sb("qkp", [128, 640], BF16)
        kvf = wst[1][:, 640:896]
        qTs = sb("qTs", [64, 8, 128], BF16)
        kTs = [sb("kTs%d" % i, [64, 2, 128], BF16) for i in range(2)]
        vsw = [sb("vsw%d" % i, [128, 2, 66], BF16) for i in range(2)]
        pex = [sb("pex%d" % i, [128, 512], BF16) for i in range(2)]
        osw = wst[1][:, 0:512]
        gsil2t = sb("gsil2", [128, 512], BF16)
        gsil2 = gsil2t[:]
        dsk_t = E[0][:, 0:512]
        st10 = sb("st10", [128, 4, 16])
        ost = E[0].rearrange("p (c t) -> p c t", c=8)
        ostm = E[2].rearrange("p (c t) -> p c t", c=8)

        def load_layer(l):
            dma(qkw_b[:, 0, :], qnw[l].partition_broadcast(128), (), ["qkw_b"], "par2")
            dma(qkw_b[:, 1, :], knw[l].partition_broadcast(128), (), ["qkw_b"], "par3")
            dma(par8[:, 0, :], dtb[l].partition_broadcast(128), (), ["par8"], "par4")
            dma(par8[:, 1, :], alog[l].partition_broadcast(128), (), ["par8"], "par5")
            dma(par8[:, 2, :], dsk[l].partition_broadcast(128), (), ["par8"], "par6")
            dma(par8[:, 3, :], snk[l].partition_broadcast(128), (), ["par8"], "par7")
            dma(par8[:, 4, 0:4], a_ib[l].partition_broadcast(128), (), ["par8"], "par8")
            dma(par8[:, 4, 4:8], a_fb[l].partition_broadcast(128), (), ["par8"], "par9")
            act(par8[:, 1, :], par8[:, 1, :], AF.Exp, ["par8"], ["par8"])
            ts(par8[:, 1, :], par8[:, 1, :], -1.0, None, ALU.mult, None, ["par8"], ["par8"])
            act(par8[:, 3, :], par8[:, 3, :], AF.Exp, ["par8"], ["par8"])
            for j in range(4):
                dma(cwt[:, :, j], cw[l, j].rearrange("(c p) -> p c", p=128), (), ["cwt"], "par10", slow=True)
            dma(cbt[:], cb[l].rearrange("(c p) -> p c", p=128), (), ["cbt"], "par11", slow=True)
            i = 0
            for kc in range(8):
                for q4 in range(5):
                    c0 = q4 * 1024
                    w_ = min(1024, D_IN - c0)
                    stg = wst[i % 2]
                    key = "wst%d" % (i % 2)
                    dma(stg[:, 0:w_], w_in[l, kc * 128:(kc + 1) * 128, c0:c0 + w_], (), [key], key)
                    if i % 2:
                        act(Wb[:, kc, c0:c0 + w_], stg[:, 0:w_], AF.Copy, [key, "nw"], ["Wb"], scale=nw[:, l, kc:kc + 1])
                    else:
                        ts(Wb[:, kc, c0:c0 + w_], stg[:, 0:w_], nw[:, l, kc:kc + 1], None, ALU.mult, None, [key, "nw"], ["Wb"])
                    i += 1
            ts(Wb[:, :, 256:512], Wb[:, :, 256:512], 0.125, None, ALU.mult, None, ["Wb"], ["Wb"])
            for kc in range(12):
                stg = wst[i % 2]
                key = "wst%d" % (i % 2)
                dma(stg[:, 0:1024], w_out[l, kc * 128:(kc + 1) * 128, :], (), [key], key)
                if i % 2:
                    act(Wo[:, kc, :], stg[:, 0:1024], AF.Copy, [key, "ow"], ["Wo"], scale=ow[:, l, kc:kc + 1])
                else:
                    ts(Wo[:, kc, :], stg[:, 0:1024], ow[:, l, kc:kc + 1], None, ALU.mult, None, [key, "ow"], ["Wo"])
                i += 1
            mset(CnT[:], 0.0, ["CnT"])
            mset(hT[:], 0.0, ["hT"])
            mset(mst[:], 0.0, ["mst"])
            mset(cvb[:], 0.0, ["cvb"])
            mset(hTb[:], 0.0, ["hTb"])

        def front_a(l, t):
            par = t % 2
            x = xt[par]
            xk = "xt%d" % par
            src = xp if l == 0 else yp
            dma(x[:], src[t * 128:(t + 1) * 128, :], ["yp%d" % t], [xk], xk)
            rp = ropet[par]
            rk = "ropet%d" % par
            dma(rp[:], rope[t * 128:(t + 1) * 128, :], (), [rk], rk)
            act(xsb[:], x[:], AF.Square, [xk], ["xsb", "st1"], scale=1.0 / 32, accum_out=st1[:, 0:1])
            rsq(st1[:, 1:2], st1[:, 0:1], 1.0, ["st1"], ["st1"])
            act(xsb[:], x[:], AF.Copy, [xk, "st1"], ["xsb"], scale=st1[:, 1:2])
            with atomic():
                for c in range(8):
                    tr(ptr[:, c, :], xsb[:, c * 128:(c + 1) * 128], identb, ["xsb", "cmb"], ["ptr"])
                cp(xnT[:], ptr[:], ["ptr"], ["xnT"])

        def proj_utm(l, t, banks):
            chunks = []
            for (c0, n, off) in ((256, 2312, 0), (3592, 1288, 2312)):
                o = 0
                while o < n:
                    w_ = min(512, n - o)
                    chunks.append((c0 + o, w_, off + o))
                    o += w_
            for i, (c0, w_, off) in enumerate(chunks):
                b = banks[i % len(banks)]
                for kc in range(8):
                    mm(pj[:, b, 0:w_], xnT[:, kc, :], Wb[:, kc, c0:c0 + w_], kc == 0, kc == 7, ["xnT", "Wb"], ["pj%d" % b])
                cp(utm[:, off:off + w_], pj[:, b, 0:w_], ["pj%d" % b], ["utm"], eng=("scalar" if i % 2 else "vector"))

        def tile_step(l, t):
            par = t % 2
            x = xt[par]
            xk = "xt%d" % par
            rp = ropet[par]
            rk = "ropet%d" % par
            if t == 0:
                front_a(l, 0)
                proj_utm(l, 0, [0, 1])
            act(gsig, utm[:, 768:1280], AF.Sigmoid, ["utm"], ["E3"])
            act(gsil, utm[:, 1280:1792], AF.Silu, ["utm"], ["E3"])
            act(yb2, utm[:, 1800:2312], AF.Silu, ["utm"], ["wst0g", "wst0"])
            act(gsil2, utm[:, 3088:3600], AF.Silu, ["utm"], ["gsil2"])
            tt(gsig, gsig, gsil, ALU.mult, ["E3", "E3"], ["E3"], eng="gpsimd")
            for c in range(8):
                for kc in range(8):
                    mm(pmA[0:64, c // 4, (c % 4) * 128:(c % 4 + 1) * 128], Wb[:, kc, c * 64:(c + 1) * 64], xnT[:, kc, :], kc == 0, kc == 7,
                       ["xnT", "Wb"], ["pmA%d" % (c // 4)])
            cp(qkT[:], pmA[0:64, :, :].rearrange("p b (c t) -> p (b c) t", c=4), ["pmA0", "pmA1"], ["qkT"], eng="scalar")
            for c in range(8):
                for kc in range(8):
                    mm(pmB[:, c // 4, (c % 4) * 128:(c % 4 + 1) * 128], Wb[:, kc, 2568 + c * 128:2568 + (c + 1) * 128], xnT[:, kc, :],
                       kc == 0, kc == 7, ["xnT", "Wb"], ["pmB%d" % (c // 4)])
            cp(cvb[:, :, 3:131], pmB[:].rearrange("p b (c t) -> p (b c) t", c=4), ["pmB0", "pmB1"], ["cvb"])

            fns = [lambda: mlstm_tile(l, t), lambda: ssd_tile(l, t), lambda: swa_tile(l, t, rp, rk), lambda: ssd_a(l, t)]
            def tail_chain():
                if t > 0:
                    outproj_mm(l, t - 1)
                if t + 1 < NT:
                    front_a(l, t + 1)
                    for nm_ in ("utm_m", "utm_a", "utm_s"):
                        wait_label(nm_)
                    proj_utm(l, t + 1, [1])
            fns.append(tail_chain)
            run_chains(fns)
            for rnd, (k0, k1) in enumerate(((0, 8), (8, 12))):
                for kc in range(k0, k1):
                    tr(ptr[:, kc - k0, :], ycat[:, kc * 128:(kc + 1) * 128], identb, ["ycat0", "ycat1", "ycat2", "cmb"], ["ptr"])
                cp(ycatT[:, k0:k1, :], ptr[:, 0:k1 - k0, :], ["ptr"], ["ycatT"], eng=("vector" if rnd else "scalar"))
            if t == NT - 1:
                outproj_mm(l, t)

        def outproj_mm(l, t):
            par = t % 2
            x = xt[par]
            xk = "xt%d" % par
            for n in range(2):
                for kc in range(12):
                    mm(pj[:, 1, :], ycatT[:, kc, :], Wo[:, kc, n * 512:(n + 1) * 512], kc == 0, kc == 11, ["ycatT", "Wo"], ["pj1"])
                tt(x[:, n * 512:(n + 1) * 512], pj[:, 1, :], x[:, n * 512:(n + 1) * 512], ALU.add, ["pj1", xk], [xk])
            dma(yp[t * 128:(t + 1) * 128, :], x[:], [xk], ["yp%d" % t], "ypst%d" % par)

        def mlstm_tile(l, t):
            last = (t == NT - 1)
            tt(g4[:, 0, :], utm[:, 1792:1796], par8[:, 4, 0:4], ALU.add, ["utm", "par8"], ["g4"])
            tt(g4[:, 1, :], utm[:, 1796:1800], par8[:, 4, 4:8], ALU.add, ["utm", "par8"], ["g4"])
            act(g4[:, 2, :], g4[:, 1, :], AF.Exp, ["g4"], ["g4"], scale=-1.0)
            act(g4[:, 2, :], g4[:, 2, :], AF.Ln, ["g4"], ["g4"], bias=1.0)
            mm(ptf[:, 0:4], trif, g4[:, 2, :], True, True, ["cm", "g4"], ["ptf"])
            cp(g4[:, 3, :], ptf[:, 0:4], ["ptf"], ["g4"])
            tt(g4[:, 4, :], g4[:, 0, :], g4[:, 3, :], ALU.add, ["g4"], ["g4"])
            tr(ptf[0:4, 0:128], g4[:, 4, :], identf, ["g4", "cm"], ["ptf"])
            tr(ptf[0:4, 128:256], g4[:, 3, :], identf, ["g4", "cm"], ["ptf"])
            cp(gT[:, 0, :], ptf[0:4, 0:128], ["ptf"], ["gT"])
            cp(gT[:, 1, :], ptf[0:4, 128:256], ["ptf"], ["gT"])
            red(mst[:, 1:2], gT[:, 0, :], ALU.max, ["gT"], ["mst"])
            tt(mst[:, 2:3], mst[:, 1:2], mst[:, 0:1], ALU.max, ["mst"], ["mst"])
            ts(mst[:, 3:4], mst[:, 2:3], -1.0, None, ALU.mult, None, ["mst"], ["mst"])
            tt(mst[:, 5:6], mst[:, 0:1], mst[:, 2:3], ALU.subtract, ["mst"], ["mst"])
            act(mst[:, 4:5], mst[:, 5:6], AF.Exp, ["mst"], ["mst"])
            act(gT[:, 2, :], gT[:, 0, :], AF.Exp, ["gT", "mst"], ["gT"], bias=mst[:, 3:4])
            act(gT[:, 3, :], gT[:, 1, :], AF.Exp, ["gT", "mst"], ["gT"], bias=mst[:, 3:4])
            tt(mst[:, 0:1], mst[:, 2:3], gT[:, 1, 127:128], ALU.subtract, ["mst", "gT"], ["mst"])
            tr(ptf[:, 256:260], gT[:, 2, :], identf[0:4, 0:4], ["gT", "cm"], ["ptf"])
            tr(ptf[:, 260:264], gT[:, 3, :], identf[0:4, 0:4], ["gT", "cm"], ["ptf"])
            cp(g4[:, 5:7, :], ptf[:, 256:264].rearrange("p (a h) -> p a h", a=2), ["ptf"], ["g4"])
            ts(dg4[:], identf[0:4, 0:4], mst[:, 4:5], None, ALU.mult, None, ["cm", "mst"], ["dg4"])
            mm(ptf[0:64, 264:268], onesf[0:4, 0:64], dg4[:], True, True, ["cm", "dg4"], ["ptf"])
            cp(facb[:], ptf[0:64, 264:268], ["ptf"], ["facb"])
            tt(CnT[:, :, 0:129], CnT[:, :, 0:129], facb[:].unsqueeze(2).to_broadcast([64, 4, 129]), ALU.mult, ["CnT", "facb"], ["CnT"])
            cp(CnB[:], CnT[:], ["CnT"], ["CnB"], eng="scalar")
            tt(vaug[:, :, 0:128], utm[:, 256:768].rearrange("p (h v) -> p h v", h=4), g4[:, 5, :].unsqueeze(2).to_broadcast([128, 4, 128]),
               ALU.mult, ["utm", "g4"], ["vaug"])
            cp(vaug[:, :, 128:129], g4[:, 5, :].unsqueeze(2), ["g4"], ["vaug"], eng="gpsimd")
            cp(kbf[:], utm[:, 0:256], ["utm"], ["kbf"], eng="scalar")
            label("utm_m")
            for h in range(4):
                mm(pmA[:, 1, h * 128:(h + 1) * 128], qkT[:, 4 + h, :], qkT[:, h, :], True, True, ["qkT"], ["pmA1"])
            tt(STm, pmA[:, 1, :].rearrange("p (h t) -> p h t", h=4), trif.unsqueeze(1).to_broadcast([128, 4, 128]), ALU.mult,
               ["pmA1", "cm"], ["STm"])
            for h in range(4):
                p0 = (h % 2) * 64
                o_ = pmA[:, h // 2, (h % 2) * 256:(h % 2) * 256 + 129]
                mm(o_, STm[:, h, :], vaug[:, h, 0:129], True, False, ["STm", "vaug"], ["pmA%d" % (h // 2)])
                mm(o_, qkT[:, h, :], CnB[:, h, 0:129], False, True, ["qkT", "CnB"], ["pmA%d" % (h // 2)])
            pn4 = pmA[:].rearrange("p b (h c) -> p (b h) c", h=2)
            act(g4[:, 7, :], pn4[:, :, 128], AF.Abs, ["pmA0", "pmA1"], ["g4"])
            tt(g4[:, 7, :], g4[:, 7, :], g4[:, 6, :], ALU.max, ["g4"], ["g4"])
            op("vector", lambda e: e.reciprocal(out=g4[:, 7, :], in_=g4[:, 7, :]), ["g4"], ["g4"])
            tt(hb.rearrange("p (h v) -> p h v", h=4), pn4[:, :, 0:128], g4[:, 7, :].unsqueeze(2).to_broadcast([128, 4, 128]), ALU.mult,
               ["pmA0", "pmA1", "g4"], ["E2"])
            for h in range(4):
                p0 = (h % 2) * 64
                mm(pmA[0:64, h // 2, (h % 2) * 256:(h % 2) * 256 + 129], kbf[:, h * 64:(h + 1) * 64], vaug[:, h, 0:129], True, True,
                   ["kbf", "vaug"], ["pmA%d" % (h // 2)])
            for h in range(4):
                p0 = (h % 2) * 64
                tt(CnT[:, h, 0:129], CnT[:, h, 0:129], pmA[0:64, h // 2, (h % 2) * 256:(h % 2) * 256 + 129], ALU.add,
                   ["CnT", "pmA%d" % (h // 2)], ["CnT"])
            for h in range(4):
                act(hb2[:, h * 128:(h + 1) * 128], hb[:, h * 128:(h + 1) * 128], AF.Square, ["E2"], ["E2", "st10a"], accum_out=st10[:, 0, h:h + 1])
            rsq(st10[:, 0, 4:8], st10[:, 0, 0:4], 1.0 / 128, ["st10a"], ["st10a"])
            tt(hb.rearrange("p (h v) -> p h v", h=4), hb.rearrange("p (h v) -> p h v", h=4),
               st10[:, 0, 4:8].unsqueeze(2).to_broadcast([128, 4, 128]), ALU.mult, ["E2", "st10a"], ["E2"])
            tt(ycat[:, 0:512], hb, gsig, ALU.mult, ["E2", "E3"], ["ycat0"])
            if last:
                for h in range(4):
                    p0 = (h % 2) * 64
                    tr(ptf[:, h * 64:(h + 1) * 64], CnT[:, h, 0:128], identf[0:64, 0:64], ["CnT", "cm"], ["ptf"])
                cp(ostm[:, 0:2, :], ptf[:, 0:256].rearrange("p (a b) -> p a b", a=2), ["ptf"], ["E2"])
                dma(o_pC[l].rearrange("h v k -> v h k"), ostm[:, 0:2, :].rearrange("p a (c k) -> p (a c) k", c=2), ["E2"], [], "o_pC")
                for h in range(4):
                    p0 = (h % 2) * 64
                    dma(o_pn[l, h].unsqueeze(1), CnT[:, h, 128:129], ["CnT"], [], "o_pn", slow=True)
                dma(o_pm[l].unsqueeze(1), mst[:, 0:1], ["mst"], [], "o_pm", slow=True)

        def ssd_a(l, t):
            RtH = wst[0][:, 0:512]
            tt(sm8[:, 0, :], utm[:, 2312:2320], par8[:, 0, :], ALU.add, ["utm", "par8"], ["sm8"])
            label("utm_a")
            act(sm8[:, 0, :], sm8[:, 0, :], AF.Exp, ["sm8"], ["sm8"])
            act(sm8[:, 0, :], sm8[:, 0, :], AF.Ln, ["sm8"], ["sm8"], bias=1.0)
            tt(sm8[:, 1, :], sm8[:, 0, :], par8[:, 1, :], ALU.mult, ["sm8", "par8"], ["sm8"])
            mm(pmB[:, 1, 0:8], trif, sm8[:, 1, :], True, True, ["cm", "sm8"], ["pmB1"])
            ts(sm8[:, 2, :], pmB[:, 1, 0:8], -1.0, None, ALU.mult, None, ["pmB1"], ["sm8"])
            act(sm8[:, 3, :], pmB[:, 1, 0:8], AF.Exp, ["pmB1"], ["sm8"])
            for b in range(2):
                tt(RtH.rearrange("p (h t) -> p h t", h=4), trif.unsqueeze(1).to_broadcast([128, 4, 128]),
                   sm8[:, 1, 4 * b:4 * b + 4].unsqueeze(2).to_broadcast([128, 4, 128]), ALU.mult, ["cm", "sm8"], ["wst0"])
                mm(pmB[:, b, :], onesf, RtH, True, False, ["cm", "wst0"], ["pmB%d" % b])
                for h in range(4):
                    mm(pmB[:, b, h * 128:(h + 1) * 128], identf, negmf, False, h == 3, ["cm"], ["pmB%d" % b])
            pa8 = pmB[:].rearrange("p b (h t) -> p (b h) t", h=4)
            for h in range(8):
                act(dec[:, h, :], pa8[:, h, :], AF.Exp, ["pmB0", "pmB1", "sm8"], ["E1"], bias=sm8[:, 2, h:h + 1])
            act(sm8[:, 4, :], pa8[:, :, 127], AF.Exp, ["pmB0", "pmB1"], ["sm8"])
            label("dec_ready")

        def ssd_tile(l, t):
            last = (t == NT - 1)
            ck_ = ["E0c%d" % c for c in range(8)]
            for c in range(8):
                act(cacc[:, c, :], cvb[:, c, 3:131], AF.Identity, ["cvb", "cwt", "cbt", "E0"], [ck_[c]], scale=cwt[:, c, 3:4], bias=cbt[:, c:c + 1])
            for j in (2, 1, 0):
                for c in range(8):
                    stt(cacc[:, c, :], cvb[:, c, j:j + 128], cwt[:, c, j:j + 1], cacc[:, c, :], ALU.mult, ALU.add, ["cvb", "cwt", ck_[c]], [ck_[c]])
            act(xbcT[:], cacc, AF.Silu, ck_, ["xbcT", "E0"])
            if last:
                for r_ in range(3):
                    dma(o_pconv[l, r_].rearrange("(c p) -> p c", p=128), cvb[:, :, 128 + r_], ["cvb"], [], "o_pconv", slow=True)
            cp(cvb[:, :, 0:3], cvb[:, :, 128:131], ["cvb"], ["cvb"], eng="gpsimd")
            with atomic():
                for c in range(6):
                    tr(ptr[:, c, :], xbcT[:, c, :], identb, ["xbcT", "cmb"], ["ptr"])
                cp(xBtm[:], ptr[:, 0:6, :], ["ptr"], ["xBtm"], eng="scalar")
            xtm = xBtm[:, 0:4, :].rearrange("p c (e d) -> p (c e) d", e=2)
            wait_label("dec_ready")
            tt(xdt[:].rearrange("p (h d) -> p h d", h=8), xtm, sm8[:, 0, :].unsqueeze(2).to_broadcast([128, 8, 64]), ALU.mult,
               ["xBtm", "sm8"], ["xdt"])
            tt(xss[:].rearrange("p (h d) -> p h d", h=8), xdt[:].rearrange("p (h d) -> p h d", h=8),
               dec[:, :, 127].unsqueeze(2).to_broadcast([128, 8, 64]), ALU.mult, ["xdt", "E1"], ["xss"], eng="gpsimd")
            for g in range(2):
                mm(pmB[:, 0, g * 128:(g + 1) * 128], xbcT[:, 4 + g, :], xbcT[:, 6 + g, :], True, True, ["xbcT"], ["pmB0"])
            for g in range(2):
                tt(wm[:, g * 4:(g + 1) * 4, :], dec[:, g * 4:(g + 1) * 4, :], pmB[:, 0, g * 128:(g + 1) * 128].unsqueeze(1).to_broadcast([128, 4, 128]),
                   ALU.mult, ["E1", "pmB0"], ["wm"])
            for h in range(8):
                mm(pmB[:, 1, h * 64:(h + 1) * 64], wm[:, h, :], xdt[:, h * 64:(h + 1) * 64], True, True, ["wm", "xdt"], ["pmB1"])
            for g in range(2):
                mm(pmB[:, 0, g * 256:(g + 1) * 256], xbcT[:, 6 + g, :], hTb[:, g * 256:(g + 1) * 256], True, True, ["xbcT", "hTb"], ["pmB0"])
            tt(yb1.rearrange("p (h d) -> p h d", h=8), pmB[:, 0, :].rearrange("p (h d) -> p h d", h=8),
               sm8[:, 3, :].unsqueeze(2).to_broadcast([128, 8, 64]), ALU.mult, ["pmB0", "sm8"], ["wst0"])
            tt(yb1, yb1, pmB[:, 1, :], ALU.add, ["wst0", "pmB1"], ["wst0"])
            tt(dsk_t.rearrange("p (h d) -> p h d", h=8), xtm, par8[:, 2, :].unsqueeze(2).to_broadcast([128, 8, 64]), ALU.mult,
               ["xBtm", "par8"], ["E0"], eng="gpsimd")
            tt(yb1, yb1, dsk_t, ALU.add, ["wst0", "E0"], ["wst0"])
            tt(yb1, yb1, yb2, ALU.mult, ["wst0", "wst0g"], ["wst0"])
            for g in range(2):
                act(cacc[:, g, :], yb1[:, g * 256:(g + 1) * 256].rearrange("p (a b) -> p a b", a=2)[:, 0, :], AF.Square, ["wst0"], ["E0", "st10b"],
                    accum_out=st10[:, 1, 4 + g:5 + g])
                act(cacc[:, 2 + g, :], yb1[:, g * 256:(g + 1) * 256].rearrange("p (a b) -> p a b", a=2)[:, 1, :], AF.Square, ["wst0"], ["E0", "st10b"],
                    accum_out=st10[:, 1, 6 + g:7 + g])
            tt(st10[:, 1, 0:2], st10[:, 1, 4:6], st10[:, 1, 6:8], ALU.add, ["st10b"], ["st10b"])
            rsq(st10[:, 1, 2:4], st10[:, 1, 0:2], 1.0 / 256, ["st10b"], ["st10b"])
            for g in range(2):
                ts(ycat[:, 512 + g * 256:512 + (g + 1) * 256], yb1[:, g * 256:(g + 1) * 256], st10[:, 1, 2 + g:3 + g], None, ALU.mult, None,
                   ["wst0", "st10b"], ["ycat1"])
            for g in range(2):
                mm(pmB[:, 0, g * 256:(g + 1) * 256], xBtm[:, 4 + g, :], xss[:, g * 256:(g + 1) * 256], True, True, ["xBtm", "xss"], ["pmB0"])
            tt(hT[:].rearrange("p (h d) -> p h d", h=8), hT[:].rearrange("p (h d) -> p h d", h=8),
               sm8[:, 4, :].unsqueeze(2).to_broadcast([128, 8, 64]), ALU.mult, ["hT", "sm8"], ["hT"], eng="gpsimd")
            tt(hT[:], hT[:], pmB[:, 0, :], ALU.add, ["hT", "pmB0"], ["hT"])
            cp(hTb[:], hT[:], ["hT"], ["hTb"], eng="scalar")
            if last:
                for h in range(8):
                    tr(pmB[0:64, h // 4, (h % 4) * 128:(h % 4 + 1) * 128], hT[:, h * 64:(h + 1) * 64], identf, ["hT", "cm"], ["pmB%d" % (h // 4)])
                cp(ost[0:64, :, :], pmB[0:64, :, :].rearrange("p b (h s) -> p (b h) s", h=4), ["pmB0", "pmB1"], ["E0"])
                dma(o_ph[l].rearrange("h p s -> p h s"), ost[0:64, :, :], ["E0"], [], "o_ph")

        def swa_tile(l, t, rp, rk):
            last = (t == NT - 1)
            par = t % 2
            qkraw = utm[:, 2320:2960].rearrange("p (h d) -> p h d", h=10)
            tt(qkA, utm[:, 2320:2960], utm[:, 2320:2960], ALU.mult, ["utm"], ["qkA"], eng="gpsimd")
            red(st10[:, 2, 0:10], qkA.rearrange("p (h d) -> p h d", h=10), ALU.add, ["qkA"], ["st10c"])
            rsq(st10[:, 3, 0:10], st10[:, 2, 0:10], 1.0 / 64, ["st10c"], ["st10c"])
            qkn3 = qkn.rearrange("p (h d) -> p h d", h=10)
            tt(qkn3, qkraw, st10[:, 3, 0:10].unsqueeze(2).to_broadcast([128, 10, 64]), ALU.mult, ["utm", "st10c"], ["wst1"])
            vs = vsw[par]
            vk = "vsw%d" % par
            cp(vs[:, :, 0:64], utm[:, 2960:3088].rearrange("p (g d) -> p g d", g=2), ["utm"], [vk], eng="scalar")
            mset(vs[:, :, 64:65], 1.0, [vk])
            if last:
                cp(kvf[:, 128:256], utm[:, 2960:3088], ["utm"], ["wst1"], eng="gpsimd")
            label("utm_s")
            tt(qkn3[:, 0:8, :], qkn3[:, 0:8, :], qkw_b[:, 0, :].unsqueeze(1).to_broadcast([128, 8, 64]), ALU.mult, ["wst1", "qkw_b"], ["wst1"], eng="gpsimd")
            tt(qkn3[:, 8:10, :], qkn3[:, 8:10, :], qkw_b[:, 1, :].unsqueeze(1).to_broadcast([128, 2, 64]), ALU.mult, ["wst1", "qkw_b"], ["wst1"], eng="gpsimd")
            qkA3 = qkA.rearrange("p (h d) -> p h d", h=10)
            qkB3 = qkB.rearrange("p (h d) -> p h d", h=10)
            tt(qkA3, qkn3, rp[:, 0:64].unsqueeze(1).to_broadcast([128, 10, 64]), ALU.mult, ["wst1", rk], ["qkA"])
            tt(qkB3[:, :, 0:32], qkn3[:, :, 32:64], rp[:, 64:96].unsqueeze(1).to_broadcast([128, 10, 32]), ALU.mult, ["wst1", rk], ["qkB"], eng="gpsimd")
            tt(qkB3[:, :, 32:64], qkn3[:, :, 0:32], rp[:, 96:128].unsqueeze(1).to_broadcast([128, 10, 32]), ALU.mult, ["wst1", rk], ["qkB"], eng="gpsimd")
            tt(qkp[:], qkA, qkB, ALU.add, ["qkA", "qkB"], ["qkp"])
            if last:
                kt_ = wst[1][:, 896:1024].rearrange("p (h d) -> p h d", h=2)
                kk_ = qkn3[:, 8:10, :]
                ko_ = kvf[:, 0:128].rearrange("p (h d) -> p h d", h=2)
                tt(ko_, kk_, rp[:, 0:64].unsqueeze(1).to_broadcast([128, 2, 64]), ALU.mult, ["wst1", rk], ["wst1"])
                tt(kt_[:, :, 0:32], kk_[:, :, 32:64], rp[:, 64:96].unsqueeze(1).to_broadcast([128, 2, 32]), ALU.mult, ["wst1", rk], ["wst1"])
                tt(kt_[:, :, 32:64], kk_[:, :, 0:32], rp[:, 96:128].unsqueeze(1).to_broadcast([128, 2, 32]), ALU.mult, ["wst1", rk], ["wst1"])
                tt(ko_, ko_, kt_, ALU.add, ["wst1"], ["wst1"])
                dma(o_pk[l].rearrange("t g d -> t (g d)"), kvf[:, 0:128], ["wst1"], [], "o_pk")
                dma(o_pv[l].rearrange("t g d -> t (g d)"), kvf[:, 128:256], ["wst1"], [], "o_pv")
            kT = kTs[par]
            kTk = "kTs%d" % par
            with atomic():
                for c in range(8):
                    tr(ptr[0:64, c, :], qkp[:, c * 64:(c + 1) * 64], identb, ["qkp", "cmb"], ["ptr"])
                cp(qTs[:], ptr[0:64, :, :], ["ptr"], ["qkT2"])
            with atomic():
                for c in range(2):
                    tr(ptr[0:64, c, :], qkp[:, 512 + c * 64:512 + (c + 1) * 64], identb, ["qkp", "cmb"], ["ptr"])
                cp(kT[:], ptr[0:64, 0:2, :], ["ptr"], [kTk], eng="scalar")
            blocks = [(kTs[1 - par], "kTs%d" % (1 - par), vsw[1 - par], "vsw%d" % (1 - par), ntrib)] if t > 0 else []
            blocks.append((kT, kTk, vs, vk, trib))
            nb = len(blocks)
            for g in range(2):
                for bi, (kT_, kTk_, v_, vk_, msk) in enumerate(blocks):
                    bk = 0
                    mm(pj[:, bk, :], kT_[:, g, :], qTs[:, 4 * g:4 * g + 4, :].rearrange("p h t -> p (h t)"), True, True,
                       [kTk_, "qkT2"], ["pj%d" % bk])
                    act(pex[bi][:], pj[:, bk, :], AF.Exp, ["pj%d" % bk], ["pex%d" % bi], scale=0.125)
                    tt(pex[bi][:].rearrange("p (h t) -> p h t", h=4), pex[bi][:].rearrange("p (h t) -> p h t", h=4),
                       msk.unsqueeze(1).to_broadcast([128, 4, 128]), ALU.mult, ["pex%d" % bi, "cmb"], ["pex%d" % bi],
                       eng=("gpsimd" if bi else "vector"))
                for j in range(4):
                    for bi, (kT_, kTk_, v_, vk_, msk) in enumerate(blocks):
                        mm(pj[:, 0, j * 128:j * 128 + 65], pex[bi][:, j * 128:(j + 1) * 128], v_[:, g, 0:65], bi == 0, bi == nb - 1,
                           ["pex%d" % bi, vk_], ["pj0"])
                po = pj[:, 0, :].rearrange("p (h c) -> p h c", h=4)
                tt(st10[:, 2, 4 * g:4 * g + 4], po[:, :, 64], par8[:, 3, 4 * g:4 * g + 4], ALU.add, ["pj0", "par8"], ["st10c"])
                op("vector", lambda e, g=g: e.reciprocal(out=st10[:, 2, 4 * g:4 * g + 4], in_=st10[:, 2, 4 * g:4 * g + 4]), ["st10c"], ["st10c"])
                tt(osw[:, g * 256:(g + 1) * 256].rearrange("p (h d) -> p h d", h=4), po[:, :, 0:64],
                   st10[:, 2, 4 * g:4 * g + 4].unsqueeze(2).to_broadcast([128, 4, 64]), ALU.mult, ["pj0", "st10c"], ["wst1"])
            tt(ycat[:, 1024:1536], osw, gsil2, ALU.mult, ["wst1", "gsil2"], ["ycat2"])

        if with_sample:
            cst = sb("cst", [128, 1152])
            dma(cst[:], csel, (), ["cst"], "cst")

            class _Sel8:
                def __getitem__(self, key):
                    j = key[1]
                    return cst[0:16, 7 - j:135 - j]
            Sel8 = _Sel8()
            Sel8T = cst[:, 136:264].rearrange("p (j b) -> p j b", j=8)
            Sel2 = cst[0:64, 264:392]
            SelP4 = cst[0:4, 392:520]
            Pair = cst[:, 520:648]
            Quad = cst[:, 648:776]
            Sel2T = cst[:, 776:840]
            M8 = cst[:, 840:848]
            ropeS = cst[0:16, 1024:1152]
            xs_t = sb("xs_t", [16, 1024])
            dma(xs_t[:], xs_in, (), ["xs_t"], "xs_t")
            xnTs = xnT[:, :, 0:16]
            mA = wst[0][:, 0:392]
            mB = wst[0][:, 392:464]
            mP = sb("mP", [128, 8])
            ms = sb("ms", [128, 16])
            mn1 = wst[0][:, 464:536]
            t64 = wst[1][:, 256:320]
            mh = wst[1][:, 320:384]
            g1 = wst[1][:, 384:448]
            g2 = wst[1][:, 448:512]
            nm64 = wst[1][0:64, 512:584]
            par4 = sb("par4", [4, 2])
            sd = sb("sd", [16, 16])
            fmc = xt[1][:, 0:512].rearrange("p (r c b) -> p r c b", r=4, c=8)
            fma = wst[1][:, 0:128].rearrange("p (c b) -> p c b", c=8)
            fmb = wst[1][:, 128:256].rearrange("p (c b) -> p c b", c=8)
            sTt = wst[0][:, 664:792]
            oTs = wst[0][0:64, 536:664]
            qm = xt[1][0:16, 0:512]
            ycs = ycat[0:16, :]
            ycTs = ycatT[:, :, 0:16]
            sb_qm2 = xt[1][0:16, 512:1024]
            sTmp = xt[0][:, 0:512]
            sA = Eall[:, 0:2048]
            sB = Eall[:, 2048:4096]
            KA = ["E0", "E1"]
            KB = ["E2", "E3"]

        def sample_layer(l):
            u = utm[0:16, :]
            uq = xt[1][0:16, 0:256]
            ubx = xt[0][0:16, :]
            act(xsb[0:16, :], xs_t[:], AF.Square, ["xs_t"], ["xsb", "st1"], scale=1.0 / 32, accum_out=st1[0:16, 0:1])
            rsq(st1[0:16, 1:2], st1[0:16, 0:1], 1.0, ["st1"], ["st1"])
            act(xsb[0:16, :], xs_t[:], AF.Copy, ["xs_t", "st1"], ["xsb"], scale=st1[0:16, 1:2])
            for c in range(8):
                tr(ptr[:, c, 0:16], xsb[0:16, c * 128:(c + 1) * 128], identb[0:16, 0:16], ["xsb", "cmb"], ["ptr"])
            cp(xnTs, ptr[:, :, 0:16], ["ptr"], ["xnT"])
            plan = []
            for (c0, n, dst, dk, off) in ((0, 256, uq, "xt1", 0), (256, 2312, u, "utm", 0), (2568, 1024, ubx, "xt0", 0), (3592, 1288, u, "utm", 2312)):
                o = 0
                while o < n:
                    w_ = min(512, n - o)
                    plan.append((c0 + o, w_, dst, dk, off + o))
                    o += w_
            for i, (c0, w_, dst, dk, off) in enumerate(plan):
                b = i % 2
                for kc in range(8):
                    mm(pj[0:16, b, 0:w_], xnTs[:, kc, :], Wb[:, kc, c0:c0 + w_], kc == 0, kc == 7, ["xnT", "Wb"], ["pj%d" % b])
                cp(dst[:, off:off + w_], pj[0:16, b, 0:w_], ["pj%d" % b], [dk], eng=("scalar" if i % 2 else "vector"))

            def expand(dst_ps, pkey, srcs, rkeys):
                for j in range(8):
                    mm(dst_ps, Sel8[:, j, :], srcs[j], j == 0, j == 7, ["cst"] + rkeys, [pkey])

            hv = [(j // 2, j % 2) for j in range(8)]
            expand(pmA[:, 0, 0:64], "pmA0", [uq[:, h * 64:(h + 1) * 64] for h, v in hv], ["xt1"])
            expand(pmA[:, 0, 64:128], "pmA0", [u[:, h * 64:(h + 1) * 64] for h, v in hv], ["utm"])
            expand(pmA[:, 0, 128:192], "pmA0", [u[:, 256 + h * 128 + v * 64:256 + h * 128 + v * 64 + 64] for h, v in hv], ["utm"])
            expand(pmA[:, 0, 192:256], "pmA0", [u[:, 768 + h * 128 + v * 64:768 + h * 128 + v * 64 + 64] for h, v in hv], ["utm"])
            expand(pmA[:, 0, 256:320], "pmA0", [u[:, 1280 + h * 128 + v * 64:1280 + h * 128 + v * 64 + 64] for h, v in hv], ["utm"])
            expand(pmA[:, 0, 320:321], "pmA0", [u[:, 1792 + h:1793 + h] for h, v in hv], ["utm"])
            expand(pmA[:, 0, 321:322], "pmA0", [u[:, 1796 + h:1797 + h] for h, v in hv], ["utm"])
            cp(mA[:, 0:322], pmA[:, 0, 0:322], ["pmA0"], ["wst0"])
            q_, k_, v_ = mA[:, 0:64], mA[:, 64:128], mA[:, 128:192]
            dma(nm64[:, 0:64], i_sn[l].rearrange("b h k -> (b h) k"), (), ["wst1"], "nm64a")
            dma(nm64[:, 64:65], i_sm[l].rearrange("b (h o) -> (b h) o", o=1), (), ["wst1"], "nm64b", slow=True)
            mm(pmA[:, 1, 0:65], Sel2, nm64[:, 0:65], True, True, ["cst", "wst1"], ["pmA1"])
            dma(par4[:, 0:1], a_ib[l].rearrange("(h o) -> h o", o=1), (), ["par4"], "par4a", slow=True)
            dma(par4[:, 1:2], a_fb[l].rearrange("(h o) -> h o", o=1), (), ["par4"], "par4b", slow=True)
            mm(pmA[:, 1, 128:130], SelP4, par4[:], True, True, ["cst", "par4"], ["pmA1"])
            cp(mB[:, 0:65], pmA[:, 1, 0:65], ["pmA1"], ["wst0"])
            cp(mP[:, 0:2], pmA[:, 1, 128:130], ["pmA1"], ["mP"], eng="scalar")
            tt(ms[:, 0:1], mA[:, 321:322], mP[:, 1:2], ALU.add, ["wst0", "mP"], ["ms"])
            act(ms[:, 1:2], ms[:, 0:1], AF.Exp, ["ms"], ["ms"], scale=-1.0)
            act(ms[:, 1:2], ms[:, 1:2], AF.Ln, ["ms"], ["ms"], bias=1.0)
            tt(ms[:, 2:3], mB[:, 64:65], ms[:, 1:2], ALU.subtract, ["wst0", "ms"], ["ms"])
            tt(ms[:, 3:4], mA[:, 320:321], mP[:, 0:1], ALU.add, ["wst0", "mP"], ["ms"])
            tt(ms[:, 4:5], ms[:, 2:3], ms[:, 3:4], ALU.max, ["ms"], ["ms"])
            tt(ms[:, 5:6], ms[:, 2:3], ms[:, 4:5], ALU.subtract, ["ms"], ["ms"])
            act(ms[:, 5:6], ms[:, 5:6], AF.Exp, ["ms"], ["ms"])
            tt(ms[:, 6:7], ms[:, 3:4], ms[:, 4:5], ALU.subtract, ["ms"], ["ms"])
            act(ms[:, 6:7], ms[:, 6:7], AF.Exp, ["ms"], ["ms"])
            act(ms[:, 7:8], ms[:, 4:5], AF.Exp, ["ms"], ["ms"], scale=-1.0)
            ts(mn1[:, 0:64], mB[:, 0:64], ms[:, 5:6], None, ALU.mult, None, ["wst0", "ms"], ["wst0"])
            stt(mn1[:, 0:64], k_, ms[:, 6:7], mn1[:, 0:64], ALU.mult, ALU.add, ["wst0", "ms", "wst0"], ["wst0"])
            cp(mn1[:, 64:65], ms[:, 4:5], ["ms"], ["wst0"])
            tt(t64, mn1[:, 0:64], q_, ALU.mult, ["wst0", "wst0"], ["wst1"])
            red(ms[:, 8:9], t64, ALU.add, ["wst1"], ["ms"])
            act(ms[:, 8:9], ms[:, 8:9], AF.Abs, ["ms"], ["ms"])
            tt(ms[:, 8:9], ms[:, 8:9], ms[:, 7:8], ALU.max, ["ms"], ["ms"])
            op("vector", lambda e: e.reciprocal(out=ms[:, 8:9], in_=ms[:, 8:9]), ["ms"], ["ms"])
            Cin = i_sC[l].rearrange("b h (vh v) k -> (b h vh) (v k)", vh=2)
            Cout = o_sC[l].rearrange("b h (vh v) k -> (b h vh) (v k)", vh=2)
            for r in range(2):
                dma(sA, Cin[:, r * 2048:(r + 1) * 2048], (), KA, "sA")
                tt(sB.rearrange("p (v k) -> p v k", v=32), v_[:, r * 32:(r + 1) * 32].unsqueeze(2).to_broadcast([128, 32, 64]),
                   k_.unsqueeze(1).to_broadcast([128, 32, 64]), ALU.mult, ["wst0"], KB)
                ts(sA, sA, ms[:, 5:6], None, ALU.mult, None, KA + ["ms"], KA)
                stt(sA, sB, ms[:, 6:7], sA, ALU.mult, ALU.add, KA + KB + ["ms"], KA)
                dma(Cout[:, r * 2048:(r + 1) * 2048], sA, KA, [], "oC")
                tt(sB.rearrange("p (v k) -> p v k", v=32), sA.rearrange("p (v k) -> p v k", v=32), q_.unsqueeze(1).to_broadcast([128, 32, 64]),
                   ALU.mult, KA + ["wst0"], KB)
                red(mh[:, r * 32:(r + 1) * 32], sB.rearrange("p (v k) -> p v k", v=32), ALU.add, KB, ["wst1"])
            ts(mh, mh, ms[:, 8:9], None, ALU.mult, None, ["wst1", "ms"], ["wst1"])
            tt(t64, mh, mh, ALU.mult, ["wst1"], ["wst1"])
            red(ms[:, 9:10], t64, ALU.add, ["wst1"], ["ms"])
            mm(ptf[:, 300:301], Pair, ms[:, 9:10], True, True, ["cst", "ms"], ["ptf"])
            rsq(ms[:, 10:11], ptf[:, 300:301], 1.0 / 128, ["ptf"], ["ms"])
            ts(mh, mh, ms[:, 10:11], None, ALU.mult, None, ["wst1", "ms"], ["wst1"])
            act(g1, mA[:, 192:256], AF.Sigmoid, ["wst0"], ["wst1"])
            act(g2, mA[:, 256:320], AF.Silu, ["wst0"], ["wst1"])
            tt(g1, g1, g2, ALU.mult, ["wst1", "wst1"], ["wst1"])
            tt(mh, mh, g1, ALU.mult, ["wst1", "wst1"], ["wst1"])
            for j in range(8):
                mm(pmB[0:16, 0, j * 64:(j + 1) * 64], Sel8T[:, j, :], mh, True, True, ["cst", "wst1"], ["pmB0"])
            cp(ycs[:, 0:512], pmB[0:16, 0, :], ["pmB0"], ["ycat0"])
            mm(ptf[0:64, 304:369], Sel2T, mn1[:, 0:65], True, True, ["cst", "wst0"], ["ptf"])
            cp(nm64[:, 0:65], ptf[0:64, 304:369], ["ptf"], ["wst1"])
            dma(o_sn[l].rearrange("b h k -> (b h) k"), nm64[:, 0:64], ["wst1"], [], "on")
            dma(o_sm[l].rearrange("b (h o) -> (b h) o", o=1), nm64[:, 64:65], ["wst1"], [], "om", slow=True)

            cb3 = Eall[0:16, 0:3072]
            xbs = Eall[0:16, 3072:4096]
            dma(cb3, i_sconv[l].rearrange("b r c -> b (r c)"), (), ["E0", "E1", "E2"], "cb3")
            dma(o_sconv[l, :, 0:2, :].rearrange("b r c -> b (r c)"), cb3[:, 1024:3072], ["E0", "E1", "E2"], [], "oconv_a")
            dma(o_sconv[l, :, 2, :], ubx, ["xt0"], [], "oconv_b")
            for r in range(4):
                src = cb3[:, r * 1024:(r + 1) * 1024] if r < 3 else ubx
                for c in range(8):
                    tr(ptf[:, c * 16:(c + 1) * 16], src[:, c * 128:(c + 1) * 128], identf[0:16, 0:16], ["E0", "E1", "E2", "xt0", "cm"], ["ptf"])
                cp(fmc[:, r, :, :], ptf[:, 0:128].rearrange("p (c b) -> p c b", c=8), ["ptf"], ["xt1"], eng=("scalar" if r % 2 else "vector"))
            tt(fma, fmc[:, 3, :, :], cwt[:, :, 3].unsqueeze(2).to_broadcast([128, 8, 16]), ALU.mult, ["xt1", "cwt"], ["wst1"])
            for j in range(3):
                tt(fmb, fmc[:, j, :, :], cwt[:, :, j].unsqueeze(2).to_broadcast([128, 8, 16]), ALU.mult, ["xt1", "cwt"], ["wst1"])
                tt(fma, fma, fmb, ALU.add, ["wst1", "wst1"], ["wst1"])
            tt(fma, fma, cbt[:].unsqueeze(2).to_broadcast([128, 8, 16]), ALU.add, ["wst1", "cbt"], ["wst1"])
            act(fma, fma, AF.Silu, ["wst1"], ["wst1"])
            for c in range(8):
                pt_ = pmA if c < 4 else pmB
                tr(pt_[0:16, 1, (c % 4) * 128:(c % 4 + 1) * 128], fma[:, c, :], identf, ["wst1", "cm"], ["pmA1" if c < 4 else "pmB1"])
            cp(xbs[:, 0:512], pmA[0:16, 1, :], ["pmA1"], ["E3"])
            cp(xbs[:, 512:1024], pmB[0:16, 1, :], ["pmB1"], ["E3"], eng="scalar")
            tt(sd[:, 0:8], u[:, 2312:2320], par8[0:16, 0, :], ALU.add, ["utm", "par8"], ["sd"])
            act(sd[:, 0:8], sd[:, 0:8], AF.Exp, ["sd"], ["sd"])
            act(sd[:, 0:8], sd[:, 0:8], AF.Ln, ["sd"], ["sd"], bias=1.0)
            tt(sd[:, 8:16], sd[:, 0:8], par8[0:16, 1, :], ALU.mult, ["sd", "par8"], ["sd"])
            act(sd[:, 8:16], sd[:, 8:16], AF.Exp, ["sd"], ["sd"])
            hh = list(range(8))
            expand(pmA[:, 0, 0:64], "pmA0", [xbs[:, h * 64:(h + 1) * 64] for h in hh], ["E3"])
            expand(pmA[:, 0, 64:192], "pmA0", [xbs[:, 512 + (h // 4) * 128:512 + (h // 4 + 1) * 128] for h in hh], ["E3"])
            expand(pmA[:, 0, 192:320], "pmA0", [xbs[:, 768 + (h // 4) * 128:768 + (h // 4 + 1) * 128] for h in hh], ["E3"])
            expand(pmA[:, 0, 320:384], "pmA0", [u[:, 1800 + h * 64:1800 + (h + 1) * 64] for h in hh], ["utm"])
            expand(pmA[:, 0, 384:385], "pmA0", [sd[:, h:h + 1] for h in hh], ["sd"])
            expand(pmA[:, 0, 385:386], "pmA0", [sd[:, 8 + h:9 + h] for h in hh], ["sd"])
            cp(mA[:, 0:386], pmA[:, 0, 0:386], ["pmA0"], ["wst0"])
            x_, B_, C_, z_ = mA[:, 0:64], mA[:, 64:192], mA[:, 192:320], mA[:, 320:384]
            tt(mP[:, 0:8], par8[:, 2, :], M8, ALU.mult, ["par8", "cst"], ["mP"])
            red(ms[:, 11:12], mP[:, 0:8], ALU.add, ["mP"], ["ms"])
            tt(mP[:, 0:8], par8[:, 3, :], M8, ALU.mult, ["par8", "cst"], ["mP"])
            red(ms[:, 12:13], mP[:, 0:8], ALU.add, ["mP"], ["ms"])
            ts(t64, x_, mA[:, 384:385], None, ALU.mult, None, ["wst0"], ["wst1"])
            Hin = i_sh[l].rearrange("b h p s -> (b h) (p s)")
            Hout = o_sh[l].rearrange("b h p s -> (b h) (p s)")
            for r in range(4):
                dma(sA, Hin[:, r * 2048:(r + 1) * 2048], (), KA, "sA")
                tt(sB.rearrange("p (a s) -> p a s", a=16), t64[:, r * 16:(r + 1) * 16].unsqueeze(2).to_broadcast([128, 16, 128]),
                   B_.unsqueeze(1).to_broadcast([128, 16, 128]), ALU.mult, ["wst1", "wst0"], KB)
                stt(sA, sA, mA[:, 385:386], sB, ALU.mult, ALU.add, KA + KB + ["wst0"], KA)
                dma(Hout[:, r * 2048:(r + 1) * 2048], sA, KA, [], "oh")
                tt(sB.rearrange("p (a s) -> p a s", a=16), sA.rearrange("p (a s) -> p a s", a=16), C_.unsqueeze(1).to_broadcast([128, 16, 128]),
                   ALU.mult, KA + ["wst0"], KB)
                red(mh[:, r * 16:(r + 1) * 16], sB.rearrange("p (a s) -> p a s", a=16), ALU.add, KB, ["wst1"])
            stt(mh, x_, ms[:, 11:12], mh, ALU.mult, ALU.add, ["wst0", "ms", "wst1"], ["wst1"])
            act(g1, z_, AF.Silu, ["wst0"], ["wst1"])
            tt(mh, mh, g1, ALU.mult, ["wst1", "wst1"], ["wst1"])
            tt(t64, mh, mh, ALU.mult, ["wst1"], ["wst1"])
            red(ms[:, 9:10], t64, ALU.add, ["wst1"], ["ms"])
            mm(ptf[:, 300:301], Quad, ms[:, 9:10], True, True, ["cst", "ms"], ["ptf"])
            rsq(ms[:, 10:11], ptf[:, 300:301], 1.0 / 256, ["ptf"], ["ms"])
            ts(mh, mh, ms[:, 10:11], None, ALU.mult, None, ["wst1", "ms"], ["wst1"])
            for j in range(8):
                mm(pmB[0:16, 1, j * 64:(j + 1) * 64], Sel8T[:, j, :], mh, True, True, ["cst", "wst1"], ["pmB1"])
            cp(ycs[:, 512:1024], pmB[0:16, 1, :], ["pmB1"], ["ycat1"])

            qkn_s, qkA_s, qkB_s = Eall[0:16, 0:640], Eall[0:16, 1024:1664], Eall[0:16, 2048:2688]
            tt(qkA_s, u[:, 2320:2960], u[:, 2320:2960], ALU.mult, ["utm"], ["E1"])
            red(sd[:, 0:10], qkA_s.rearrange("p (h d) -> p h d", h=10), ALU.add, ["E1"], ["sd"])
            rsq(sd[:, 0:10], sd[:, 0:10], 1.0 / 64, ["sd"], ["sd"])
            n3 = qkn_s.rearrange("p (h d) -> p h d", h=10)
            tt(n3, u[:, 2320:2960].rearrange("p (h d) -> p h d", h=10), sd[:, 0:10].unsqueeze(2).to_broadcast([16, 10, 64]), ALU.mult,
               ["utm", "sd"], ["E0"])
            tt(n3[:, 0:8, :], n3[:, 0:8, :], qkw_b[0:16, 0, :].unsqueeze(1).to_broadcast([16, 8, 64]), ALU.mult, ["E0", "qkw_b"], ["E0"])
            tt(n3[:, 8:10, :], n3[:, 8:10, :], qkw_b[0:16, 1, :].unsqueeze(1).to_broadcast([16, 2, 64]), ALU.mult, ["E0", "qkw_b"], ["E0"])
            A3 = qkA_s.rearrange("p (h d) -> p h d", h=10)
            B3 = qkB_s.rearrange("p (h d) -> p h d", h=10)
            tt(A3, n3, ropeS[:, 0:64].unsqueeze(1).to_broadcast([16, 10, 64]), ALU.mult, ["E0", "cst"], ["E1"])
            tt(B3[:, :, 0:32], n3[:, :, 32:64], ropeS[:, 64:96].unsqueeze(1).to_broadcast([16, 10, 32]), ALU.mult, ["E0", "cst"], ["E2"])
            tt(B3[:, :, 32:64], n3[:, :, 0:32], ropeS[:, 96:128].unsqueeze(1).to_broadcast([16, 10, 32]), ALU.mult, ["E0", "cst"], ["E2"])
            tt(qkn_s, qkA_s, qkB_s, ALU.add, ["E1", "E2"], ["E0"])
            cp(qm, qkn_s[:, 0:512], ["E0"], ["xt1"])
            okl = "ok%d" % l
            dma(o_sk[l, :, 127, :, :].rearrange("b g d -> b (g d)"), qkn_s[:, 512:640], ["E0"], [okl], "ok_new")
            dma(o_sv[l, :, 127, :, :].rearrange("b g d -> b (g d)"), u[:, 2960:3088], ["utm"], [okl], "ov_new")
            dma(o_sk[l, :, 0:127, :, :].rearrange("b j g d -> b (j g d)"), i_ck[l, :, 1:128, :, :].rearrange("b j g d -> b (j g d)"), (), [okl], "ok_cp")
            dma(o_sv[l, :, 0:127, :, :].rearrange("b j g d -> b (j g d)"), i_cv[l, :, 1:128, :, :].rearrange("b j g d -> b (j g d)"), (), [okl], "ov_cp")
            K1 = sA.rearrange("p (b n) -> p b n", b=16)
            V1 = sB.rearrange("p (b n) -> p b n", b=16)
            dma(K1, o_sk[l].rearrange("b j g d -> j b (g d)"), [okl], KA, "sA")
            dma(V1, o_sv[l].rearrange("b j g d -> j b (g d)"), [okl], KB, "sB")
            qm2 = sb_qm2
            for b in range(16):
                ts(qm2, qm, identf[0:16, b:b + 1], None, ALU.mult, None, ["xt1", "cm"], ["xt1"])
                mm(pj[:, b % 2, :], onesf[0:16, :], qm2, True, True, ["cm", "xt1"], ["pj%d" % (b % 2)])
                tt(sTmp.rearrange("p (g h d) -> p g h d", g=2, h=4), K1[:, b, :].rearrange("p (g d) -> p g d", g=2).unsqueeze(2).to_broadcast([128, 2, 4, 64]),
                   pj[:, b % 2, :].rearrange("p (g h d) -> p g h d", g=2, h=4), ALU.mult, KA + ["pj%d" % (b % 2)], ["xt0"])
                red(sTt[:, b * 8:(b + 1) * 8], sTmp.rearrange("p (h d) -> p h d", h=8), ALU.add, ["xt0"], ["wst0"])
            act(sTt, sTt, AF.Exp, ["wst0"], ["wst0"], scale=0.125)
            mm(ptf[:, 300:301], sTt, onesf[:, 0:1], True, True, ["wst0", "cm"], ["ptf"])
            for b in range(16):
                for g in range(2):
                    mm(pmA[0:64, 0, (b * 8 + g * 4):(b * 8 + g * 4 + 4)], V1[:, b, g * 64:(g + 1) * 64], sTt[:, b * 8 + g * 4:b * 8 + g * 4 + 4],
                       True, True, KB + ["wst0"], ["pmA0"])
            cp(oTs, pmA[0:64, 0, 0:128], ["pmA0"], ["wst0"])
            tr(ptf[:, 320:384], oTs, identf[0:64, 0:64], ["wst0", "cm"], ["ptf"])
            tt(ms[:, 13:14], ptf[:, 300:301], ms[:, 12:13], ALU.add, ["ptf", "ms"], ["ms"])
            op("vector", lambda e: e.reciprocal(out=ms[:, 13:14], in_=ms[:, 13:14]), ["ms"], ["ms"])
            ts(mh, ptf[:, 320:384], ms[:, 13:14], None, ALU.mult, None, ["ptf", "ms"], ["wst1"])
            expand(pmA[:, 1, 0:64], "pmA1", [u[:, 3088 + h * 64:3088 + (h + 1) * 64] for h in hh], ["utm"])
            act(g1, pmA[:, 1, 0:64], AF.Silu, ["pmA1"], ["wst1"])
            tt(mh, mh, g1, ALU.mult, ["wst1", "wst1"], ["wst1"])
            for j in range(8):
                mm(pmB[0:16, 0, j * 64:(j + 1) * 64], Sel8T[:, j, :], mh, True, True, ["cst", "wst1"], ["pmB0"])
            cp(ycs[:, 1024:1536], pmB[0:16, 0, :], ["pmB0"], ["ycat2"])

            for rnd, (k0, k1) in enumerate(((0, 8), (8, 12))):
                for kc in range(k0, k1):
                    tr(ptr[:, kc - k0, 0:16], ycs[:, kc * 128:(kc + 1) * 128], identb[0:16, 0:16], ["ycat0", "ycat1", "ycat2", "cmb"], ["ptr"])
                cp(ycTs[:, k0:k1, :], ptr[:, 0:k1 - k0, 0:16], ["ptr"], ["ycatT"])
            for n in range(2):
                for kc in range(12):
                    mm(pj[0:16, n, :], ycTs[:, kc, :], Wo[:, kc, n * 512:(n + 1) * 512], kc == 0, kc == 11, ["ycatT", "Wo"], ["pj%d" % n])
                tt(xs_t[:, n * 512:(n + 1) * 512], pj[0:16, n, :], xs_t[:, n * 512:(n + 1) * 512], ALU.add, ["pj%d" % n, "xs_t"], ["xs_t"])
            if l == DEPTH - 1:
                dma(o_ys, xs_t[:], ["xs_t"], [], "ys")

        for l in range(DEPTH):
            load_layer(l)
            if with_sample:
                sample_layer(l)
            for t in range(NT):
                tile_step(l, t)
        S.emit(st)
        build.stats = S.stats
    return nc


PNAMES = ["w_in", "w_out", "norm_w", "a_igate_b", "a_fgate_b", "a_norm_w", "b_conv_w", "b_conv_b", "b_dt_bias", "b_A_log",
          "b_D", "b_norm_w", "c_qnorm_w", "c_knorm_w", "c_sinks"]


def make_in_maps(inputs, NT, DEPTH, with_sample):
    T = NT * 128
    consts = host_consts(T)
    if not with_sample:
        consts.pop("csel")
    shared = {n: np.ascontiguousarray(inputs[n][:DEPTH], dtype=np.float32) for n in PNAMES}
    shared.update(consts)
    in_maps = []
    for c in range(8):
        m = dict(shared)
        m["xp"] = np.ascontiguousarray(inputs["x_prompt"][c % 2, :T], dtype=np.float32)
        if with_sample:
            b0 = c * 16
            m["xs_in"] = np.ascontiguousarray(inputs["x_sample"][b0:b0 + 16, 0, :], dtype=np.float32)
            for nm, key in (("sC", "state_mlstm_C"), ("sn", "state_mlstm_n"), ("sm", "state_mlstm_m"), ("sh", "state_ssm"),
                            ("sconv", "state_conv"), ("ck", "cache_k"), ("cv", "cache_v")):
                m[nm] = np.ascontiguousarray(inputs[key][:DEPTH, b0:b0 + 16], dtype=np.float32)
        in_maps.append(m)
    return in_maps


def prompt_in_maps(inputs, NT, DEPTH):
    return make_in_maps(inputs, NT, DEPTH, False)


def run_prompt_only(inputs, NT, DEPTH):
    nc = build(NT, DEPTH)
    in_maps = prompt_in_maps(inputs, NT, DEPTH)
    res = run_bass_kernel_spmd(nc, in_maps, core_ids=list(range(8)))
    return res.results


def assemble(results, DEPTH):
    r = results
    hp = np.stack([r[0]["yp"], r[1]["yp"]], 0)
    hs = np.concatenate([r[c]["ys"] for c in range(8)], 0)[:, None, :]

    def pstack(nm):
        return np.stack([r[0][nm], r[1][nm]], 1)

    def scat(nm):
        return np.concatenate([r[c][nm] for c in range(8)], 1)

    outs = (hp, hs, pstack("pC"), pstack("pn"), pstack("pm"), pstack("ph"), pstack("pconv"), pstack("pk"), pstack("pv"),
            scat("oC"), scat("on"), scat("om"), scat("oh"), scat("oconv"), scat("ok"), scat("ov"))
    return tuple(np.ascontiguousarray(o, dtype=np.float32) for o in outs)


def kernel(**inputs):
    NT = inputs["x_prompt"].shape[1] // 128
    DEPTH = inputs["w_in"].shape[0]
    nc = build(NT, DEPTH, with_sample=True)
    in_maps = make_in_maps(inputs, NT, DEPTH, True)
    res = run_bass_kernel_spmd(nc, in_maps, core_ids=list(range(8)))
    return assemble(res.results, DEPTH)
```

```python
import contextlib
import math
import os
import numpy as np
import concourse.bass as bass
import concourse.mybir as mybir
from concourse.bass_utils import run_bass_kernel_spmd

F32 = mybir.dt.float32
BF16 = mybir.dt.bfloat16
AF = mybir.ActivationFunctionType
ALU = mybir.AluOpType
AX = mybir.AxisListType

D_MODEL = 1024
D_IN = 4880
D_MIX = 1536
EPS = 1e-6
NEG = -30000.0


class _Op:
    __slots__ = ("eng", "fn", "reads", "writes", "dma", "deps", "sig", "waits", "clock", "idx")


class Sched:
    ENGS = ("tensor", "vector", "scalar", "gpsimd", "sync")

    def __init__(self, nc):
        self.nc = nc
        self.ops = []
        self.last_w = {}
        self.readers = {}

    def op(self, eng, fn, reads=(), writes=(), dma=None):
        o = _Op()
        o.eng, o.fn, o.reads, o.writes, o.dma = eng, fn, tuple(reads), tuple(writes), dma
        o.idx = len(self.ops)
        deps = set()
        for k in o.reads:
            w = self.last_w.get(k)
            if w is not None:
                deps.add(w)
        for k in o.writes:
            w = self.last_w.get(k)
            if w is not None:
                deps.add(w)
            for r in self.readers.get(k, ()):
                deps.add(r)
        if eng == "tensor":
            deps = {d for d in deps if self.ops[d].eng != "tensor"}
        o.deps = deps
        for k in o.reads:
            self.readers.setdefault(k, []).append(o.idx)
        for k in o.writes:
            self.last_w[k] = o.idx
            self.readers[k] = []
        self.ops.append(o)
        return o

    def emit(self, stack, final_wait_eng="sync"):
        nc = self.nc
        ops = self.ops
        needed = set()
        for o in ops:
            needed.update(o.deps)
        sems = {}
        counts = {}
        for o in ops:
            if o.dma is not None:
                sname = "d_" + o.dma
                counts[sname] = counts.get(sname, 0) + 16
                o.sig = (sname, counts[sname])
            elif o.idx in needed:
                sname = "e_" + o.eng
                counts[sname] = counts.get(sname, 0) + 1
                o.sig = (sname, counts[sname])
            else:
                o.sig = None
        known = {e: {} for e in self.ENGS}
        for o in ops:
            kn = known[o.eng]
            waits = []
            for d in sorted(o.deps, reverse=True):
                do = ops[d]
                sname, val = do.sig
                if kn.get(sname, 0) >= val:
                    continue
                waits.append((sname, val))
                kn[sname] = val
                for s2, v2 in do.clock.items():
                    if kn.get(s2, 0) < v2:
                        kn[s2] = v2
            o.waits = waits
            o.clock = dict(kn)
        for s in counts:
            sems[s] = stack.enter_context(nc.semaphore(s))
        by_eng = {e: [o for o in ops if o.eng == e] for e in self.ENGS}
        self.stats = dict(n_ops=len(ops), n_waits=sum(len(o.waits) for o in ops), n_sems=len(sems),
                          per_eng={e: len(v) for e, v in by_eng.items()})

        def run(engobj, lst, do_final):
            for o in lst:
                for sname, val in o.waits:
                    engobj.wait_ge(sems[sname], val)
                ins = o.fn(engobj)
                if o.sig is not None:
                    ins.then_inc(sems[o.sig[0]], 16 if o.dma is not None else 1)
            if do_final:
                for sname, val in counts.items():
                    if sname.startswith("d_"):
                        engobj.wait_ge(sems[sname], val)

        with nc.Block() as block:
            @block.tensor
            def _(e):
                run(e, by_eng["tensor"], final_wait_eng == "tensor")

            @block.vector
            def _(e):
                run(e, by_eng["vector"], final_wait_eng == "vector")

            @block.scalar
            def _(e):
                run(e, by_eng["scalar"], final_wait_eng == "scalar")

            @block.gpsimd
            def _(e):
                run(e, by_eng["gpsimd"], final_wait_eng == "gpsimd")

            @block.sync
            def _(e):
                run(e, by_eng["sync"], final_wait_eng == "sync")


def host_consts(T):
    c = {}
    ident = np.eye(128, dtype=np.float32)
    s = np.arange(128)
    tri = (s[:, None] <= s[None, :]).astype(np.float32)
    negm = np.where(s[None, :] < s[:, None], NEG, 0.0).astype(np.float32)
    c["cmat"] = np.concatenate([ident, tri, 1.0 - tri, np.ones((128, 128), np.float32),
                                np.tile(negm[:, None, :], (1, 1, 1)).reshape(128, 128)], axis=1)
    pos = np.arange(T, dtype=np.float32)
    inv = (10000.0 ** (-np.arange(32, dtype=np.float32) / 32)).astype(np.float32)
    ang = pos[:, None] * inv[None, :]
    cos, sin = np.cos(ang).astype(np.float32), np.sin(ang).astype(np.float32)
    c["rope"] = np.concatenate([cos, cos, -sin, sin], axis=1).astype(np.float32)
    cs = np.zeros((128, 1152), np.float32)
    p = np.arange(128)
    for b in range(16):
        cs[b, 7 + 8 * b] = 1.0
        for j in range(8):
            cs[b * 8 + j, 136 + j * 16 + b] = 1.0
    cs[p // 2, 264 + p] = 1.0
    cs[(p % 8) // 2, 392 + p] = 1.0
    cs[:, 520:648] = (p[:, None] // 2 == p[None, :] // 2)
    cs[:, 648:776] = (p[:, None] // 4 == p[None, :] // 4)
    cs[2 * np.arange(64), 776 + np.arange(64)] = 1.0
    cs[p, 840 + p % 8] = 1.0
    angs = np.float32(8192.0) * inv
    cS, sS = np.cos(angs).astype(np.float32), np.sin(angs).astype(np.float32)
    cs[0:16, 1024:1152] = np.concatenate([cS, cS, -sS, sS])[None, :]
    c["csel"] = cs
    return c


def build(NT, DEPTH, with_sample=False, nc=None, ins=None, outs=None):
    T = NT * 128
    if nc is None:
        nc = bass.Bass("TRN2", target_bir_lowering=False)

    def din(name, shape):
        if ins is not None:
            assert list(ins[name].shape) == list(shape), (name, ins[name].shape, shape)
            return ins[name]
        return nc.dram_tensor(name, shape, F32, kind="ExternalInput").ap()

    def dout(name, shape):
        if outs is not None:
            assert list(outs[name].shape) == list(shape), (name, outs[name].shape, shape)
            return outs[name]
        return nc.dram_tensor(name, shape, F32, kind="ExternalOutput").ap()

    xp = din("xp", [T, 1024])
    w_in = din("w_in", [DEPTH, 1024, D_IN])
    w_out = din("w_out", [DEPTH, D_MIX, 1024])
    norm_w = din("norm_w", [DEPTH, 1024])
    a_ib = din("a_igate_b", [DEPTH, 4])
    a_fb = din("a_fgate_b", [DEPTH, 4])
    a_nw = din("a_norm_w", [DEPTH, 512])
    cw = din("b_conv_w", [DEPTH, 4, 1024])
    cb = din("b_conv_b", [DEPTH, 1024])
    dtb = din("b_dt_bias", [DEPTH, 8])
    alog = din("b_A_log", [DEPTH, 8])
    dsk = din("b_D", [DEPTH, 8])
    b_nw = din("b_norm_w", [DEPTH, 512])
    qnw = din("c_qnorm_w", [DEPTH, 64])
    knw = din("c_knorm_w", [DEPTH, 64])
    snk = din("c_sinks", [DEPTH, 8])
    cmat = din("cmat", [128, 640])
    rope = din("rope", [T, 128])

    if with_sample:
        csel = din("csel", [128, 1152])
        xs_in = din("xs_in", [16, 1024])
        i_sC = din("sC", [DEPTH, 16, 4, 128, 64])
        i_sn = din("sn", [DEPTH, 16, 4, 64])
        i_sm = din("sm", [DEPTH, 16, 4])
        i_sh = din("sh", [DEPTH, 16, 8, 64, 128])
        i_sconv = din("sconv", [DEPTH, 16, 3, 1024])
        i_ck = din("ck", [DEPTH, 16, 128, 2, 64])
        i_cv = din("cv", [DEPTH, 16, 128, 2, 64])
        o_ys = dout("ys", [16, 1024])
        o_sC = dout("oC", [DEPTH, 16, 4, 128, 64])
        o_sn = dout("on", [DEPTH, 16, 4, 64])
        o_sm = dout("om", [DEPTH, 16, 4])
        o_sh = dout("oh", [DEPTH, 16, 8, 64, 128])
        o_sconv = dout("oconv", [DEPTH, 16, 3, 1024])
        o_sk = dout("ok", [DEPTH, 16, 128, 2, 64])
        o_sv = dout("ov", [DEPTH, 16, 128, 2, 64])
    yp = dout("yp", [T, 1024])
    o_pC = dout("pC", [DEPTH, 4, 128, 64])
    o_pn = dout("pn", [DEPTH, 4, 64])
    o_pm = dout("pm", [DEPTH, 4])
    o_ph = dout("ph", [DEPTH, 8, 64, 128])
    o_pconv = dout("pconv", [DEPTH, 3, 1024])
    o_pk = dout("pk", [DEPTH, 128, 2, 64])
    o_pv = dout("pv", [DEPTH, 128, 2, 64])

    st = contextlib.ExitStack()
    with st:
        def sb(name, shape, dt=F32):
            return st.enter_context(nc.sbuf_tensor(name, shape, dt))

        def ps(name, shape, dt=F32):
            return st.enter_context(nc.psum_tensor(name, shape, dt))

        S = Sched(nc)
        chain_ctx = [None, None]

        def op(eng, fn, reads=(), writes=(), dma=None, cost=None):
            if chain_ctx[0] is None:
                return S.op(eng, fn, reads, writes, dma)
            item = (eng, fn, tuple(reads), tuple(writes), dma, cost)
            if chain_ctx[1] is not None:
                chain_ctx[1].append(item)
            else:
                chain_ctx[0].append([item])
            return None

        def label(name):
            chain_ctx[0].append([("label", name)])

        def wait_label(name):
            chain_ctx[0].append([("wait", name)])

        class atomic:
            def __enter__(self):
                if chain_ctx[0] is not None:
                    chain_ctx[1] = []
            def __exit__(self, *a):
                if chain_ctx[0] is not None:
                    chain_ctx[0].append(chain_ctx[1])
                    chain_ctx[1] = None
                return False

        COST = {"tensor": 0.18, "vector": 0.55, "scalar": 0.5, "gpsimd": 1.4, "sync": 2.0}
        SYNC_LAT = 0.6
        eng_free = {e: 0.0 for e in Sched.ENGS}
        fin = {}

        def est_ready(item):
            eng, fn, r, w, d = item[:5]
            tmax = 0.0
            for k in r:
                x = S.last_w.get(k)
                if x is not None:
                    tmax = max(tmax, fin.get(x, 0.0) + (SYNC_LAT if S.ops[x].eng != eng else 0.1))
            for k in w:
                x = S.last_w.get(k)
                if x is not None:
                    tmax = max(tmax, fin.get(x, 0.0) + (SYNC_LAT if S.ops[x].eng != eng else 0.1))
                for x in S.readers.get(k, ()):
                    tmax = max(tmax, fin.get(x, 0.0) + (SYNC_LAT if S.ops[x].eng != eng else 0.1))
            return max(tmax, eng_free[eng])

        def emit_item(item):
            t0 = est_ready(item)
            o = S.op(*item[:5])
            t1 = t0 + (item[5] if item[5] is not None else COST[item[0]])
            eng_free[item[0]] = t1
            fin[o.idx] = t1

        def run_chains(fns, prio=None):
            prio = prio or [0.0] * len(fns)
            chains = []
            for f in fns:
                chain_ctx[0] = []
                f()
                chains.append(chain_ctx[0])
            chain_ctx[0] = None
            base = max(eng_free.values())
            for e in eng_free:
                eng_free[e] = max(eng_free[e], base - 3.0)
            pos = [0] * len(chains)
            labels = set()
            while True:
                best, bt = None, None
                progressed = False
                for i, c in enumerate(chains):
                    while pos[i] < len(c) and c[pos[i]][0][0] in ("label", "wait"):
                        kind, name = c[pos[i]][0][0], c[pos[i]][0][1]
                        if kind == "label":
                            labels.add(name)
                            pos[i] += 1
                            progressed = True
                        elif name in labels:
                            pos[i] += 1
                            progressed = True
                        else:
                            break
                    if pos[i] < len(c) and c[pos[i]][0][0] not in ("label", "wait"):
                        t_ = est_ready(c[pos[i]][0]) + prio[i]
                        if best is None or t_ < bt:
                            best, bt = i, t_
                if best is None:
                    if progressed:
                        continue
                    assert all(pos[i] >= len(c) for i, c in enumerate(chains)), "chain deadlock"
                    break
                for it in chains[best][pos[best]]:
                    emit_item(it)
                pos[best] += 1

        def dma(out, in_, r, w, sem, eng="sync", slow=False):
            if slow:
                return op(eng, lambda e: e.dma_start(out=out, in_=in_, allow_slow_non_contiguous=True), r, w, dma=sem)
            return op(eng, lambda e: e.dma_start(out=out, in_=in_), r, w, dma=sem)

        def fsz(ap):
            n = 1
            for d_ in ap.shape[1:]:
                n *= d_
            return n

        def mm(out, lhsT, rhs, start, stop, r, w):
            c_ = max(0.06, fsz(out) / 2400.0) * (4.0 if lhsT.dtype == F32 else 1.0)
            return op("tensor", lambda e: e.matmul(out, lhsT=lhsT, rhs=rhs, start=start, stop=stop), r, w, cost=c_)

        def tr(out, in_, ident, r, w):
            return op("tensor", lambda e: e.transpose(out=out, in_=in_, identity=ident), r, w, cost=0.12)

        def act(out, in_, func, r, w, **kw):
            return op("scalar", lambda e: e.activation(out=out, in_=in_, func=func, **kw), r, w, cost=0.2 + fsz(out) / 1200.0)

        def ecost(eng, out):
            return (0.1 + fsz(out) / 900.0) * (2.2 if eng == "gpsimd" else 1.0)

        def tt(out, in0, in1, o, r, w, eng="vector"):
            return op(eng, lambda e: e.tensor_tensor(out=out, in0=in0, in1=in1, op=o), r, w, cost=ecost(eng, out))

        def ts(out, in0, s1, s2, o0, o1, r, w, eng="vector"):
            if s2 is None:
                return op(eng, lambda e: e.tensor_scalar(out=out, in0=in0, scalar1=s1, scalar2=None, op0=o0), r, w, cost=ecost(eng, out))
            return op(eng, lambda e: e.tensor_scalar(out=out, in0=in0, scalar1=s1, scalar2=s2, op0=o0, op1=o1), r, w, cost=ecost(eng, out))

        def stt(out, in0, scalar, in1, o0, o1, r, w):
            return op("vector", lambda e: e.scalar_tensor_tensor(out=out, in0=in0, scalar=scalar, in1=in1, op0=o0, op1=o1), r, w, cost=ecost("vector", out))

        def red(out, in_, o, r, w):
            return op("vector", lambda e: e.tensor_reduce(out=out, in_=in_, axis=AX.X, op=o), r, w, cost=ecost("vector", in_))

        def cp(out, in_, r, w, eng="vector"):
            if eng == "scalar":
                return op("scalar", lambda e: e.copy(out=out, in_=in_), r, w, cost=0.2 + fsz(out) / 1200.0)
            return op(eng, lambda e: e.tensor_copy(out=out, in_=in_), r, w, cost=ecost(eng, out))

        def mset(ap, val, w, eng="gpsimd"):
            return op(eng, lambda e: e.memset(ap, val), (), w)

        def rsq(out, in_, scale, r, w):
            act(out, in_, AF.Ln, r, w, scale=scale, bias=EPS)
            act(out, out, AF.Exp, w, w, scale=-0.5)

        cm = sb("cm", [128, 640])
        dma(cm[:], cmat, (), ["cm"], "cm")
        identf = cm[:, 0:128]
        trif = cm[:, 128:256]
        onesf = cm[:, 384:512]
        negmf = cm[:, 512:640]
        cmb = sb("cmb", [128, 384], BF16)
        cp(cmb[:], cm[:, 0:384], ["cm"], ["cmb"])
        identb = cmb[:, 0:128]
        trib = cmb[:, 128:256]
        ntrib = cmb[:, 256:384]
        nw = sb("nw", [128, DEPTH, 8])
        for l_ in range(DEPTH):
            dma(nw[:, l_, :], norm_w[l_].rearrange("(c p) -> p c", p=128), (), ["nw"], "nw", slow=True)
        ow = sb("ow", [128, DEPTH, 12])
        mset(ow[:], 1.0, ["ow"])
        for l_ in range(DEPTH):
            dma(ow[:, l_, 0:4], a_nw[l_].rearrange("(c p) -> p c", p=128), (), ["ow"], "ow_a", slow=True)
            dma(ow[:, l_, 4:8], b_nw[l_].rearrange("(c p) -> p c", p=128), (), ["ow"], "ow_b", slow=True)

        Wb = sb("Wb", [128, 8, D_IN], BF16)
        Wo = sb("Wo", [128, 12, 1024], BF16)
        wst = [sb("wst%d" % i, [128, 1024]) for i in range(2)]

        qkw_b = sb("qkw_b", [128, 2, 64])
        par8 = sb("par8", [128, 5, 8])
        cwt = sb("cwt", [128, 8, 4])
        cbt = sb("cbt", [128, 8])

        pj = ps("pj", [128, 2, 512])
        ptr = ps("ptr", [128, 8, 128], BF16)
        ptf = ps("ptf", [128, 512])
        pmA = ps("pmA", [128, 2, 512])
        pmB = ps("pmB", [128, 2, 512])

        xt = [sb("xt%d" % i, [128, 1024]) for i in range(2)]
        Eall = sb("Eall", [128, 4096])
        E = [Eall[:, i * 1024:(i + 1) * 1024] for i in range(4)]
        st1 = sb("st1", [128, 16])
        xsb = sb("xsb", [128, 1024], BF16)
        xnT = sb("xnT", [128, 8, 128], BF16)
        utm = sb("utm", [128, 3600])
        qkT = sb("qkT", [64, 8, 128], BF16)
        cvb = sb("cvb", [128, 8, 131])
        cacc = E[0].rearrange("p (c t) -> p c t", c=8)
        ctmp = E[1].rearrange("p (c t) -> p c t", c=8)
        xbcT = sb("xbcT", [128, 8, 128], BF16)
        ycat = sb("ycat", [128, 1536], BF16)
        ycatT = sb("ycatT", [128, 12, 128], BF16)
        ropet = [sb("ropet%d" % i, [128, 128]) for i in range(2)]
        sm8 = sb("sm8", [128, 8, 8])
        Rt = E[0].rearrange("p (c t) -> p c t", c=8)
        dec = E[1].rearrange("p (c t) -> p c t", c=8)
        wm = sb("wm", [128, 8, 128], BF16)
        STm_t = sb("STm", [128, 4, 128], BF16)
        STm = STm_t[:]
        xBtm = sb("xBtm", [128, 6, 128], BF16)
        xdt = sb("xdt", [128, 512], BF16)
        xss = sb("xss", [128, 512], BF16)
        hT = sb("hT", [128, 512])
        hTb = sb("hTb", [128, 512], BF16)
        yb1 = wst[0][:, 0:512]
        yb2 = wst[0][:, 512:1024]
        g4 = sb("g4", [128, 8, 4])
        gT = sb("gT", [4, 8, 128])
        mst = sb("mst", [4, 8])
        dg4 = sb("dg4", [4, 4])
        facb = sb("facb", [64, 4])
        kbf = sb("kbf", [128, 256], BF16)
        vaug = sb("vaug", [128, 4, 132], BF16)
        CnT = sb("CnT", [64, 4, 132])
        CnB = sb("CnB", [64, 4, 132], BF16)
        hb = E[2][:, 0:512]
        hb2 = E[2][:, 512:1024]
        gsig = E[3][:, 0:512]
        gsil = E[3][:, 512:1024]
        qkn = wst[1][:, 0:640]
        qkA_t = sb("qkA", [128, 640], BF16)
        qkA = qkA_t[:]
        qkB_t = sb("qkB", [128, 640], BF16)
        qkB = qkB_t[:]
        qkp = sb("qkp", [128, 640], BF16)
        kvf = wst[1][:, 640:896]
        qTs = sb("qTs", [64, 8, 128], BF16)
        kTs = [sb("kTs%d" % i, [64, 2, 128], BF16) for i in range(2)]
        vsw = [sb("vsw%d" % i, [128, 2, 66], BF16) for i in range(2)]
        pex = [sb("pex%d" % i, [128, 512], BF16) for i in range(2)]
        osw = wst[1][:, 0:512]
        gsil2t = sb("gsil2", [128, 512], BF16)
        gsil2 = gsil2t[:]
        dsk_t = E[0][:, 0:512]
        st10 = sb("st10", [128, 4, 16])
        ost = E[0].rearrange("p (c t) -> p c t", c=8)
        ostm = E[2].rearrange("p (c t) -> p c t", c=8)

        def load_layer(l):
            dma(qkw_b[:, 0, :], qnw[l].partition_broadcast(128), (), ["qkw_b"], "par2")
            dma(qkw_b[:, 1, :], knw[l].partition_broadcast(128), (), ["qkw_b"], "par3")
            dma(par8[:, 0, :], dtb[l].partition_broadcast(128), (), ["par8"], "par4")
            dma(par8[:, 1, :], alog[l].partition_broadcast(128), (), ["par8"], "par5")
            dma(par8[:, 2, :], dsk[l].partition_broadcast(128), (), ["par8"], "par6")
            dma(par8[:, 3, :], snk[l].partition_broadcast(128), (), ["par8"], "par7")
            dma(par8[:, 4, 0:4], a_ib[l].partition_broadcast(128), (), ["par8"], "par8")
            dma(par8[:, 4, 4:8], a_fb[l].partition_broadcast(128), (), ["par8"], "par9")
            act(par8[:, 1, :], par8[:, 1, :], AF.Exp, ["par8"], ["par8"])
            ts(par8[:, 1, :], par8[:, 1, :], -1.0, None, ALU.mult, None, ["par8"], ["par8"])
            act(par8[:, 3, :], par8[:, 3, :], AF.Exp, ["par8"], ["par8"])
            for j in range(4):
                dma(cwt[:, :, j], cw[l, j].rearrange("(c p) -> p c", p=128), (), ["cwt"], "par10", slow=True)
            dma(cbt[:], cb[l].rearrange("(c p) -> p c", p=128), (), ["cbt"], "par11", slow=True)
            i = 0
            for kc in range(8):
                for q4 in range(5):
                    c0 = q4 * 1024
                    w_ = min(1024, D_IN - c0)
                    stg = wst[i % 2]
                    key = "wst%d" % (i % 2)
                    dma(stg[:, 0:w_], w_in[l, kc * 128:(kc + 1) * 128, c0:c0 + w_], (), [key], key)
                    if i % 2:
                        act(Wb[:, kc, c0:c0 + w_], stg[:, 0:w_], AF.Copy, [key, "nw"], ["Wb"], scale=nw[:, l, kc:kc + 1])
                    else:
                        ts(Wb[:, kc, c0:c0 + w_], stg[:, 0:w_], nw[:, l, kc:kc + 1], None, ALU.mult, None, [key, "nw"], ["Wb"])
                    i += 1
            ts(Wb[:, :, 256:512], Wb[:, :, 256:512], 0.125, None, ALU.mult, None, ["Wb"], ["Wb"])
            for kc in range(12):
                stg = wst[i % 2]
                key = "wst%d" % (i % 2)
                dma(stg[:, 0:1024], w_out[l, kc * 128:(kc + 1) * 128, :], (), [key], key)
                if i % 2:
                    act(Wo[:, kc, :], stg[:, 0:1024], AF.Copy, [key, "ow"], ["Wo"], scale=ow[:, l, kc:kc + 1])
                else:
                    ts(Wo[:, kc, :], stg[:, 0:1024], ow[:, l, kc:kc + 1], None, ALU.mult, None, [key, "ow"], ["Wo"])
                i += 1
            mset(CnT[:], 0.0, ["CnT"])
            mset(hT[:], 0.0, ["hT"])
            mset(mst[:], 0.0, ["mst"])
            mset(cvb[:], 0.0, ["cvb"])
            mset(hTb[:], 0.0, ["hTb"])

        def front_a(l, t):
            par = t % 2
            x = xt[par]
            xk = "xt%d" % par
            src = xp if l == 0 else yp
            dma(x[:], src[t * 128:(t + 1) * 128, :], ["yp%d" % t], [xk], xk)
            rp = ropet[par]
            rk = "ropet%d" % par
            dma(rp[:], rope[t * 128:(t + 1) * 128, :], (), [rk], rk)
            act(xsb[:], x[:], AF.Square, [xk], ["xsb", "st1"], scale=1.0 / 32, accum_out=st1[:, 0:1])
            rsq(st1[:, 1:2], st1[:, 0:1], 1.0, ["st1"], ["st1"])
            act(xsb[:], x[:], AF.Copy, [xk, "st1"], ["xsb"], scale=st1[:, 1:2])
            with atomic():
                for c in range(8):
                    tr(ptr[:, c, :], xsb[:, c * 128:(c + 1) * 128], identb, ["xsb", "cmb"], ["ptr"])
                cp(xnT[:], ptr[:], ["ptr"], ["xnT"])

        def proj_utm(l, t, banks):
            chunks = []
            for (c0, n, off) in ((256, 2312, 0), (3592, 1288, 2312)):
                o = 0
                while o < n:
                    w_ = min(512, n - o)
                    chunks.append((c0 + o, w_, off + o))
                    o += w_
            for i, (c0, w_, off) in enumerate(chunks):
                b = banks[i % len(banks)]
                for kc in range(8):
                    mm(pj[:, b, 0:w_], xnT[:, kc, :], Wb[:, kc, c0:c0 + w_], kc == 0, kc == 7, ["xnT", "Wb"], ["pj%d" % b])
                cp(utm[:, off:off + w_], pj[:, b, 0:w_], ["pj%d" % b], ["utm"], eng=("scalar" if i % 2 else "vector"))

        def tile_step(l, t):
            par = t % 2
            x = xt[par]
            xk = "xt%d" % par
            rp = ropet[par]
            rk = "ropet%d" % par
            if t == 0:
                front_a(l, 0)
                proj_utm(l, 0, [0, 1])
            act(gsig, utm[:, 768:1280], AF.Sigmoid, ["utm"], ["E3"])
            act(gsil, utm[:, 1280:1792], AF.Silu, ["utm"], ["E3"])
            act(yb2, utm[:, 1800:2312], AF.Silu, ["utm"], ["wst0g", "wst0"])
            act(gsil2, utm[:, 3088:3600], AF.Silu, ["utm"], ["gsil2"])
            tt(gsig, gsig, gsil, ALU.mult, ["E3", "E3"], ["E3"], eng="gpsimd")
            for c in range(8):
                for kc in range(8):
                    mm(pmB[:, c // 4, (c % 4) * 128:(c % 4 + 1) * 128], Wb[:, kc, 2568 + c * 128:2568 + (c + 1) * 128], xnT[:, kc, :],
                       kc == 0, kc == 7, ["xnT", "Wb"], ["pmB%d" % (c // 4)])
            cp(cvb[:, :, 3:131], pmB[:].rearrange("p b (c t) -> p (b c) t", c=4), ["pmB0", "pmB1"], ["cvb"])
            for c in range(8):
                for kc in range(8):
                    mm(pmA[0:64, c // 4, (c % 4) * 128:(c % 4 + 1) * 128], Wb[:, kc, c * 64:(c + 1) * 64], xnT[:, kc, :], kc == 0, kc == 7,
                       ["xnT", "Wb"], ["pmA%d" % (c // 4)])
            cp(qkT[:], pmA[0:64, :, :].rearrange("p b (c t) -> p (b c) t", c=4), ["pmA0", "pmA1"], ["qkT"], eng="scalar")

            fns = [lambda: mlstm_tile(l, t), lambda: ssd_tile(l, t), lambda: swa_tile(l, t, rp, rk), lambda: ssd_a(l, t)]
            def tail_chain():
                if t > 0:
                    outproj_mm(l, t - 1)
                if t + 1 < NT:
                    front_a(l, t + 1)
                    for nm_ in ("utm_m", "utm_a", "utm_s"):
                        wait_label(nm_)
                    proj_utm(l, t + 1, [1])
            fns.append(tail_chain)
            run_chains(fns)
            for rnd, (k0, k1) in enumerate(((0, 8), (8, 12))):
                for kc in range(k0, k1):
                    tr(ptr[:, kc - k0, :], ycat[:, kc * 128:(kc + 1) * 128], identb, ["ycat0", "ycat1", "ycat2", "cmb"], ["ptr"])
                cp(ycatT[:, k0:k1, :], ptr[:, 0:k1 - k0, :], ["ptr"], ["ycatT"], eng=("vector" if rnd else "scalar"))
            if t == NT - 1:
                outproj_mm(l, t)

        def outproj_mm(l, t):
            par = t % 2
            x = xt[par]
            xk = "xt%d" % par
            for n in range(2):
                for kc in range(12):
                    mm(pj[:, 1, :], ycatT[:, kc, :], Wo[:, kc, n * 512:(n + 1) * 512], kc == 0, kc == 11, ["ycatT", "Wo"], ["pj1"])
                tt(x[:, n * 512:(n + 1) * 512], pj[:, 1, :], x[:, n * 512:(n + 1) * 512], ALU.add, ["pj1", xk], [xk])
            dma(yp[t * 128:(t + 1) * 128, :], x[:], [xk], ["yp%d" % t], "ypst%d" % par)

        def mlstm_tile(l, t):
            last = (t == NT - 1)
            tt(g4[:, 0, :], utm[:, 1792:1796], par8[:, 4, 0:4], ALU.add, ["utm", "par8"], ["g4"])
            tt(g4[:, 1, :], utm[:, 1796:1800], par8[:, 4, 4:8], ALU.add, ["utm", "par8"], ["g4"])
            act(g4[:, 2, :], g4[:, 1, :], AF.Exp, ["g4"], ["g4"], scale=-1.0)
            act(g4[:, 2, :], g4[:, 2, :], AF.Ln, ["g4"], ["g4"], bias=1.0)
            mm(ptf[:, 0:4], trif, g4[:, 2, :], True, True, ["cm", "g4"], ["ptf"])
            cp(g4[:, 3, :], ptf[:, 0:4], ["ptf"], ["g4"])
            tt(g4[:, 4, :], g4[:, 0, :], g4[:, 3, :], ALU.add, ["g4"], ["g4"])
            tr(ptf[0:4, 0:128], g4[:, 4, :], identf, ["g4", "cm"], ["ptf"])
            tr(ptf[0:4, 128:256], g4[:, 3, :], identf, ["g4", "cm"], ["ptf"])
            cp(gT[:, 0, :], ptf[0:4, 0:128], ["ptf"], ["gT"])
            cp(gT[:, 1, :], ptf[0:4, 128:256], ["ptf"], ["gT"])
            red(mst[:, 1:2], gT[:, 0, :], ALU.max, ["gT"], ["mst"])
            tt(mst[:, 2:3], mst[:, 1:2], mst[:, 0:1], ALU.max, ["mst"], ["mst"])
            ts(mst[:, 3:4], mst[:, 2:3], -1.0, None, ALU.mult, None, ["mst"], ["mst"])
            tt(mst[:, 5:6], mst[:, 0:1], mst[:, 2:3], ALU.subtract, ["mst"], ["mst"])
            act(mst[:, 4:5], mst[:, 5:6], AF.Exp, ["mst"], ["mst"])
            act(gT[:, 2, :], gT[:, 0, :], AF.Exp, ["gT", "mst"], ["gT"], bias=mst[:, 3:4])
            act(gT[:, 3, :], gT[:, 1, :], AF.Exp, ["gT", "mst"], ["gT"], bias=mst[:, 3:4])
            tt(mst[:, 0:1], mst[:, 2:3], gT[:, 1, 127:128], ALU.subtract, ["mst", "gT"], ["mst"])
            tr(ptf[:, 256:260], gT[:, 2, :], identf[0:4, 0:4], ["gT", "cm"], ["ptf"])
            tr(ptf[:, 260:264], gT[:, 3, :], identf[0:4, 0:4], ["gT", "cm"], ["ptf"])
            cp(g4[:, 5:7, :], ptf[:, 256:264].rearrange("p (a h) -> p a h", a=2), ["ptf"], ["g4"])
            ts(dg4[:], identf[0:4, 0:4], mst[:, 4:5], None, ALU.mult, None, ["cm", "mst"], ["dg4"])
            mm(ptf[0:64, 264:268], onesf[0:4, 0:64], dg4[:], True, True, ["cm", "dg4"], ["ptf"])
            cp(facb[:], ptf[0:64, 264:268], ["ptf"], ["facb"])
            tt(CnT[:, :, 0:129], CnT[:, :, 0:129], facb[:].unsqueeze(2).to_broadcast([64, 4, 129]), ALU.mult, ["CnT", "facb"], ["CnT"])
            cp(CnB[:], CnT[:], ["CnT"], ["CnB"], eng="scalar")
            tt(vaug[:, :, 0:128], utm[:, 256:768].rearrange("p (h v) -> p h v", h=4), g4[:, 5, :].unsqueeze(2).to_broadcast([128, 4, 128]),
               ALU.mult, ["utm", "g4"], ["vaug"])
            cp(vaug[:, :, 128:129], g4[:, 5, :].unsqueeze(2), ["g4"], ["vaug"], eng="gpsimd")
            cp(kbf[:], utm[:, 0:256], ["utm"], ["kbf"], eng="scalar")
            label("utm_m")
            for h in range(4):
                mm(pmA[:, 1, h * 128:(h + 1) * 128], qkT[:, 4 + h, :], qkT[:, h, :], True, True, ["qkT"], ["pmA1"])
            tt(STm, pmA[:, 1, :].rearrange("p (h t) -> p h t", h=4), trif.unsqueeze(1).to_broadcast([128, 4, 128]), ALU.mult,
               ["pmA1", "cm"], ["STm"])
            for h in range(4):
                p0 = (h % 2) * 64
                o_ = pmA[:, h // 2, (h % 2) * 256:(h % 2) * 256 + 129]
                mm(o_, STm[:, h, :], vaug[:, h, 0:129], True, False, ["STm", "vaug"], ["pmA%d" % (h // 2)])
                mm(o_, qkT[:, h, :], CnB[:, h, 0:129], False, True, ["qkT", "CnB"], ["pmA%d" % (h // 2)])
            pn4 = pmA[:].rearrange("p b (h c) -> p (b h) c", h=2)
            act(g4[:, 7, :], pn4[:, :, 128], AF.Abs, ["pmA0", "pmA1"], ["g4"])
            tt(g4[:, 7, :], g4[:, 7, :], g4[:, 6, :], ALU.max, ["g4"], ["g4"])
            op("vector", lambda e: e.reciprocal(out=g4[:, 7, :], in_=g4[:, 7, :]), ["g4"], ["g4"])
            tt(hb.rearrange("p (h v) -> p h v", h=4), pn4[:, :, 0:128], g4[:, 7, :].unsqueeze(2).to_broadcast([128, 4, 128]), ALU.mult,
               ["pmA0", "pmA1", "g4"], ["E2"])
            for h in range(4):
                p0 = (h % 2) * 64
                mm(pmA[0:64, h // 2, (h % 2) * 256:(h % 2) * 256 + 129], kbf[:, h * 64:(h + 1) * 64], vaug[:, h, 0:129], True, True,
                   ["kbf", "vaug"], ["pmA%d" % (h // 2)])
            for h in range(4):
                p0 = (h % 2) * 64
                tt(CnT[:, h, 0:129], CnT[:, h, 0:129], pmA[0:64, h // 2, (h % 2) * 256:(h % 2) * 256 + 129], ALU.add,
                   ["CnT", "pmA%d" % (h // 2)], ["CnT"])
            for h in range(4):
                act(hb2[:, h * 128:(h + 1) * 128], hb[:, h * 128:(h + 1) * 128], AF.Square, ["E2"], ["E2", "st10a"], accum_out=st10[:, 0, h:h + 1])
            rsq(st10[:, 0, 4:8], st10[:, 0, 0:4], 1.0 / 128, ["st10a"], ["st10a"])
            tt(hb.rearrange("p (h v) -> p h v", h=4), hb.rearrange("p (h v) -> p h v", h=4),
               st10[:, 0, 4:8].unsqueeze(2).to_broadcast([128, 4, 128]), ALU.mult, ["E2", "st10a"], ["E2"])
            tt(ycat[:, 0:512], hb, gsig, ALU.mult, ["E2", "E3"], ["ycat0"])
            if last:
                for h in range(4):
                    p0 = (h % 2) * 64
                    tr(ptf[:, h * 64:(h + 1) * 64], CnT[:, h, 0:128], identf[0:64, 0:64], ["CnT", "cm"], ["ptf"])
                cp(ostm[:, 0:2, :], ptf[:, 0:256].rearrange("p (a b) -> p a b", a=2), ["ptf"], ["E2"])
                dma(o_pC[l].rearrange("h v k -> v h k"), ostm[:, 0:2, :].rearrange("p a (c k) -> p (a c) k", c=2), ["E2"], [], "o_pC")
                for h in range(4):
                    p0 = (h % 2) * 64
                    dma(o_pn[l, h].unsqueeze(1), CnT[:, h, 128:129], ["CnT"], [], "o_pn", slow=True)
                dma(o_pm[l].unsqueeze(1), mst[:, 0:1], ["mst"], [], "o_pm", slow=True)

        def ssd_a(l, t):
            RtH = wst[0][:, 0:512]
            tt(sm8[:, 0, :], utm[:, 2312:2320], par8[:, 0, :], ALU.add, ["utm", "par8"], ["sm8"])
            label("utm_a")
            act(sm8[:, 0, :], sm8[:, 0, :], AF.Exp, ["sm8"], ["sm8"])
            act(sm8[:, 0, :], sm8[:, 0, :], AF.Ln, ["sm8"], ["sm8"], bias=1.0)
            tt(sm8[:, 1, :], sm8[:, 0, :], par8[:, 1, :], ALU.mult, ["sm8", "par8"], ["sm8"])
            mm(pmB[:, 1, 0:8], trif, sm8[:, 1, :], True, True, ["cm", "sm8"], ["pmB1"])
            ts(sm8[:, 2, :], pmB[:, 1, 0:8], -1.0, None, ALU.mult, None, ["pmB1"], ["sm8"])
            act(sm8[:, 3, :], pmB[:, 1, 0:8], AF.Exp, ["pmB1"], ["sm8"])
            for b in range(2):
                tt(RtH.rearrange("p (h t) -> p h t", h=4), trif.unsqueeze(1).to_broadcast([128, 4, 128]),
                   sm8[:, 1, 4 * b:4 * b + 4].unsqueeze(2).to_broadcast([128, 4, 128]), ALU.mult, ["cm", "sm8"], ["wst0"])
                mm(pmB[:, b, :], onesf, RtH, True, False, ["cm", "wst0"], ["pmB%d" % b])
                for h in range(4):
                    mm(pmB[:, b, h * 128:(h + 1) * 128], identf, negmf, False, h == 3, ["cm"], ["pmB%d" % b])
            pa8 = pmB[:].rearrange("p b (h t) -> p (b h) t", h=4)
            for h in range(8):
                act(dec[:, h, :], pa8[:, h, :], AF.Exp, ["pmB0", "pmB1", "sm8"], ["E1"], bias=sm8[:, 2, h:h + 1])
            act(sm8[:, 4, :], pa8[:, :, 127], AF.Exp, ["pmB0", "pmB1"], ["sm8"])
            label("dec_ready")

        def ssd_tile(l, t):
            last = (t == NT - 1)
            ck_ = ["E0c%d" % c for c in range(8)]
            for c in range(8):
                act(cacc[:, c, :], cvb[:, c, 3:131], AF.Identity, ["cvb", "cwt", "cbt", "E0"], [ck_[c]], scale=cwt[:, c, 3:4], bias=cbt[:, c:c + 1])
            for j in (2, 1, 0):
                for c in range(8):
                    stt(cacc[:, c, :], cvb[:, c, j:j + 128], cwt[:, c, j:j + 1], cacc[:, c, :], ALU.mult, ALU.add, ["cvb", "cwt", ck_[c]], [ck_[c]])
            act(xbcT[:], cacc, AF.Silu, ck_, ["xbcT", "E0"])
            if last:
                for r_ in range(3):
                    dma(o_pconv[l, r_].rearrange("(c p) -> p c", p=128), cvb[:, :, 128 + r_], ["cvb"], [], "o_pconv", slow=True)
            cp(cvb[:, :, 0:3], cvb[:, :, 128:131], ["cvb"], ["cvb"], eng="gpsimd")
            with atomic():
                for c in range(6):
                    tr(ptr[:, c, :], xbcT[:, c, :], identb, ["xbcT", "cmb"], ["ptr"])
                cp(xBtm[:], ptr[:, 0:6, :], ["ptr"], ["xBtm"], eng="scalar")
            xtm = xBtm[:, 0:4, :].rearrange("p c (e d) -> p (c e) d", e=2)
            wait_label("dec_ready")
            tt(xdt[:].rearrange("p (h d) -> p h d", h=8), xtm, sm8[:, 0, :].unsqueeze(2).to_broadcast([128, 8, 64]), ALU.mult,
               ["xBtm", "sm8"], ["xdt"])
            tt(xss[:].rearrange("p (h d) -> p h d", h=8), xdt[:].rearrange("p (h d) -> p h d", h=8),
               dec[:, :, 127].unsqueeze(2).to_broadcast([128, 8, 64]), ALU.mult, ["xdt", "E1"], ["xss"], eng="gpsimd")
            for g in range(2):
                mm(pmB[:, 0, g * 128:(g + 1) * 128], xbcT[:, 4 + g, :], xbcT[:, 6 + g, :], True, True, ["xbcT"], ["pmB0"])
            for g in range(2):
                tt(wm[:, g * 4:(g + 1) * 4, :], dec[:, g * 4:(g + 1) * 4, :], pmB[:, 0, g * 128:(g + 1) * 128].unsqueeze(1).to_broadcast([128, 4, 128]),
                   ALU.mult, ["E1", "pmB0"], ["wm"])
            for h in range(8):
                mm(pmB[:, 1, h * 64:(h + 1) * 64], wm[:, h, :], xdt[:, h * 64:(h + 1) * 64], True, True, ["wm", "xdt"], ["pmB1"])
            for g in range(2):
                mm(pmB[:, 0, g * 256:(g + 1) * 256], xbcT[:, 6 + g, :], hTb[:, g * 256:(g + 1) * 256], True, True, ["xbcT", "hTb"], ["pmB0"])
            tt(yb1.rearrange("p (h d) -> p h d", h=8), pmB[:, 0, :].rearrange("p (h d) -> p h d", h=8),
               sm8[:, 3, :].unsqueeze(2).to_broadcast([128, 8, 64]), ALU.mult, ["pmB0", "sm8"], ["wst0"])
            tt(yb1, yb1, pmB[:, 1, :], ALU.add, ["wst0", "pmB1"], ["wst0"])
            tt(dsk_t.rearrange("p (h d) -> p h d", h=8), xtm, par8[:, 2, :].unsqueeze(2).to_broadcast([128, 8, 64]), ALU.mult,
               ["xBtm", "par8"], ["E0"], eng="gpsimd")
            tt(yb1, yb1, dsk_t, ALU.add, ["wst0", "E0"], ["wst0"])
            tt(yb1, yb1, yb2, ALU.mult, ["wst0", "wst0g"], ["wst0"])
            for g in range(2):
                act(cacc[:, g, :], yb1[:, g * 256:(g + 1) * 256].rearrange("p (a b) -> p a b", a=2)[:, 0, :], AF.Square, ["wst0"], ["E0", "st10b"],
                    accum_out=st10[:, 1, 4 + g:5 + g])
                act(cacc[:, 2 + g, :], yb1[:, g * 256:(g + 1) * 256].rearrange("p (a b) -> p a b", a=2)[:, 1, :], AF.Square, ["wst0"], ["E0", "st10b"],
                    accum_out=st10[:, 1, 6 + g:7 + g])
            tt(st10[:, 1, 0:2], st10[:, 1, 4:6], st10[:, 1, 6:8], ALU.add, ["st10b"], ["st10b"])
            rsq(st10[:, 1, 2:4], st10[:, 1, 0:2], 1.0 / 256, ["st10b"], ["st10b"])
            for g in range(2):
                ts(ycat[:, 512 + g * 256:512 + (g + 1) * 256], yb1[:, g * 256:(g + 1) * 256], st10[:, 1, 2 + g:3 + g], None, ALU.mult, None,
                   ["wst0", "st10b"], ["ycat1"])
            for g in range(2):
                mm(pmB[:, 0, g * 256:(g + 1) * 256], xBtm[:, 4 + g, :], xss[:, g * 256:(g + 1) * 256], True, True, ["xBtm", "xss"], ["pmB0"])
            tt(hT[:].rearrange("p (h d) -> p h d", h=8), hT[:].rearrange("p (h d) -> p h d", h=8),
               sm8[:, 4, :].unsqueeze(2).to_broadcast([128, 8, 64]), ALU.mult, ["hT", "sm8"], ["hT"], eng="gpsimd")
            tt(hT[:], hT[:], pmB[:, 0, :], ALU.add, ["hT", "pmB0"], ["hT"])
            cp(hTb[:], hT[:], ["hT"], ["hTb"], eng="scalar")
            if last:
                for h in range(8):
                    tr(pmB[0:64, h // 4, (h % 4) * 128:(h % 4 + 1) * 128], hT[:, h * 64:(h + 1) * 64], identf, ["hT", "cm"], ["pmB%d" % (h // 4)])
                cp(ost[0:64, :, :], pmB[0:64, :, :].rearrange("p b (h s) -> p (b h) s", h=4), ["pmB0", "pmB1"], ["E0"])
                dma(o_ph[l].rearrange("h p s -> p h s"), ost[0:64, :, :], ["E0"], [], "o_ph")

        def swa_tile(l, t, rp, rk):
            last = (t == NT - 1)
            par = t % 2
            qkraw = utm[:, 2320:2960].rearrange("p (h d) -> p h d", h=10)
            tt(qkA, utm[:, 2320:2960], utm[:, 2320:2960], ALU.mult, ["utm"], ["qkA"], eng="gpsimd")
            red(st10[:, 2, 0:10], qkA.rearrange("p (h d) -> p h d", h=10), ALU.add, ["qkA"], ["st10c"])
            rsq(st10[:, 3, 0:10], st10[:, 2, 0:10], 1.0 / 64, ["st10c"], ["st10c"])
            qkn3 = qkn.rearrange("p (h d) -> p h d", h=10)
            tt(qkn3, qkraw, st10[:, 3, 0:10].unsqueeze(2).to_broadcast([128, 10, 64]), ALU.mult, ["utm", "st10c"], ["wst1"])
            vs = vsw[par]
            vk = "vsw%d" % par
            cp(vs[:, :, 0:64], utm[:, 2960:3088].rearrange("p (g d) -> p g d", g=2), ["utm"], [vk], eng="scalar")
            mset(vs[:, :, 64:65], 1.0, [vk])
            if last:
                cp(kvf[:, 128:256], utm[:, 2960:3088], ["utm"], ["wst1"], eng="gpsimd")
            label("utm_s")
            tt(qkn3[:, 0:8, :], qkn3[:, 0:8, :], qkw_b[:, 0, :].unsqueeze(1).to_broadcast([128, 8, 64]), ALU.mult, ["wst1", "qkw_b"], ["wst1"], eng="gpsimd")
            tt(qkn3[:, 8:10, :], qkn3[:, 8:10, :], qkw_b[:, 1, :].unsqueeze(1).to_broadcast([128, 2, 64]), ALU.mult, ["wst1", "qkw_b"], ["wst1"], eng="gpsimd")
            qkA3 = qkA.rearrange("p (h d) -> p h d", h=10)
            qkB3 = qkB.rearrange("p (h d) -> p h d", h=10)
            tt(qkA3, qkn3, rp[:, 0:64].unsqueeze(1).to_broadcast([128, 10, 64]), ALU.mult, ["wst1", rk], ["qkA"])
            tt(qkB3[:, :, 0:32], qkn3[:, :, 32:64], rp[:, 64:96].unsqueeze(1).to_broadcast([128, 10, 32]), ALU.mult, ["wst1", rk], ["qkB"], eng="gpsimd")
            tt(qkB3[:, :, 32:64], qkn3[:, :, 0:32], rp[:, 96:128].unsqueeze(1).to_broadcast([128, 10, 32]), ALU.mult, ["wst1", rk], ["qkB"], eng="gpsimd")
            tt(qkp[:], qkA, qkB, ALU.add, ["qkA", "qkB"], ["qkp"])
            if last:
                kt_ = wst[1][:, 896:1024].rearrange("p (h d) -> p h d", h=2)
                kk_ = qkn3[:, 8:10, :]
                ko_ = kvf[:, 0:128].rearrange("p (h d) -> p h d", h=2)
                tt(ko_, kk_, rp[:, 0:64].unsqueeze(1).to_broadcast([128, 2, 64]), ALU.mult, ["wst1", rk], ["wst1"])
                tt(kt_[:, :, 0:32], kk_[:, :, 32:64], rp[:, 64:96].unsqueeze(1).to_broadcast([128, 2, 32]), ALU.mult, ["wst1", rk], ["wst1"])
                tt(kt_[:, :, 32:64], kk_[:, :, 0:32], rp[:, 96:128].unsqueeze(1).to_broadcast([128, 2, 32]), ALU.mult, ["wst1", rk], ["wst1"])
                tt(ko_, ko_, kt_, ALU.add, ["wst1"], ["wst1"])
                dma(o_pk[l].rearrange("t g d -> t (g d)"), kvf[:, 0:128], ["wst1"], [], "o_pk")
                dma(o_pv[l].rearrange("t g d -> t (g d)"), kvf[:, 128:256], ["wst1"], [], "o_pv")
            kT = kTs[par]
            kTk = "kTs%d" % par
            with atomic():
                for c in range(8):
                    tr(ptr[0:64, c, :], qkp[:, c * 64:(c + 1) * 64], identb, ["qkp", "cmb"], ["ptr"])
                cp(qTs[:], ptr[0:64, :, :], ["ptr"], ["qkT2"])
            with atomic():
                for c in range(2):
                    tr(ptr[0:64, c, :], qkp[:, 512 + c * 64:512 + (c + 1) * 64], identb, ["qkp", "cmb"], ["ptr"])
                cp(kT[:], ptr[0:64, 0:2, :], ["ptr"], [kTk], eng="scalar")
            blocks = [(kTs[1 - par], "kTs%d" % (1 - par), vsw[1 - par], "vsw%d" % (1 - par), ntrib)] if t > 0 else []
            blocks.append((kT, kTk, vs, vk, trib))
            nb = len(blocks)
            for g in range(2):
                for bi, (kT_, kTk_, v_, vk_, msk) in enumerate(blocks):
                    bk = 0
                    mm(pj[:, bk, :], kT_[:, g, :], qTs[:, 4 * g:4 * g + 4, :].rearrange("p h t -> p (h t)"), True, True,
                       [kTk_, "qkT2"], ["pj%d" % bk])
                    act(pex[bi][:], pj[:, bk, :], AF.Exp, ["pj%d" % bk], ["pex%d" % bi], scale=0.125)
                    tt(pex[bi][:].rearrange("p (h t) -> p h t", h=4), pex[bi][:].rearrange("p (h t) -> p h t", h=4),
                       msk.unsqueeze(1).to_broadcast([128, 4, 128]), ALU.mult, ["pex%d" % bi, "cmb"], ["pex%d" % bi],
                       eng=("gpsimd" if bi else "vector"))
                for j in range(4):
                    for bi, (kT_, kTk_, v_, vk_, msk) in enumerate(blocks):
                        mm(pj[:, 0, j * 128:j * 128 + 65], pex[bi][:, j * 128:(j + 1) * 128], v_[:, g, 0:65], bi == 0, bi == nb - 1,
                           ["pex%d" % bi, vk_], ["pj0"])
                po = pj[:, 0, :].rearrange("p (h c) -> p h c", h=4)
                tt(st10[:, 2, 4 * g:4 * g + 4], po[:, :, 64], par8[:, 3, 4 * g:4 * g + 4], ALU.add, ["pj0", "par8"], ["st10c"])
                op("vector", lambda e, g=g: e.reciprocal(out=st10[:, 2, 4 * g:4 * g + 4], in_=st10[:, 2, 4 * g:4 * g + 4]), ["st10c"], ["st10c"])
                tt(osw[:, g * 256:(g + 1) * 256].rearrange("p (h d) -> p h d", h=4), po[:, :, 0:64],
                   st10[:, 2, 4 * g:4 * g + 4].unsqueeze(2).to_broadcast([128, 4, 64]), ALU.mult, ["pj0", "st10c"], ["wst1"])
            tt(ycat[:, 1024:1536], osw, gsil2, ALU.mult, ["wst1", "gsil2"], ["ycat2"])

        if with_sample:
            cst = sb("cst", [128, 1152])
            dma(cst[:], csel, (), ["cst"], "cst")

            class _Sel8:
                def __getitem__(self, key):
                    j = key[1]
                    return cst[0:16, 7 - j:135 - j]
            Sel8 = _Sel8()
            Sel8T = cst[:, 136:264].rearrange("p (j b) -> p j b", j=8)
            Sel2 = cst[0:64, 264:392]
            SelP4 = cst[0:4, 392:520]
            Pair = cst[:, 520:648]
            Quad = cst[:, 648:776]
            Sel2T = cst[:, 776:840]
            M8 = cst[:, 840:848]
            ropeS = cst[0:16, 1024:1152]
            xs_t = sb("xs_t", [16, 1024])
            dma(xs_t[:], xs_in, (), ["xs_t"], "xs_t")
            xnTs = xnT[:, :, 0:16]
            mA = wst[0][:, 0:392]
            mB = wst[0][:, 392:464]
            mP = sb("mP", [128, 8])
            ms = sb("ms", [128, 16])
            mn1 = wst[0][:, 464:536]
            t64 = wst[1][:, 256:320]
            mh = wst[1][:, 320:384]
            g1 = wst[1][:, 384:448]
            g2 = wst[1][:, 448:512]
            nm64 = wst[1][0:64, 512:584]
            par4 = sb("par4", [4, 2])
            sd = sb("sd", [16, 16])
            fmc = xt[1][:, 0:512].rearrange("p (r c b) -> p r c b", r=4, c=8)
            fma = wst[1][:, 0:128].rearrange("p (c b) -> p c b", c=8)
            fmb = wst[1][:, 128:256].rearrange("p (c b) -> p c b", c=8)
            sTt = wst[0][:, 664:792]
            oTs = wst[0][0:64, 536:664]
            qm = xt[1][0:16, 0:512]
            ycs = ycat[0:16, :]
            ycTs = ycatT[:, :, 0:16]
            sb_qm2 = xt[1][0:16, 512:1024]
            sTmp = xt[0][:, 0:512]
            sA = Eall[:, 0:2048]
            sB = Eall[:, 2048:4096]
            KA = ["E0", "E1"]
            KB = ["E2", "E3"]

        def sample_layer(l):
            u = utm[0:16, :]
            uq = xt[1][0:16, 0:256]
            ubx = xt[0][0:16, :]
            act(xsb[0:16, :], xs_t[:], AF.Square, ["xs_t"], ["xsb", "st1"], scale=1.0 / 32, accum_out=st1[0:16, 0:1])
            rsq(st1[0:16, 1:2], st1[0:16, 0:1], 1.0, ["st1"], ["st1"])
            act(xsb[0:16, :], xs_t[:], AF.Copy, ["xs_t", "st1"], ["xsb"], scale=st1[0:16, 1:2])
            for c in range(8):
                tr(ptr[:, c, 0:16], xsb[0:16, c * 128:(c + 1) * 128], identb[0:16, 0:16], ["xsb", "cmb"], ["ptr"])
            cp(xnTs, ptr[:, :, 0:16], ["ptr"], ["xnT"])
            plan = []
            for (c0, n, dst, dk, off) in ((0, 256, uq, "xt1", 0), (256, 2312, u, "utm", 0), (2568, 1024, ubx, "xt0", 0), (3592, 1288, u, "utm", 2312)):
                o = 0
                while o < n:
                    w_ = min(512, n - o)
                    plan.append((c0 + o, w_, dst, dk, off + o))
                    o += w_
            for i, (c0, w_, dst, dk, off) in enumerate(plan):
                b = i % 2
                for kc in range(8):
                    mm(pj[0:16, b, 0:w_], xnTs[:, kc, :], Wb[:, kc, c0:c0 + w_], kc == 0, kc == 7, ["xnT", "Wb"], ["pj%d" % b])
                cp(dst[:, off:off + w_], pj[0:16, b, 0:w_], ["pj%d" % b], [dk], eng=("scalar" if i % 2 else "vector"))

            def expand(dst_ps, pkey, srcs, rkeys):
                for j in range(8):
                    mm(dst_ps, Sel8[:, j, :], srcs[j], j == 0, j == 7, ["cst"] + rkeys, [pkey])

            hv = [(j // 2, j % 2) for j in range(8)]
            expand(pmA[:, 0, 0:64], "pmA0", [uq[:, h * 64:(h + 1) * 64] for h, v in hv], ["xt1"])
            expand(pmA[:, 0, 64:128], "pmA0", [u[:, h * 64:(h + 1) * 64] for h, v in hv], ["utm"])
            expand(pmA[:, 0, 128:192], "pmA0", [u[:, 256 + h * 128 + v * 64:256 + h * 128 + v * 64 + 64] for h, v in hv], ["utm"])
            expand(pmA[:, 0, 192:256], "pmA0", [u[:, 768 + h * 128 + v * 64:768 + h * 128 + v * 64 + 64] for h, v in hv], ["utm"])
            expand(pmA[:, 0, 256:320], "pmA0", [u[:, 1280 + h * 128 + v * 64:1280 + h * 128 + v * 64 + 64] for h, v in hv], ["utm"])
            expand(pmA[:, 0, 320:321], "pmA0", [u[:, 1792 + h:1793 + h] for h, v in hv], ["utm"])
            expand(pmA[:, 0, 321:322], "pmA0", [u[:, 1796 + h:1797 + h] for h, v in hv], ["utm"])
            cp(mA[:, 0:322], pmA[:, 0, 0:322], ["pmA0"], ["wst0"])
            q_, k_, v_ = mA[:, 0:64], mA[:, 64:128], mA[:, 128:192]
            dma(nm64[:, 0:64], i_sn[l].rearrange("b h k -> (b h) k"), (), ["wst1"], "nm64a")
            dma(nm64[:, 64:65], i_sm[l].rearrange("b (h o) -> (b h) o", o=1), (), ["wst1"], "nm64b", slow=True)
            mm(pmA[:, 1, 0:65], Sel2, nm64[:, 0:65], True, True, ["cst", "wst1"], ["pmA1"])
            dma(par4[:, 0:1], a_ib[l].rearrange("(h o) -> h o", o=1), (), ["par4"], "par4a", slow=True)
            dma(par4[:, 1:2], a_fb[l].rearrange("(h o) -> h o", o=1), (), ["par4"], "par4b", slow=True)
            mm(pmA[:, 1, 128:130], SelP4, par4[:], True, True, ["cst", "par4"], ["pmA1"])
            cp(mB[:, 0:65], pmA[:, 1, 0:65], ["pmA1"], ["wst0"])
            cp(mP[:, 0:2], pmA[:, 1, 128:130], ["pmA1"], ["mP"], eng="scalar")
            tt(ms[:, 0:1], mA[:, 321:322], mP[:, 1:2], ALU.add, ["wst0", "mP"], ["ms"])
            act(ms[:, 1:2], ms[:, 0:1], AF.Exp, ["ms"], ["ms"], scale=-1.0)
            act(ms[:, 1:2], ms[:, 1:2], AF.Ln, ["ms"], ["ms"], bias=1.0)
            tt(ms[:, 2:3], mB[:, 64:65], ms[:, 1:2], ALU.subtract, ["wst0", "ms"], ["ms"])
            tt(ms[:, 3:4], mA[:, 320:321], mP[:, 0:1], ALU.add, ["wst0", "mP"], ["ms"])
            tt(ms[:, 4:5], ms[:, 2:3], ms[:, 3:4], ALU.max, ["ms"], ["ms"])
            tt(ms[:, 5:6], ms[:, 2:3], ms[:, 4:5], ALU.subtract, ["ms"], ["ms"])
            act(ms[:, 5:6], ms[:, 5:6], AF.Exp, ["ms"], ["ms"])
            tt(ms[:, 6:7], ms[:, 3:4], ms[:, 4:5], ALU.subtract, ["ms"], ["ms"])
            act(ms[:, 6:7], ms[:, 6:7], AF.Exp, ["ms"], ["ms"])
            act(ms[:, 7:8], ms[:, 4:5], AF.Exp, ["ms"], ["ms"], scale=-1.0)
            ts(mn1[:, 0:64], mB[:, 0:64], ms[:, 5:6], None, ALU.mult, None, ["wst0", "ms"], ["wst0"])
            stt(mn1[:, 0:64], k_, ms[:, 6:7], mn1[:, 0:64], ALU.mult, ALU.add, ["wst0", "ms", "wst0"], ["wst0"])
            cp(mn1[:, 64:65], ms[:, 4:5], ["ms"], ["wst0"])
            tt(t64, mn1[:, 0:64], q_, ALU.mult, ["wst0", "wst0"], ["wst1"])
            red(ms[:, 8:9], t64, ALU.add, ["wst1"], ["ms"])
            act(ms[:, 8:9], ms[:, 8:9], AF.Abs, ["ms"], ["ms"])
            tt(ms[:, 8:9], ms[:, 8:9], ms[:, 7:8], ALU.max, ["ms"], ["ms"])
            op("vector", lambda e: e.reciprocal(out=ms[:, 8:9], in_=ms[:, 8:9]), ["ms"], ["ms"])
            Cin = i_sC[l].rearrange("b h (vh v) k -> (b h vh) (v k)", vh=2)
            Cout = o_sC[l].rearrange("b h (vh v) k -> (b h vh) (v k)", vh=2)
            for r in range(2):
                dma(sA, Cin[:, r * 2048:(r + 1) * 2048], (), KA, "sA")
                tt(sB.rearrange("p (v k) -> p v k", v=32), v_[:, r * 32:(r + 1) * 32].unsqueeze(2).to_broadcast([128, 32, 64]),
                   k_.unsqueeze(1).to_broadcast([128, 32, 64]), ALU.mult, ["wst0"], KB)
                ts(sA, sA, ms[:, 5:6], None, ALU.mult, None, KA + ["ms"], KA)
                stt(sA, sB, ms[:, 6:7], sA, ALU.mult, ALU.add, KA + KB + ["ms"], KA)
                dma(Cout[:, r * 2048:(r + 1) * 2048], sA, KA, [], "oC")
                tt(sB.rearrange("p (v k) -> p v k", v=32), sA.rearrange("p (v k) -> p v k", v=32), q_.unsqueeze(1).to_broadcast([128, 32, 64]),
                   ALU.mult, KA + ["wst0"], KB)
                red(mh[:, r * 32:(r + 1) * 32], sB.rearrange("p (v k) -> p v k", v=32), ALU.add, KB, ["wst1"])
            ts(mh, mh, ms[:, 8:9], None, ALU.mult, None, ["wst1", "ms"], ["wst1"])
            tt(t64, mh, mh, ALU.mult, ["wst1"], ["wst1"])
            red(ms[:, 9:10], t64, ALU.add, ["wst1"], ["ms"])
            mm(ptf[:, 300:301], Pair, ms[:, 9:10], True, True, ["cst", "ms"], ["ptf"])
            rsq(ms[:, 10:11], ptf[:, 300:301], 1.0 / 128, ["ptf"], ["ms"])
            ts(mh, mh, ms[:, 10:11], None, ALU.mult, None, ["wst1", "ms"], ["wst1"])
            act(g1, mA[:, 192:256], AF.Sigmoid, ["wst0"], ["wst1"])
            act(g2, mA[:, 256:320], AF.Silu, ["wst0"], ["wst1"])
            tt(g1, g1, g2, ALU.mult, ["wst1", "wst1"], ["wst1"])
            tt(mh, mh, g1, ALU.mult, ["wst1", "wst1"], ["wst1"])
            for j in range(8):
                mm(pmB[0:16, 0, j * 64:(j + 1) * 64], Sel8T[:, j, :], mh, True, True, ["cst", "wst1"], ["pmB0"])
            cp(ycs[:, 0:512], pmB[0:16, 0, :], ["pmB0"], ["ycat0"])
            mm(ptf[0:64, 304:369], Sel2T, mn1[:, 0:65], True, True, ["cst", "wst0"], ["ptf"])
            cp(nm64[:, 0:65], ptf[0:64, 304:369], ["ptf"], ["wst1"])
            dma(o_sn[l].rearrange("b h k -> (b h) k"), nm64[:, 0:64], ["wst1"], [], "on")
            dma(o_sm[l].rearrange("b (h o) -> (b h) o", o=1), nm64[:, 64:65], ["wst1"], [], "om", slow=True)

            cb3 = Eall[0:16, 0:3072]
            xbs = Eall[0:16, 3072:4096]
            dma(cb3, i_sconv[l].rearrange("b r c -> b (r c)"), (), ["E0", "E1", "E2"], "cb3")
            dma(o_sconv[l, :, 0:2, :].rearrange("b r c -> b (r c)"), cb3[:, 1024:3072], ["E0", "E1", "E2"], [], "oconv_a")
            dma(o_sconv[l, :, 2, :], ubx, ["xt0"], [], "oconv_b")
            for r in range(4):
                src = cb3[:, r * 1024:(r + 1) * 1024] if r < 3 else ubx
                for c in range(8):
                    tr(ptf[:, c * 16:(c + 1) * 16], src[:, c * 128:(c + 1) * 128], identf[0:16, 0:16], ["E0", "E1", "E2", "xt0", "cm"], ["ptf"])
                cp(fmc[:, r, :, :], ptf[:, 0:128].rearrange("p (c b) -> p c b", c=8), ["ptf"], ["xt1"], eng=("scalar" if r % 2 else "vector"))
            tt(fma, fmc[:, 3, :, :], cwt[:, :, 3].unsqueeze(2).to_broadcast([128, 8, 16]), ALU.mult, ["xt1", "cwt"], ["wst1"])
            for j in range(3):
                tt(fmb, fmc[:, j, :, :], cwt[:, :, j].unsqueeze(2).to_broadcast([128, 8, 16]), ALU.mult, ["xt1", "cwt"], ["wst1"])
                tt(fma, fma, fmb, ALU.add, ["wst1", "wst1"], ["wst1"])
            tt(fma, fma, cbt[:].unsqueeze(2).to_broadcast([128, 8, 16]), ALU.add, ["wst1", "cbt"], ["wst1"])
            act(fma, fma, AF.Silu, ["wst1"], ["wst1"])
            for c in range(8):
                pt_ = pmA if c < 4 else pmB
                tr(pt_[0:16, 1, (c % 4) * 128:(c % 4 + 1) * 128], fma[:, c, :], identf, ["wst1", "cm"], ["pmA1" if c < 4 else "pmB1"])
            cp(xbs[:, 0:512], pmA[0:16, 1, :], ["pmA1"], ["E3"])
            cp(xbs[:, 512:1024], pmB[0:16, 1, :], ["pmB1"], ["E3"], eng="scalar")
            tt(sd[:, 0:8], u[:, 2312:2320], par8[0:16, 0, :], ALU.add, ["utm", "par8"], ["sd"])
            act(sd[:, 0:8], sd[:, 0:8], AF.Exp, ["sd"], ["sd"])
            act(sd[:, 0:8], sd[:, 0:8], AF.Ln, ["sd"], ["sd"], bias=1.0)
            tt(sd[:, 8:16], sd[:, 0:8], par8[0:16, 1, :], ALU.mult, ["sd", "par8"], ["sd"])
            act(sd[:, 8:16], sd[:, 8:16], AF.Exp, ["sd"], ["sd"])
            hh = list(range(8))
            expand(pmA[:, 0, 0:64], "pmA0", [xbs[:, h * 64:(h + 1) * 64] for h in hh], ["E3"])
            expand(pmA[:, 0, 64:192], "pmA0", [xbs[:, 512 + (h // 4) * 128:512 + (h // 4 + 1) * 128] for h in hh], ["E3"])
            expand(pmA[:, 0, 192:320], "pmA0", [xbs[:, 768 + (h // 4) * 128:768 + (h // 4 + 1) * 128] for h in hh], ["E3"])
            expand(pmA[:, 0, 320:384], "pmA0", [u[:, 1800 + h * 64:1800 + (h + 1) * 64] for h in hh], ["utm"])
            expand(pmA[:, 0, 384:385], "pmA0", [sd[:, h:h + 1] for h in hh], ["sd"])
            expand(pmA[:, 0, 385:386], "pmA0", [sd[:, 8 + h:9 + h] for h in hh], ["sd"])
            cp(mA[:, 0:386], pmA[:, 0, 0:386], ["pmA0"], ["wst0"])
            x_, B_, C_, z_ = mA[:, 0:64], mA[:, 64:192], mA[:, 192:320], mA[:, 320:384]
            tt(mP[:, 0:8], par8[:, 2, :], M8, ALU.mult, ["par8", "cst"], ["mP"])
            red(ms[:, 11:12], mP[:, 0:8], ALU.add, ["mP"], ["ms"])
            tt(mP[:, 0:8], par8[:, 3, :], M8, ALU.mult, ["par8", "cst"], ["mP"])
            red(ms[:, 12:13], mP[:, 0:8], ALU.add, ["mP"], ["ms"])
            ts(t64, x_, mA[:, 384:385], None, ALU.mult, None, ["wst0"], ["wst1"])
            Hin = i_sh[l].rearrange("b h p s -> (b h) (p s)")
            Hout = o_sh[l].rearrange("b h p s -> (b h) (p s)")
            for r in range(4):
                dma(sA, Hin[:, r * 2048:(r + 1) * 2048], (), KA, "sA")
                tt(sB.rearrange("p (a s) -> p a s", a=16), t64[:, r * 16:(r + 1) * 16].unsqueeze(2).to_broadcast([128, 16, 128]),
                   B_.unsqueeze(1).to_broadcast([128, 16, 128]), ALU.mult, ["wst1", "wst0"], KB)
                stt(sA, sA, mA[:, 385:386], sB, ALU.mult, ALU.add, KA + KB + ["wst0"], KA)
                dma(Hout[:, r * 2048:(r + 1) * 2048], sA, KA, [], "oh")
                tt(sB.rearrange("p (a s) -> p a s", a=16), sA.rearrange("p (a s) -> p a s", a=16), C_.unsqueeze(1).to_broadcast([128, 16, 128]),
                   ALU.mult, KA + ["wst0"], KB)
                red(mh[:, r * 16:(r + 1) * 16], sB.rearrange("p (a s) -> p a s", a=16), ALU.add, KB, ["wst1"])
            stt(mh, x_, ms[:, 11:12], mh, ALU.mult, ALU.add, ["wst0", "ms", "wst1"], ["wst1"])
            act(g1, z_, AF.Silu, ["wst0"], ["wst1"])
            tt(mh, mh, g1, ALU.mult, ["wst1", "wst1"], ["wst1"])
            tt(t64, mh, mh, ALU.mult, ["wst1"], ["wst1"])
            red(ms[:, 9:10], t64, ALU.add, ["wst1"], ["ms"])
            mm(ptf[:, 300:301], Quad, ms[:, 9:10], True, True, ["cst", "ms"], ["ptf"])
            rsq(ms[:, 10:11], ptf[:, 300:301], 1.0 / 256, ["ptf"], ["ms"])
            ts(mh, mh, ms[:, 10:11], None, ALU.mult, None, ["wst1", "ms"], ["wst1"])
            for j in range(8):
                mm(pmB[0:16, 1, j * 64:(j + 1) * 64], Sel8T[:, j, :], mh, True, True, ["cst", "wst1"], ["pmB1"])
            cp(ycs[:, 512:1024], pmB[0:16, 1, :], ["pmB1"], ["ycat1"])

            qkn_s, qkA_s, qkB_s = Eall[0:16, 0:640], Eall[0:16, 1024:1664], Eall[0:16, 2048:2688]
            tt(qkA_s, u[:, 2320:2960], u[:, 2320:2960], ALU.mult, ["utm"], ["E1"])
            red(sd[:, 0:10], qkA_s.rearrange("p (h d) -> p h d", h=10), ALU.add, ["E1"], ["sd"])
            rsq(sd[:, 0:10], sd[:, 0:10], 1.0 / 64, ["sd"], ["sd"])
            n3 = qkn_s.rearrange("p (h d) -> p h d", h=10)
            tt(n3, u[:, 2320:2960].rearrange("p (h d) -> p h d", h=10), sd[:, 0:10].unsqueeze(2).to_broadcast([16, 10, 64]), ALU.mult,
               ["utm", "sd"], ["E0"])
            tt(n3[:, 0:8, :], n3[:, 0:8, :], qkw_b[0:16, 0, :].unsqueeze(1).to_broadcast([16, 8, 64]), ALU.mult, ["E0", "qkw_b"], ["E0"])
            tt(n3[:, 8:10, :], n3[:, 8:10, :], qkw_b[0:16, 1, :].unsqueeze(1).to_broadcast([16, 2, 64]), ALU.mult, ["E0", "qkw_b"], ["E0"])
            A3 = qkA_s.rearrange("p (h d) -> p h d", h=10)
            B3 = qkB_s.rearrange("p (h d) -> p h d", h=10)
            tt(A3, n3, ropeS[:, 0:64].unsqueeze(1).to_broadcast([16, 10, 64]), ALU.mult, ["E0", "cst"], ["E1"])
            tt(B3[:, :, 0:32], n3[:, :, 32:64], ropeS[:, 64:96].unsqueeze(1).to_broadcast([16, 10, 32]), ALU.mult, ["E0", "cst"], ["E2"])
            tt(B3[:, :, 32:64], n3[:, :, 0:32], ropeS[:, 96:128].unsqueeze(1).to_broadcast([16, 10, 32]), ALU.mult, ["E0", "cst"], ["E2"])
            tt(qkn_s, qkA_s, qkB_s, ALU.add, ["E1", "E2"], ["E0"])
            cp(qm, qkn_s[:, 0:512], ["E0"], ["xt1"])
            okl = "ok%d" % l
            dma(o_sk[l, :, 127, :, :].rearrange("b g d -> b (g d)"), qkn_s[:, 512:640], ["E0"], [okl], "ok_new")
            dma(o_sv[l, :, 127, :, :].rearrange("b g d -> b (g d)"), u[:, 2960:3088], ["utm"], [okl], "ov_new")
            dma(o_sk[l, :, 0:127, :, :].rearrange("b j g d -> b (j g d)"), i_ck[l, :, 1:128, :, :].rearrange("b j g d -> b (j g d)"), (), [okl], "ok_cp")
            dma(o_sv[l, :, 0:127, :, :].rearrange("b j g d -> b (j g d)"), i_cv[l, :, 1:128, :, :].rearrange("b j g d -> b (j g d)"), (), [okl], "ov_cp")
            K1 = sA.rearrange("p (b n) -> p b n", b=16)
            V1 = sB.rearrange("p (b n) -> p b n", b=16)
            dma(K1, o_sk[l].rearrange("b j g d -> j b (g d)"), [okl], KA, "sA")
            dma(V1, o_sv[l].rearrange("b j g d -> j b (g d)"), [okl], KB, "sB")
            qm2 = sb_qm2
            for b in range(16):
                ts(qm2, qm, identf[0:16, b:b + 1], None, ALU.mult, None, ["xt1", "cm"], ["xt1"])
                mm(pj[:, b % 2, :], onesf[0:16, :], qm2, True, True, ["cm", "xt1"], ["pj%d" % (b % 2)])
                tt(sTmp.rearrange("p (g h d) -> p g h d", g=2, h=4), K1[:, b, :].rearrange("p (g d) -> p g d", g=2).unsqueeze(2).to_broadcast([128, 2, 4, 64]),
                   pj[:, b % 2, :].rearrange("p (g h d) -> p g h d", g=2, h=4), ALU.mult, KA + ["pj%d" % (b % 2)], ["xt0"])
                red(sTt[:, b * 8:(b + 1) * 8], sTmp.rearrange("p (h d) -> p h d", h=8), ALU.add, ["xt0"], ["wst0"])
            act(sTt, sTt, AF.Exp, ["wst0"], ["wst0"], scale=0.125)
            mm(ptf[:, 300:301], sTt, onesf[:, 0:1], True, True, ["wst0", "cm"], ["ptf"])
            for b in range(16):
                for g in range(2):
                    mm(pmA[0:64, 0, (b * 8 + g * 4):(b * 8 + g * 4 + 4)], V1[:, b, g * 64:(g + 1) * 64], sTt[:, b * 8 + g * 4:b * 8 + g * 4 + 4],
                       True, True, KB + ["wst0"], ["pmA0"])
            cp(oTs, pmA[0:64, 0, 0:128], ["pmA0"], ["wst0"])
            tr(ptf[:, 320:384], oTs, identf[0:64, 0:64], ["wst0", "cm"], ["ptf"])
            tt(ms[:, 13:14], ptf[:, 300:301], ms[:, 12:13], ALU.add, ["ptf", "ms"], ["ms"])
            op("vector", lambda e: e.reciprocal(out=ms[:, 13:14], in_=ms[:, 13:14]), ["ms"], ["ms"])
            ts(mh, ptf[:, 320:384], ms[:, 13:14], None, ALU.mult, None, ["ptf", "ms"], ["wst1"])
            expand(pmA[:, 1, 0:64], "pmA1", [u[:, 3088 + h * 64:3088 + (h + 1) * 64] for h in hh], ["utm"])
            act(g1, pmA[:, 1, 0:64], AF.Silu, ["pmA1"], ["wst1"])
            tt(mh, mh, g1, ALU.mult, ["wst1", "wst1"], ["wst1"])
            for j in range(8):
                mm(pmB[0:16, 0, j * 64:(j + 1) * 64], Sel8T[:, j, :], mh, True, True, ["cst", "wst1"], ["pmB0"])
            cp(ycs[:, 1024:1536], pmB[0:16, 0, :], ["pmB0"], ["ycat2"])

            for rnd, (k0, k1) in enumerate(((0, 8), (8, 12))):
                for kc in range(k0, k1):
                    tr(ptr[:, kc - k0, 0:16], ycs[:, kc * 128:(kc + 1) * 128], identb[0:16, 0:16], ["ycat0", "ycat1", "ycat2", "cmb"], ["ptr"])
                cp(ycTs[:, k0:k1, :], ptr[:, 0:k1 - k0, 0:16], ["ptr"], ["ycatT"])
            for n in range(2):
                for kc in range(12):
                    mm(pj[0:16, n, :], ycTs[:, kc, :], Wo[:, kc, n * 512:(n + 1) * 512], kc == 0, kc == 11, ["ycatT", "Wo"], ["pj%d" % n])
                tt(xs_t[:, n * 512:(n + 1) * 512], pj[0:16, n, :], xs_t[:, n * 512:(n + 1) * 512], ALU.add, ["pj%d" % n, "xs_t"], ["xs_t"])
            if l == DEPTH - 1:
                dma(o_ys, xs_t[:], ["xs_t"], [], "ys")

        for l in range(DEPTH):
            load_layer(l)
            if with_sample:
                sample_layer(l)
            for t in range(NT):
                tile_step(l, t)
        S.emit(st)
        build.stats = S.stats
    return nc


PNAMES = ["w_in", "w_out", "norm_w", "a_igate_b", "a_fgate_b", "a_norm_w", "b_conv_w", "b_conv_b", "b_dt_bias", "b_A_log",
          "b_D", "b_norm_w", "c_qnorm_w", "c_knorm_w", "c_sinks"]


def make_in_maps(inputs, NT, DEPTH, with_sample):
    T = NT * 128
    consts = host_consts(T)
    if not with_sample:
        consts.pop("csel")
    shared = {n: np.ascontiguousarray(inputs[n][:DEPTH], dtype=np.float32) for n in PNAMES}
    shared.update(consts)
    in_maps = []
    for c in range(8):
        m = dict(shared)
        m["xp"] = np.ascontiguousarray(inputs["x_prompt"][c % 2, :T], dtype=np.float32)
        if with_sample:
            b0 = c * 16
            m["xs_in"] = np.ascontiguousarray(inputs["x_sample"][b0:b0 + 16, 0, :], dtype=np.float32)
            for nm, key in (("sC", "state_mlstm_C"), ("sn", "state_mlstm_n"), ("sm", "state_mlstm_m"), ("sh", "state_ssm"),
                            ("sconv", "state_conv"), ("ck", "cache_k"), ("cv", "cache_v")):
                m[nm] = np.ascontiguousarray(inputs[key][:DEPTH, b0:b0 + 16], dtype=np.float32)
        in_maps.append(m)
    return in_maps


def prompt_in_maps(inputs, NT, DEPTH):
    return make_in_maps(inputs, NT, DEPTH, False)


def run_prompt_only(inputs, NT, DEPTH):
    nc = build(NT, DEPTH)
    in_maps = prompt_in_maps(inputs, NT, DEPTH)
    res = run_bass_kernel_spmd(nc, in_maps, core_ids=list(range(8)))
    return res.results


def assemble(results, DEPTH):
    r = results
    hp = np.stack([r[0]["yp"], r[1]["yp"]], 0)
    hs = np.concatenate([r[c]["ys"] for c in range(8)], 0)[:, None, :]

    def pstack(nm):
        return np.stack([r[0][nm], r[1][nm]], 1)

    def scat(nm):
        return np.concatenate([r[c][nm] for c in range(8)], 1)

    outs = (hp, hs, pstack("pC"), pstack("pn"), pstack("pm"), pstack("ph"), pstack("pconv"), pstack("pk"), pstack("pv"),
            scat("oC"), scat("on"), scat("om"), scat("oh"), scat("oconv"), scat("ok"), scat("ov"))
    return tuple(np.ascontiguousarray(o, dtype=np.float32) for o in outs)


def kernel(**inputs):
    NT = inputs["x_prompt"].shape[1] // 128
    DEPTH = inputs["w_in"].shape[0]
    nc = build(NT, DEPTH, with_sample=True)
    in_maps = make_in_maps(inputs, NT, DEPTH, True)
    res = run_bass_kernel_spmd(nc, in_maps, core_ids=list(range(8)))
    return assemble(res.results, DEPTH)
```
